# Optimizing a Trainium2 kernel written in Bass

```python
import math, functools
import jax, jax.numpy as jnp
from jax import lax
import numpy as np

D_MODEL = 2048
BATCH = 2
SEQ = 4096
DEPTH = 4
DEC_BATCH = 2
DEC_SEQ = 8192
PAST_LEN = 128

N_MIXERS = 3
MIX_WIDTH = 3 * D_MODEL // 2
XA_HEADS = 4
XA_HEAD_DIM = D_MODEL // 8
XA_WIDTH = XA_HEADS * XA_HEAD_DIM
BRANCH = MIX_WIDTH + XA_WIDTH
MEM_LEN = 256
CONV_W = 4
CONV_LEFT = 2
LRU_BLOCKS = 12
LRU_BLOCK = MIX_WIDTH // LRU_BLOCKS
LRU_C = 8.0
QK_NOPE = 128
QK_ROPE = 64
V_DIM = 128
MLA_HEADS = MIX_WIDTH // V_DIM
Q_LORA = D_MODEL // 4
KV_LORA = D_MODEL // 8
ROPE_THETA = 10000.0
Q_BLOCK = 128
POOL_WINDOWS = (2, 4, 8, 16)
POOL_GROUPS = len(POOL_WINDOWS)
POOL_GROUP = MIX_WIDTH // POOL_GROUPS
IN_A = MIX_WIDTH + XA_WIDTH + BRANCH
IN_B = Q_LORA + KV_LORA + QK_ROPE + XA_WIDTH + BRANCH
IN_C = MIX_WIDTH + XA_WIDTH + BRANCH
N_A = (DEPTH + 2) // 3
N_B = (DEPTH + 1) // 3
N_C = DEPTH // 3
EPS = 1e-6

kernel_name = 'hybrid_bidir_lru_mla_pool_memxattn'


def rmsnorm(x, g):
    xf = x.astype(jnp.float32)
    y = xf * lax.rsqrt(jnp.mean(xf * xf, axis=-1, keepdims=True) + EPS)
    return (y * g.astype(jnp.float32)).astype(x.dtype)


def centred_dwconv(x, w, b):
    S = x.shape[1]
    xp = jnp.pad(x, ((0, 0), (CONV_LEFT, CONV_W - 1 - CONV_LEFT), (0, 0)))
    out = b
    for k in range(CONV_W):
        out = out + xp[:, k:k + S] * w[k]
    return out


def rglru_scan(u, gate_w, gate_b, lam, reverse):
    Bsz, S, C = u.shape
    ub = u.reshape(Bsz, S, LRU_BLOCKS, LRU_BLOCK)
    g = jnp.einsum('bshi,ghij->gbshj', ub, gate_w.astype(jnp.float32)).reshape(2, Bsz, S, C)
    g = g + gate_b.astype(jnp.float32)[:, None, None, :]
    r = jax.nn.sigmoid(g[0])
    i = jax.nn.sigmoid(g[1])
    log_a = -LRU_C * r * jax.nn.softplus(-lam.astype(jnp.float32))
    a = jnp.exp(log_a)
    b = jnp.sqrt(-jnp.expm1(2.0 * log_a)) * (i * u)

    def combine(left, right):
        a1, b1 = left
        a2, b2 = right
        return a1 * a2, a2 * b1 + b2

    _, h = lax.associative_scan(combine, (a, b), reverse=reverse, axis=1)
    return h


def rglru_mixer(xs, conv_w, conv_b, gate_w, gate_b, lam):
    u = centred_dwconv(xs, conv_w, conv_b).astype(jnp.float32)
    h = rglru_scan(u, gate_w[0], gate_b[0], lam[0], False) + rglru_scan(u, gate_w[1], gate_b[1], lam[1], True)
    return h.astype(xs.dtype)


def rope_tables(S):
    inv_freq = 1.0 / (ROPE_THETA ** (jnp.arange(0, QK_ROPE, 2, dtype=jnp.float32) / QK_ROPE))
    ang = jnp.arange(S, dtype=jnp.float32)[:, None] * inv_freq[None, :]
    return jnp.cos(ang), jnp.sin(ang)


def rotary(x, cos, sin):
    extra = x.ndim - 3
    c = cos.reshape(cos.shape[0], *([1] * extra), cos.shape[1])
    s = sin.reshape(sin.shape[0], *([1] * extra), sin.shape[1])
    xf = x.astype(jnp.float32)
    x1, x2 = jnp.split(xf, 2, axis=-1)
    return jnp.concatenate([x1 * c - x2 * s, x2 * c + x1 * s], axis=-1).astype(x.dtype)


def mla_mixer(c_q, c_kv, k_rope, q_norm, kv_norm, w_q_up, w_kv_up):
    Bsz, S, _ = c_q.shape
    q = (rmsnorm(c_q, q_norm) @ w_q_up).reshape(Bsz, S, MLA_HEADS, QK_NOPE + QK_ROPE)
    q_nope, q_rope = q[..., :QK_NOPE], q[..., QK_NOPE:]
    kv = (rmsnorm(c_kv, kv_norm) @ w_kv_up).reshape(Bsz, S, MLA_HEADS, QK_NOPE + V_DIM)
    k_nope, v = kv[..., :QK_NOPE], kv[..., QK_NOPE:]
    cos, sin = rope_tables(S)
    q_rope = rotary(q_rope, cos, sin)
    k_rope = rotary(k_rope, cos, sin)
    scale = (QK_NOPE + QK_ROPE) ** -0.5
    nb = S // Q_BLOCK

    def to_blocks(t):
        return jnp.moveaxis(t.reshape(Bsz, nb, Q_BLOCK, *t.shape[2:]), 1, 0)

    def one_block(args):
        qn, qr = args
        s = (jnp.einsum('bqhd,bkhd->bhqk', qn, k_nope, preferred_element_type=jnp.float32)
             + jnp.einsum('bqhr,bkr->bhqk', qr, k_rope, preferred_element_type=jnp.float32)) * scale
        p = jax.nn.softmax(s, axis=-1).astype(v.dtype)
        return jnp.einsum('bhqk,bkhd->bqhd', p, v)

    o = lax.map(one_block, (to_blocks(q_nope), to_blocks(q_rope)))
    return jnp.moveaxis(o, 0, 1).reshape(Bsz, S, MLA_HEADS * V_DIM)


def pool_mixer(xs, w_group, scale):
    Bsz, S, C = xs.shape
    xg = xs.astype(jnp.float32).reshape(Bsz, S, POOL_GROUPS, POOL_GROUP)
    cs = jnp.pad(jnp.cumsum(xg, axis=1), ((0, 0), (1, 0), (0, 0), (0, 0)))
    t = jnp.arange(S)
    outs = []
    for g, w in enumerate(POOL_WINDOWS):
        start = jnp.clip(t - w // 2, 0, S)
        end = jnp.clip(t + w - w // 2, 0, S)
        mean = (cs[:, end, g] - cs[:, start, g]) / (end - start).astype(jnp.float32)[None, :, None]
        outs.append(mean - xg[:, :, g])
    pooled = jnp.stack(outs, axis=2)
    y = jnp.einsum('bsgi,gij->bsgj', pooled, w_group.astype(jnp.float32)).reshape(Bsz, S, C)
    return (y * scale.astype(jnp.float32)).astype(xs.dtype)


def memory_cross_attention(xq, mem, norm_g, w_kv):
    Bsz, S, _ = xq.shape
    kv = rmsnorm(mem, norm_g) @ w_kv
    k, v = jnp.split(kv, 2, axis=-1)
    k = k.reshape(Bsz, MEM_LEN, XA_HEADS, XA_HEAD_DIM)
    v = v.reshape(Bsz, MEM_LEN, XA_HEADS, XA_HEAD_DIM)
    q = xq.reshape(Bsz, S, XA_HEADS, XA_HEAD_DIM)
    s = jnp.einsum('bshd,bmhd->bhsm', q, k, preferred_element_type=jnp.float32) * (XA_HEAD_DIM ** -0.5)
    p = jax.nn.softmax(s, axis=-1).astype(v.dtype)
    return jnp.einsum('bhsm,bmhd->bshd', p, v).reshape(Bsz, S, XA_WIDTH)


def trunk(x, mem, norm_pre, norm_post, norm_mem, w_mem_kv, w_out,
          a_w_in, a_conv_w, a_conv_b, a_gate_w, a_gate_b, a_lambda,
          b_w_in, b_q_norm, b_kv_norm, b_w_q_up, b_w_kv_up,
          c_w_in, c_w_group, c_scale):
    for i in range(DEPTH):
        kind, j = i % N_MIXERS, i // N_MIXERS
        h = rmsnorm(x, norm_pre[i])
        if kind == 0:
            u = h @ a_w_in[j]
            xs, xq, gate = jnp.split(u, [MIX_WIDTH, MIX_WIDTH + XA_WIDTH], axis=-1)
            mix = rglru_mixer(xs, a_conv_w[j], a_conv_b[j], a_gate_w[j], a_gate_b[j], a_lambda[j])
        elif kind == 1:
            u = h @ b_w_in[j]
            o1 = Q_LORA
            o2 = o1 + KV_LORA
            o3 = o2 + QK_ROPE
            o4 = o3 + XA_WIDTH
            c_q, c_kv, k_r, xq, gate = jnp.split(u, [o1, o2, o3, o4], axis=-1)
            mix = mla_mixer(c_q, c_kv, k_r, b_q_norm[j], b_kv_norm[j], b_w_q_up[j], b_w_kv_up[j])
        else:
            u = h @ c_w_in[j]
            xs, xq, gate = jnp.split(u, [MIX_WIDTH, MIX_WIDTH + XA_WIDTH], axis=-1)
            mix = pool_mixer(xs, c_w_group[j], c_scale[j])
        xa = memory_cross_attention(xq, mem, norm_mem[i], w_mem_kv[i])
        y = jnp.concatenate([mix.astype(x.dtype), xa.astype(x.dtype)], axis=-1) * jax.nn.silu(gate)
        x = x + rmsnorm(y @ w_out[i], norm_post[i])
    return x


def setup_inputs(seed: int = 0) -> dict:
    key = jax.random.key(seed)
    ks = iter(jax.random.split(key, 32))

    def nrm(shape, s):
        return jax.random.normal(next(ks), shape, jnp.float32) * s

    def gain(shape):
        return 1.0 + nrm(shape, 0.02)

    p = jax.random.uniform(next(ks), (N_A, 2, MIX_WIDTH), jnp.float32, minval=0.9, maxval=0.999) ** (1.0 / LRU_C)
    a_lambda = jnp.log(p) - jnp.log1p(-p)
    return {
        'x_prompt': nrm((BATCH, SEQ, D_MODEL), 1.0),
        'x_sample': nrm((DEC_BATCH, DEC_SEQ, D_MODEL), 1.0),
        'mem_prompt': nrm((BATCH, MEM_LEN, D_MODEL), 1.0),
        'mem_sample': nrm((DEC_BATCH, MEM_LEN, D_MODEL), 1.0),
        'norm_pre': gain((DEPTH, D_MODEL)),
        'norm_post': gain((DEPTH, D_MODEL)),
        'norm_mem': gain((DEPTH, D_MODEL)),
        'w_mem_kv': nrm((DEPTH, D_MODEL, 2 * XA_WIDTH), D_MODEL ** -0.5),
        'w_out': nrm((DEPTH, BRANCH, D_MODEL), BRANCH ** -0.5),
        'a_w_in': nrm((N_A, D_MODEL, IN_A), D_MODEL ** -0.5),
        'a_conv_w': nrm((N_A, CONV_W, MIX_WIDTH), CONV_W ** -0.5),
        'a_conv_b': nrm((N_A, MIX_WIDTH), 0.01),
        'a_gate_w': nrm((N_A, 2, 2, LRU_BLOCKS, LRU_BLOCK, LRU_BLOCK), LRU_BLOCK ** -0.5),
        'a_gate_b': nrm((N_A, 2, 2, MIX_WIDTH), 0.01),
        'a_lambda': a_lambda,
        'b_w_in': nrm((N_B, D_MODEL, IN_B), D_MODEL ** -0.5),
        'b_q_norm': gain((N_B, Q_LORA)),
        'b_kv_norm': gain((N_B, KV_LORA)),
        'b_w_q_up': nrm((N_B, Q_LORA, MLA_HEADS * (QK_NOPE + QK_ROPE)), Q_LORA ** -0.5),
        'b_w_kv_up': nrm((N_B, KV_LORA, MLA_HEADS * (QK_NOPE + V_DIM)), KV_LORA ** -0.5),
        'c_w_in': nrm((N_C, D_MODEL, IN_C), D_MODEL ** -0.5),
        'c_w_group': nrm((N_C, POOL_GROUPS, POOL_GROUP, POOL_GROUP), POOL_GROUP ** -0.5),
        'c_scale': gain((N_C, MIX_WIDTH)),
    }


def reference(x_prompt, x_sample, mem_prompt, mem_sample, norm_pre, norm_post, norm_mem, w_mem_kv, w_out,
              a_w_in, a_conv_w, a_conv_b, a_gate_w, a_gate_b, a_lambda,
              b_w_in, b_q_norm, b_kv_norm, b_w_q_up, b_w_kv_up,
              c_w_in, c_w_group, c_scale):
    y_prompt = trunk(x_prompt, mem_prompt, norm_pre, norm_post, norm_mem, w_mem_kv, w_out,
                     a_w_in, a_conv_w, a_conv_b, a_gate_w, a_gate_b, a_lambda,
                     b_w_in, b_q_norm, b_kv_norm, b_w_q_up, b_w_kv_up,
                     c_w_in, c_w_group, c_scale)
    y_sample = trunk(x_sample, mem_sample, norm_pre, norm_post, norm_mem, w_mem_kv, w_out,
                     a_w_in, a_conv_w, a_conv_b, a_gate_w, a_gate_b, a_lambda,
                     b_w_in, b_q_norm, b_kv_norm, b_w_q_up, b_w_kv_up,
                     c_w_in, c_w_group, c_scale)
    return (y_prompt, y_sample)
```

```python
from contextlib import ExitStack
import numpy as np
import concourse.bass as bass
import concourse.mybir as mybir
from concourse.ap import AP
from concourse.bass_utils import run_bass_kernel_spmd

F32 = mybir.dt.float32
BF16 = mybir.dt.bfloat16
ALU = mybir.AluOpType
AF = mybir.ActivationFunctionType
AX = mybir.AxisListType

SAME_SYNC = True
D = 2048
MIXW = 3072
XAW = 1024
BR = 4096
EPS = 1e-6


class Eng:
    def __init__(self, name, sem, is_pe=False):
        self.name = name
        self.sem = sem
        self.cnt = 0
        self.waited = {}
        self.is_pe = is_pe
        self.prog = []


class T:
    def __init__(self, ap, name=None):
        self.ap = ap
        self.name = name
        self.w = None
        self.r = {}
        self.dsem = None

    def __getitem__(self, k):
        return self.ap[k]


class Ctx:
    def __init__(self, nc, es):
        self.nc = nc
        self.es = es
        self.sems = {}
        self.engs = {}
        for nm, pe in (("pe", True), ("act", False), ("dve", False), ("pool", False), ("sp", False)):
            self.sems["s_" + nm] = es.enter_context(nc.semaphore("s_" + nm))
            self.engs[nm] = Eng(nm, "s_" + nm, pe)
        self.pe, self.act, self.dve = self.engs["pe"], self.engs["act"], self.engs["dve"]
        self.pool, self.sp = self.engs["pool"], self.engs["sp"]
        self.all_t = []
        self.n_ins = 0
        self._dcnt = {}
        self.dfree = []
        self.stage_ts = None
        self.stage_es = None

    def _reg(self, tt):
        self.all_t.append(tt)
        if self.stage_ts is not None:
            self.stage_ts.append(tt)
        return tt

    def sb(self, name, shape, dt, persist=False):
        es = self.es if (persist or self.stage_es is None) else self.stage_es
        self.uid = getattr(self, "uid", 0) + 1
        name = "sb%d_%s" % (self.uid, name)
        t = es.enter_context(self.nc.sbuf_tensor(name, list(shape), dt))
        tt = T(t, name)
        if persist or self.stage_ts is None:
            self.all_t.append(tt)
        else:
            self._reg(tt)
        return tt

    def ps(self, name, shape, dt=F32):
        t = self.es.enter_context(self.nc.psum_tensor(name, list(shape), dt))
        tt = T(t, name)
        self.all_t.append(tt)
        return tt

    def view(self, ap, name=None):
        return self._reg(T(ap, name))

    def begin_stage(self):
        self.stage_ts = []
        self.stage_es = ExitStack()
        self.stage_es.__enter__()

    def end_stage(self):
        self.drain_all()
        for t in self.stage_ts:
            if t.dsem is not None and t.dsem not in self.dfree:
                self.dfree.append(t.dsem)
        gone = set(id(t) for t in self.stage_ts)
        self.all_t = [t for t in self.all_t if id(t) not in gone]
        self.stage_ts = None
        self.stage_es.__exit__(None, None, None)
        self.stage_es = None

    def _dsem(self, t):
        if t.dsem is None:
            if self.dfree:
                t.dsem = self.dfree.pop()
            else:
                key = "d_%d" % len(self.sems)
                self.sems[key] = self.es.enter_context(self.nc.semaphore(key))
                t.dsem = key
        return t.dsem

    def share_dsem(self, ts):
        k = self._dsem(ts[0])
        for t in ts[1:]:
            t.dsem = k

    def _collect(self, E, reads, writes):
        need = {}

        def add(k, v):
            if need.get(k, 0) < v:
                need[k] = v
        for t in reads:
            if t.w is not None:
                add(*t.w)
        for t in writes:
            if t.w is not None:
                add(*t.w)
            for k, v in t.r.items():
                add(k, v)
        out = []
        for k, v in need.items():
            if k == E.sem and (E.is_pe or not SAME_SYNC):
                continue
            if E.waited.get(k, 0) >= v:
                continue
            E.waited[k] = v
            out.append((k, v))
        return out

    def _record(self, ev, reads, writes):
        for t in reads:
            if t.r.get(ev[0], 0) < ev[1]:
                t.r[ev[0]] = ev[1]
        for t in writes:
            t.w = ev
            t.r = {}

    def op(self, E, fn, reads=(), writes=(), inc=True):
        waits = self._collect(E, reads, writes)
        sems = self.sems
        if inc:
            E.cnt += 1
            ev = (E.sem, E.cnt)
        else:
            ev = (E.sem, E.cnt + 1)
        esem = E.sem

        def emit(h):
            for k, v in waits[1:]:
                h.wait_ge(sems[k], v)
            ins = fn(h)
            if waits:
                ins._wait_ge(sems[waits[0][0]], waits[0][1])
            if inc:
                ins.then_inc(sems[esem], 1)
        E.prog.append(emit)
        self.n_ins += 1 + len(waits)
        self._record(ev, reads, writes)
        return ev

    def dma(self, Q, out, in_, sbt, load=True, reads=(), writes=(), **kw):
        reads = list(reads)
        writes = list(writes)
        if load:
            writes.append(sbt)
        else:
            reads.append(sbt)
        waits = self._collect(Q, reads, writes)
        key = self._dsem(sbt)
        tot = self._dcnt.get(key, 0) + 16
        self._dcnt[key] = tot
        ev = (key, tot)
        sems = self.sems

        def emit(h):
            for k, v in waits[1:]:
                h.wait_ge(sems[k], v)
            ins = h.dma_start(out=out, in_=in_, **kw)
            if waits:
                ins._wait_ge(sems[waits[0][0]], waits[0][1])
            ins.then_inc(sems[key], 16)
        Q.prog.append(emit)
        self.n_ins += 1 + len(waits)
        self._record(ev, reads, writes)
        return ev

    def drain_all(self):
        targets = [(e.sem, e.cnt) for e in self.engs.values() if e.cnt > 0]
        targets += list(self._dcnt.items())
        sems = self.sems
        for E in self.engs.values():
            for k, v in targets:
                if k == E.sem or E.waited.get(k, 0) >= v:
                    continue
                E.waited[k] = v
                E.prog.append(lambda h, k=k, v=v: h.wait_ge(sems[k], v))
                self.n_ins += 1
        for t in self.all_t:
            t.w = None
            t.r = {}

    def emit(self):
        with self.nc.Block() as block:
            @block.tensor
            def _(h):
                for f in self.pe.prog:
                    f(h)

            @block.scalar
            def _(h):
                for f in self.act.prog:
                    f(h)

            @block.vector
            def _(h):
                for f in self.dve.prog:
                    f(h)

            @block.gpsimd
            def _(h):
                for f in self.pool.prog:
                    f(h)

            @block.sync
            def _(h):
                for f in self.sp.prog:
                    f(h)


def o_tt(c, E, out, a, b, op, R, W):
    return c.op(E, lambda h: h.tensor_tensor(out=out, in0=a, in1=b, op=op), R, W)


def o_ts(c, E, out, a, s1, s2, op0, op1, R, W):
    if s2 is None:
        return c.op(E, lambda h: h.tensor_scalar(out=out, in0=a, scalar1=s1, scalar2=None, op0=op0), R, W)
    return c.op(E, lambda h: h.tensor_scalar(out=out, in0=a, scalar1=s1, scalar2=s2, op0=op0, op1=op1), R, W)


def o_stt(c, E, out, a, s, b, op0, op1, R, W):
    return c.op(E, lambda h: h.scalar_tensor_tensor(out=out, in0=a, scalar=s, in1=b, op0=op0, op1=op1), R, W)


def o_act(c, out, in_, func, R, W, bias=None, scale=None):
    kw = {}
    if bias is not None:
        kw["bias"] = bias
    if scale is not None:
        kw["scale"] = scale
    return c.op(c.act, lambda h: h.activation(out=out, in_=in_, func=func, **kw), R, W)


def o_copy(c, E, out, in_, R, W):
    if E is c.act:
        return c.op(E, lambda h: h.copy(out=out, in_=in_), R, W)
    return c.op(E, lambda h: h.tensor_copy(out=out, in_=in_), R, W)


def o_mm(c, out, lhsT, rhs, start, stop, R, W, inc=None):
    return c.op(c.pe, lambda h: h.matmul(out, lhsT=lhsT, rhs=rhs, start=start, stop=stop), R, W,
                inc=(stop if inc is None else inc))


def rev_ap(ap2d):
    n = ap2d.shape[-1]
    last = ap2d[:, n - 1:n]
    return AP(ap2d.tensor, last.offset, [list(last.ap[0]), [-1, n]])


class Cfg:
    def __init__(self, LP, LS, nlayers=4):
        self.LP, self.LS = LP, LS
        self.T = LP + LS
        self.NT = self.T // 512
        self.segs = [(0, LP), (LP, LS)]
        self.nlayers = nlayers


class VecPack:
    def __init__(self):
        self.cols = []
        self.off = {}
        self.n = 0

    def add(self, name, v):
        v = np.asarray(v, np.float32).reshape(-1)
        assert v.size % 128 == 0 or v.size < 128
        if v.size < 128:
            a = np.zeros((128, 1), np.float32)
            a[:v.size, 0] = v
        else:
            a = np.ascontiguousarray(v.reshape(-1, 128).T)
        self.off[name] = (self.n, a.shape[1])
        self.cols.append(a)
        self.n += a.shape[1]

    def arr(self):
        return np.ascontiguousarray(np.concatenate(self.cols, axis=1))


class Prog:
    def __init__(self, cfg, voff, nv):
        self.cfg = cfg
        self.voff = voff
        T_ = cfg.T
        import os as _os
        nc = bass.Bass("TRN2", target_bir_lowering=False)
        self.nc = nc
        dt = nc.dram_tensor
        I = dict(kind="ExternalInput")
        self.xT = dt("xT", [D, T_], F32, **I).ap()
        self.memT = dt("memT", [D, 512], F32, **I).ap()
        self.Win = [dt("Win%d" % i, [(64, 47, 64, 64)[i], 128, 16 * 128], F32, **I).ap() for i in range(4)]
        self.Wout = [dt("Wout%d" % i, [16, 128, 32 * 128], F32, **I).ap() for i in range(4)]
        self.Wmem = [dt("Wmem%d" % i, [16, 128, 16 * 128], F32, **I).ap() for i in range(4)]
        self.gatew = dt("gatew", [96, 128, 2 * 256], F32, **I).ap()
        self.wq = dt("wq", [24, 128, 4 * 192], F32, **I).ap()
        self.wkv = dt("wkv", [24, 128, 2 * 256], F32, **I).ap()
        self.wgrp = dt("wgrp", [24, 128, 6 * 128], F32, **I).ap()
        self.vecs_d = dt("vecs", [128, nv], F32, **I).ap()
        self.rope_d = dt("rope", [4, 64, T_], F32, **I).ap()
        self.invc_d = dt("invc", [4, T_], F32, **I).ap()
        self.masks_d = dt("masks", [128, 12], F32, **I).ap()
        self.ident_d = dt("ident", [128, 128], F32, **I).ap()
        self.CQN = dt("CQN", [512, T_], BF16, **(dict(kind="ExternalOutput") if _os.environ.get("KDBG") == "2" else {})).ap()
        self.yT = dt("yT", [D, T_], F32, kind="ExternalOutput").ap()
        self.X = dt("X", [D, T_], F32).ap()
        self.Z = dt("Z", [D, T_], F32).ap()
        self.XS = dt("XS", [MIXW, T_], F32).ap()
        self.XQ = dt("XQ", [XAW, T_], BF16).ap()
        self.SG = dt("SG", [BR, T_], BF16).ap()
        import os as _os2
        dbgk = dict(kind="ExternalOutput") if _os2.environ.get("KDBG") == "2" else {}
        self.Y = dt("Y", [BR, T_], BF16, **dbgk).ap()
        self.SPL = [dt("SPL%d" % i, [MIXW, T_], F32).ap() for i in range(4)]
        self.CQ = dt("CQ", [512, T_], F32).ap()
        self.CKV = dt("CKV", [256, T_], F32).ap()
        self.KR = dt("KR", [128, T_], F32).ap()
        self.LATi = [dt("LATi%d" % i, [(128, 128, 64)[i], T_], BF16) for i in range(3)]
        self.LATo = [dt("LATo%d" % i, [4 * (128, 128, 64)[i], T_], BF16) for i in range(3)]
        self.ncc = 0
        self.dbg = None
        if _os.environ.get("KDBG") == "1":
            self.dbg = [dt("XL%d" % i, [D, T_], F32, kind="ExternalOutput").ap() for i in range(cfg.nlayers)]

    def vcol(self, name, j=0, n=1):
        o, w = self.voff[name]
        return self.vec[:, o + j:o + j + n]

    def build(self):
        nc, cfg = self.nc, self.cfg
        with ExitStack() as es:
            c = Ctx(nc, es)
            self.c = c
            for i in range(10):
                c.sems["cc%d" % i] = es.enter_context(nc.semaphore("cc%d" % i))
            self.vec = c.sb("vec", [128, self.vecs_d.shape[1]], F32, persist=True)
            self.masks = c.sb("masks", [128, 12], F32, persist=True)
            self.ones = c.sb("ones", [128, 128], BF16, persist=True)
            self.cst = c.sb("cst", [128, 4], F32, persist=True)
            self.ones_f = c.sb("ones_f", [128, 128], F32, persist=True)
            self.hin = c.sb("hin", [128, 24 * 2 * 2], F32, persist=True)
            self.ps = [c.ps("ps%d" % i, [128, 512], F32) for i in range(8)]
            c.dma(c.sp, self.vec[:], self.vecs_d[:, :], self.vec)
            c.dma(c.sp, self.masks[:], self.masks_d[:, :], self.masks)
            c.op(c.pool, lambda h: h.memset(self.ones[:], 1.0), (), [self.ones])
            c.op(c.pool, lambda h: h.memset(self.ones_f[:], 1.0), (), [self.ones_f])
            c.op(c.pool, lambda h: h.memset(self.cst[:, 0:1], EPS), (), [self.cst])
            c.op(c.pool, lambda h: h.memset(self.cst[:, 1:2], 1.0), (), [self.cst])
            c.op(c.pool, lambda h: h.memset(self.cst[:, 2:3], 0.0), (), [self.cst])
            c.drain_all()
            import os as _os
            lim = int(_os.environ.get("KSTAGES", "1000"))
            ns = 0
            for li in range(cfg.nlayers):
                kind = li % 3
                xsrc = self.xT if li == 0 else self.X
                xdst = self.yT if li == cfg.nlayers - 1 else self.X
                steps = [lambda: self.stage_inproj(li, kind, xsrc), lambda: self.stage_xattn(li),
                         (lambda: self.stage_rglru(li // 3)) if kind == 0 else
                         ((lambda: self.stage_mla()) if kind == 1 else (lambda: self.stage_pool())),
                         lambda: self.stage_outproj(li, xsrc, xdst)]
                for stp in steps:
                    if ns < lim:
                        stp()
                    ns += 1
            c.emit()
        return nc

    def exchange(self, contrib, W, name):
        c, nc = self.c, self.nc
        ein = nc.dram_tensor("exi_" + name, [128, W], F32)
        eout = nc.dram_tensor("exo_" + name, [4 * 128, W], F32)
        g = c.sb("exg_" + name, [128, 4, W], F32)
        ev = c.dma(c.pool, ein.ap()[:, :], contrib[:, 0:W], contrib, load=False)
        self._cc(ein, eout, ev)
        c.dma(c.pool, g[:], eout.ap().rearrange("(r p) w -> p r w", p=128), g)
        return g

    def _cc(self, ein, eout, ev):
        c, nc = self.c, self.nc
        key = "cc%d" % self.ncc
        self.ncc += 1
        sems = c.sems
        k0, v0 = ev

        def emit(h):
            h.wait_ge(sems[k0], v0)
            h.collective_compute("AllGather", ALU.bypass, replica_groups=[[0, 1, 2, 3], [4, 5, 6, 7]],
                                 ins=[ein.ap().opt()], outs=[eout.ap().opt()]).then_inc(sems[key])
            h.wait_ge(sems[key], 1)
        c.pool.prog.append(emit)
        c.pool.waited[k0] = max(c.pool.waited.get(k0, 0), v0)
        return key

    def rank_select(self, out_ap, g_ap_fn, mcol0, outT, gT, first_E=None):
        c = self.c
        E = c.dve
        for r in range(4):
            m = self.masks[:, mcol0 + r:mcol0 + r + 1]
            if r == 0:
                o_ts(c, E, out_ap, g_ap_fn(r), m, None, ALU.mult, None, [gT, self.masks], [outT])
            else:
                o_stt(c, E, out_ap, g_ap_fn(r), m, out_ap, ALU.mult, ALU.add, [gT, self.masks, outT], [outT])

    def gemm(self, xk, KC, tiles, jobs, epi, wf, wb, bank_sets, tile_w=512, hook=None):
        c = self.c
        nset = len(bank_sets[0])
        sets = [tiles[i:i + nset] for i in range(0, len(tiles), nset)]
        state = {"si": 0}

        def load(ji):
            job = jobs[ji]
            b = ji % 2
            w = job["w"]
            c.dma(c.sp, wf[b][:, 0:KC, 0:w], job["src"].rearrange("p (kc n) -> p kc n", kc=KC), wf[b])
            neg = job.get("neg")
            if neg is None:
                o_copy(c, c.pool, wb[b][:, 0:KC, 0:w], wf[b][:, 0:KC, 0:w], [wf[b]], [wb[b]])
            else:
                lo, hi = neg
                o_copy(c, c.pool, wb[b][:, 0:KC, 0:lo], wf[b][:, 0:KC, 0:lo], [wf[b]], [wb[b]])
                o_ts(c, c.pool, wb[b][:, 0:KC, lo:hi], wf[b][:, 0:KC, lo:hi], -1.0, None, ALU.mult, None, [wf[b]], [wb[b]])
                o_copy(c, c.pool, wb[b][:, 0:KC, hi:w], wf[b][:, 0:KC, hi:w], [wf[b]], [wb[b]])
            if hook is not None:
                hook(ji)

        load(0)
        for ji, job in enumerate(jobs):
            if ji + 1 < len(jobs):
                load(ji + 1)
            b = ji % 2
            w = job["w"]
            for s in sets:
                banks = bank_sets[state["si"] % len(bank_sets)]
                state["si"] += 1
                for k in range(KC):
                    xt_, xap = xk[k]
                    kp = xap.shape[0]
                    for idx, t in enumerate(s):
                        o_mm(c, banks[idx][0:w, 0:tile_w], wb[b][0:kp, k, 0:w], xap[:, t * tile_w:(t + 1) * tile_w],
                             k == 0, k == KC - 1, [wb[b], xt_], [banks[idx]])
                epi(ji, job, s, banks)

    def rms_stats(self, tiles_fn, nchunks, ntile, tile_w, rstd, rstdT, sqbufs, feat):
        raise NotImplementedError

    def stage_inproj(self, li, kind, xsrc):
        c, cfg = self.c, self.cfg
        T_, NT = cfg.T, cfg.NT
        H = T_ // 2
        c.begin_stage()
        hg = c.sb("hg", [128, 16, T_], BF16)
        hgT = [c.view(hg[:, k, :], "hg%d" % k) for k in range(16)]
        rstd = c.sb("rstd", [128, T_], F32)
        rstdT = [c.view(rstd[:, t * 512:(t + 1) * 512]) for t in range(NT)]
        xt = [c.sb("xt%d" % i, [128, H], F32) for i in range(2)]
        sq = [c.sb("sq%d" % i, [128, H], BF16) for i in range(2)]
        wf = [c.sb("wf%d" % i, [128, 16, 128], F32) for i in range(2)]
        wb = [c.sb("wb%d" % i, [128, 16, 128], BF16) for i in range(2)]
        of = [c.sb("of%d" % i, [128, 1536], F32) for i in range(2)]
        ob = [c.sb("ob%d" % i, [128, 1536], BF16) for i in range(2)]
        ps = self.ps
        u = 0
        for ch in range(16):
            for hf in range(2):
                b = u % 2
                u += 1
                c.dma(c.sp, xt[b][:], xsrc[ch * 128:(ch + 1) * 128, hf * H:(hf + 1) * H], xt[b])
                o_ts(c, c.pool, hg[:, ch, hf * H:(hf + 1) * H], xt[b][:], self.vcol("norm_pre", li * 16 + ch), None,
                     ALU.mult, None, [xt[b], self.vec], [hgT[ch]])
                o_act(c, sq[b][:], xt[b][:], AF.Square, [xt[b]], [sq[b]])
                for tt in range(NT // 2):
                    t = hf * (NT // 2) + tt
                    o_mm(c, ps[t][:, :], self.ones[:], sq[b][:, tt * 512:(tt + 1) * 512], ch == 0, ch == 15,
                         [self.ones, sq[b]], [ps[t]], inc=True)
        for t in range(NT):
            o_act(c, rstd[:, t * 512:(t + 1) * 512], ps[t][:, :], AF.Sqrt, [ps[t], self.cst], [rstdT[t]],
                  bias=self.cst[:, 0:1], scale=1.0 / D)
            c.op(c.dve, lambda h, t=t: h.reciprocal(out=rstd[:, t * 512:(t + 1) * 512], in_=rstd[:, t * 512:(t + 1) * 512]),
                 [rstdT[t]], [rstdT[t]])
        W = self.Win[li]
        jobs = []

        def J(kind_, dest, row0, const=1.0, neg=None):
            ji = len(jobs)
            jobs.append(dict(src=W[ji], w=128, kind=kind_, dest=dest, row0=row0, const=const, neg=neg))
        if kind != 1:
            for n in range(24):
                J("f32", self.XS, n * 128)
            for n in range(8):
                J("bf16", self.XQ, n * 128, const=1.0 / 16.0)
        else:
            for n in range(4):
                J("f32", self.CQ, n * 128)
            for n in range(2):
                J("f32", self.CKV, n * 128)
            J("f32", self.KR, 0, neg=(64, 96))
            for n in range(8):
                J("bf16", self.XQ, n * 128, const=1.0 / 16.0)
        for n in range(32):
            J("gate", self.SG, n * 128)
        st = {"i": 0}

        def epi(ji, job, s, banks):
            i = st["i"] % 2
            st["i"] += 1
            n = len(s) * 512
            tok0 = s[0] * 512
            w = job["w"]
            for idx, t in enumerate(s):
                sl = slice(idx * 512, (idx + 1) * 512)
                if job["kind"] == "bf16":
                    o_stt(c, c.dve, ob[i][0:w, sl], banks[idx][0:w, :], job["const"], rstd[0:w, t * 512:(t + 1) * 512],
                          ALU.mult, ALU.mult, [banks[idx], rstdT[t]], [ob[i]])
                else:
                    o_stt(c, c.dve, of[i][0:w, sl], banks[idx][0:w, :], job["const"], rstd[0:w, t * 512:(t + 1) * 512],
                          ALU.mult, ALU.mult, [banks[idx], rstdT[t]], [of[i]])
                    if job["kind"] == "gate":
                        o_act(c, ob[i][0:w, sl], of[i][0:w, sl], AF.Silu, [of[i]], [ob[i]])
            r0 = job["row0"]
            if job["kind"] == "f32":
                c.dma(c.sp, job["dest"][r0:r0 + w, tok0:tok0 + n], of[i][0:w, 0:n], of[i], load=False)
            else:
                c.dma(c.sp, job["dest"][r0:r0 + w, tok0:tok0 + n], ob[i][0:w, 0:n], ob[i], load=False)

        xk = [(hgT[k], hg[:, k, :]) for k in range(16)]
        nset = 3 if NT % 3 == 0 else 2
        bank_sets = [ps[0:nset], ps[nset:2 * nset]]
        self.gemm(xk, 16, list(range(NT)), jobs, epi, wf, wb, bank_sets)
        c.end_stage()

    def stage_xattn(self, li):
        c, cfg = self.c, self.cfg
        T_, NT = cfg.T, cfg.NT
        ps = self.ps
        c.begin_stage()
        mem = c.sb("mem", [128, 16, 512], F32)
        memT_ = [c.view(mem[:, k, :]) for k in range(16)]
        msq = [c.sb("msq%d" % i, [128, 512], BF16) for i in range(2)]
        mg = c.sb("mg", [128, 16, 512], BF16)
        mgT = [c.view(mg[:, k, :]) for k in range(16)]
        rstdm = c.sb("rstdm", [128, 512], F32)
        kvT = c.sb("kvT", [128, 16, 512], BF16)
        kvTT = [c.view(kvT[:, k, :]) for k in range(16)]
        vtm = c.sb("vtm", [128, 4, 1024], BF16)
        identf = c.sb("identf", [128, 128], F32)
        ident = c.sb("ident", [128, 128], BF16)
        wf = [c.sb("wf%d" % i, [128, 16, 128], F32) for i in range(2)]
        wb = [c.sb("wb%d" % i, [128, 16, 128], BF16) for i in range(2)]
        xq = [c.sb("xq%d" % i, [128, 2, T_], BF16) for i in range(2)]
        sg = [c.sb("sg%d" % i, [128, 2, T_], BF16) for i in range(2)]
        yo = [c.sb("yo%d" % i, [128, 2, T_], BF16) for i in range(2)]
        pT = [c.sb("pT%d" % i, [128, 512], BF16) for i in range(4)]
        rec = [c.sb("rec%d" % i, [128, 512], F32) for i in range(2)]
        otmp = [c.sb("otmp%d" % i, [128, 512], F32) for i in range(2)]
        c.dma(c.sp, identf[:], self.ident_d[:, :], identf)
        o_copy(c, c.pool, ident[:], identf[:], [identf], [ident])
        for k in range(16):
            c.dma(c.sp, mem[:, k, :], self.memT[k * 128:(k + 1) * 128, :], memT_[k])
            o_act(c, msq[k % 2][:], mem[:, k, :], AF.Square, [memT_[k]], [msq[k % 2]])
            o_mm(c, ps[0][:, :], self.ones[:], msq[k % 2][:], k == 0, k == 15, [self.ones, msq[k % 2]], [ps[0]], inc=True)
        o_act(c, rstdm[:], ps[0][:, :], AF.Sqrt, [ps[0], self.cst], [rstdm], bias=self.cst[:, 0:1], scale=1.0 / D)
        c.op(c.dve, lambda h: h.reciprocal(out=rstdm[:], in_=rstdm[:]), [rstdm], [rstdm])
        for k in range(16):
            o_stt(c, c.dve, mg[:, k, :], mem[:, k, :], self.vcol("norm_mem", li * 16 + k), rstdm[:], ALU.mult, ALU.mult,
                  [memT_[k], rstdm, self.vec], [mgT[k]])
        W = self.Wmem[li]
        jobs = [dict(src=W[n], w=128, n=n) for n in range(16)]

        def epi(ji, job, s, banks):
            o_copy(c, c.act, kvT[:, job["n"], :], banks[0][:, :], [banks[0]], [kvTT[job["n"]]])
        self.gemm([(mgT[k], mg[:, k, :]) for k in range(16)], 16, [0], jobs, epi, wf, wb, [[ps[1]], [ps[2]]])
        for smb in range(4):
            for q4 in range(2):
                bank = ps[3 + (smb * 2 + q4) % 2]
                for vv in range(4):
                    vc = q4 * 4 + vv
                    o_mm(c, bank[:, vv * 128:(vv + 1) * 128], kvT[:, 8 + vc, smb * 128:(smb + 1) * 128], ident[:], True, True,
                         [kvTT[8 + vc], ident], [bank])
                o_copy(c, c.act, vtm[:, smb, q4 * 512:(q4 + 1) * 512], bank[:, :], [bank], [vtm])
        XQv = self.XQ.rearrange("(k p) t -> p k t", p=128)
        SGv = self.SG.rearrange("(k p) t -> p k t", p=128)
        Yv = self.Y.rearrange("(k p) t -> p k t", p=128)
        it = 0
        for hh in range(4):
            b = hh % 2
            c.dma(c.sp, xq[b][:], XQv[:, hh * 2:hh * 2 + 2, :], xq[b])
            c.dma(c.sp, sg[b][:], SGv[:, 24 + hh * 2:24 + hh * 2 + 2, :], sg[b])
            for si, (t0, L) in enumerate(cfg.segs):
                for qt in range(L // 512):
                    tk = slice(t0 + qt * 512, t0 + (qt + 1) * 512)
                    g3 = ps[2:5] if it % 2 == 0 else ps[5:8]
                    for mb in range(2):
                        sb_ = ps[(it * 2 + mb) % 2]
                        for dc in range(2):
                            o_mm(c, sb_[:, :], kvT[:, hh * 2 + dc, si * 256 + mb * 128: si * 256 + (mb + 1) * 128],
                                 xq[b][:, dc, tk], dc == 0, dc == 1, [kvTT[hh * 2 + dc], xq[b]], [sb_])
                        p_ = pT[(it * 2 + mb) % 4]
                        o_act(c, p_[:], sb_[:, :], AF.Exp, [sb_], [p_])
                    for dc in range(2):
                        for mb in range(2):
                            p_ = pT[(it * 2 + mb) % 4]
                            o_mm(c, g3[dc][:, :], vtm[:, si * 2 + mb, hh * 256 + dc * 128: hh * 256 + (dc + 1) * 128], p_[:],
                                 mb == 0, mb == 1, [vtm, p_], [g3[dc]])
                    for mb in range(2):
                        p_ = pT[(it * 2 + mb) % 4]
                        o_mm(c, g3[2][:, :], self.ones[:], p_[:], mb == 0, mb == 1, [self.ones, p_], [g3[2]])
                    r_ = rec[it % 2]
                    c.op(c.dve, lambda h, r_=r_, g3=g3: h.reciprocal(out=r_[:], in_=g3[2][:, :]), [g3[2]], [r_])
                    for dc in range(2):
                        ot = otmp[dc]
                        o_tt(c, c.dve, ot[:], g3[dc][:, :], r_[:], ALU.mult, [g3[dc], r_], [ot])
                        o_tt(c, c.pool, yo[b][:, dc, tk], ot[:], sg[b][:, dc, tk], ALU.mult, [ot, sg[b]], [yo[b]])
                    it += 1
            c.dma(c.sp, Yv[:, 24 + hh * 2:24 + hh * 2 + 2, :], yo[b][:], yo[b], load=False)
        c.end_stage()

    def stage_outproj(self, li, xsrc, xdst):
        c, cfg = self.c, self.cfg
        T_, NT = cfg.T, cfg.NT
        ps = self.ps
        NB = NT // 2
        c.begin_stage()
        yk = c.sb("yk", [128, 32, 1024], BF16)
        ykT = [c.view(yk[:, k * 8:(k + 1) * 8, :]) for k in range(4)]
        wf = [c.sb("wf%d" % i, [128, 32, 128], F32) for i in range(2)]
        wb = [c.sb("wb%d" % i, [128, 32, 128], BF16) for i in range(2)]
        zt = [c.sb("zt%d" % i, [128, 1024], F32) for i in range(2)]
        sq = [c.sb("sq%d" % i, [128, 1024], BF16) for i in range(2)]
        rstd2 = [c.sb("rstd%d" % i, [128, 1024], F32) for i in range(2)]
        zl = [c.sb("zl%d" % i, [128, 1024], F32) for i in range(3)]
        xl = [c.sb("xl%d" % i, [128, 1024], F32) for i in range(3)]
        xo = [c.sb("xo%d" % i, [128, 1024], F32) for i in range(3)]
        Yv = self.Y.rearrange("(k p) t -> p k t", p=128)
        Zt = [[c.view(self.Z[n * 128:(n + 1) * 128, blk * 1024:(blk + 1) * 1024]) for n in range(16)] for blk in range(NB)]
        W = self.Wout[li]
        jobs = [dict(src=W[n], w=128, n=n) for n in range(16)]

        def load_yk(blk):
            tok = slice(blk * 1024, (blk + 1) * 1024)
            for k in range(4):
                c.dma(c.sp, yk[:, k * 8:(k + 1) * 8, :], Yv[:, k * 8:(k + 1) * 8, tok], ykT[k])

        pst = []

        def flush_store():
            while pst:
                blk_, n_, i_ = pst.pop(0)
                tok_ = slice(blk_ * 1024, (blk_ + 1) * 1024)
                c.dma(c.sp, xdst[n_ * 128:(n_ + 1) * 128, tok_], xo[i_][:], xo[i_], load=False)
                if self.dbg is not None and li < len(self.dbg):
                    c.dma(c.sp, self.dbg[li][n_ * 128:(n_ + 1) * 128, tok_], xo[i_][:], xo[i_], load=False)

        def n2_iter(blk, n):
            tok = slice(blk * 1024, (blk + 1) * 1024)
            i = n % 3
            rstd = rstd2[blk % 2]
            flush_store()
            c.dma(c.sp, zl[i][:], self.Z[n * 128:(n + 1) * 128, tok], zl[i], reads=[Zt[blk][n]])
            c.dma(c.sp, xl[i][:], xsrc[n * 128:(n + 1) * 128, tok], xl[i])
            o_stt(c, c.dve, zl[i][:], zl[i][:], self.vcol("norm_post", li * 16 + n), rstd[:], ALU.mult, ALU.mult,
                  [zl[i], rstd, self.vec], [zl[i]])
            o_tt(c, c.pool, xo[i][:], zl[i][:], xl[i][:], ALU.add, [zl[i], xl[i]], [xo[i]])
            pst.append((blk, n, i))

        load_yk(0)
        for blk in range(NB):
            tok = slice(blk * 1024, (blk + 1) * 1024)
            rstd = rstd2[blk % 2]
            pend = []

            def ss_mm(n, i):
                for idx in range(2):
                    o_mm(c, ps[4 + idx][:, :], self.ones[:], sq[i][:, idx * 512:(idx + 1) * 512], n == 0, n == 15,
                         [self.ones, sq[i]], [ps[4 + idx]], inc=True)

            def epi(ji, job, s, banks, blk=blk, tok=tok):
                i = ji % 2
                n = job["n"]
                while pend:
                    ss_mm(*pend.pop(0))
                for idx in range(2):
                    sl = slice(idx * 512, (idx + 1) * 512)
                    o_copy(c, c.act, zt[i][:, sl], banks[idx][:, :], [banks[idx]], [zt[i]])
                    o_tt(c, c.dve, sq[i][:, sl], zt[i][:, sl], zt[i][:, sl], ALU.mult, [zt[i]], [sq[i]])
                c.dma(c.sp, self.Z[n * 128:(n + 1) * 128, tok], zt[i][:], zt[i], load=False, writes=[Zt[blk][n]])
                pend.append((n, i))
            hook = None
            if blk > 0:
                hook = (lambda ji, pb=blk - 1: n2_iter(pb, ji))
            xk = [(ykT[k // 8], yk[:, k, :]) for k in range(32)]
            self.gemm(xk, 32, [0, 1], jobs, epi, wf, wb, [ps[0:2], ps[2:4]], hook=hook)
            while pend:
                ss_mm(*pend.pop(0))
            for idx in range(2):
                sl = slice(idx * 512, (idx + 1) * 512)
                o_act(c, rstd[:, sl], ps[4 + idx][:, :], AF.Sqrt, [ps[4 + idx], self.cst], [rstd],
                      bias=self.cst[:, 0:1], scale=1.0 / D)
            c.op(c.dve, lambda h, rstd=rstd: h.reciprocal(out=rstd[:], in_=rstd[:]), [rstd], [rstd])
            if blk + 1 < NB:
                load_yk(blk + 1)
        for n in range(16):
            n2_iter(NB - 1, n)
        flush_store()
        c.end_stage()

    def halo_exchange(self, nL, nR, name):
        c, cfg = self.c, self.cfg
        ne = nL + nR
        hal = c.sb("hal_" + name, [128, 24, 2, ne], F32)
        XSv = self.XS.rearrange("(k p) t -> p k t", p=128)
        for si, (t0, L) in enumerate(cfg.segs):
            for k6 in range(4):
                ks = slice(k6 * 6, (k6 + 1) * 6)
                c.dma(c.sp, hal[:, ks, si, 0:nR], XSv[:, ks, t0:t0 + nR], hal, allow_slow_non_contiguous=True)
                c.dma(c.sp, hal[:, ks, si, nR:ne], XSv[:, ks, t0 + L - nL:t0 + L], hal, allow_slow_non_contiguous=True)
        g = self._exchange4(hal, 24 * 2 * ne, name)
        hl = c.sb("haloL_" + name, [128, 24, 2, nL], F32)
        hr = c.sb("haloR_" + name, [128, 24, 2, nR], F32)

        def gv(r, lo, hi):
            return g[:, r, :].rearrange("p (k s e) -> p k s e", k=24, s=2)[:, :, :, lo:hi]
        self.rank_select(hl[:], lambda r: gv(r, nR, ne), 0, hl, g)
        self.rank_select(hr[:], lambda r: gv(r, 0, nR), 4, hr, g)
        return hl, hr

    def _exchange4(self, tile4, W, name):
        c, nc = self.c, self.nc
        ein = nc.dram_tensor("exi_" + name, [128, W], F32)
        eout = nc.dram_tensor("exo_" + name, [4 * 128, W], F32)
        g = c.sb("exg_" + name, [128, 4, W], F32)
        nd = len(tile4.ap.shape)
        if nd == 4:
            flat = tile4[:].rearrange("p a b e -> p (a b e)")
        elif nd == 3:
            flat = tile4[:].rearrange("p a b -> p (a b)")
        else:
            flat = tile4[:]
        ev = c.dma(c.sp, ein.ap()[:, :], flat, tile4, load=False)
        key = self._cc(ein, eout, ev)
        sems = c.sems
        c.sp.prog.append(lambda h, key=key: h.wait_ge(sems[key], 1))
        c.dma(c.sp, g[:], eout.ap().rearrange("(r p) w -> p r w", p=128), g)
        return g

    def stage_rglru(self, aj):
        c, cfg = self.c, self.cfg
        T_, NT = cfg.T, cfg.NT
        ps = self.ps
        TE = T_ + 6
        c.begin_stage()
        kk = c.sb("kk", [128, 48], F32)
        xx = c.sb("xx", [128, 48], F32)
        o_act(c, xx[:], self.vcol("lam", aj * 48, 48), AF.Exp, [self.vec], [xx], scale=-1.0)
        o_ts(c, c.dve, kk[:], xx[:], -0.2, 0.25, ALU.mult, ALU.add, [xx], [kk])
        for cst_ in (1.0 / 3.0, 0.5, 1.0):
            o_tt(c, c.dve, kk[:], kk[:], xx[:], ALU.mult, [kk, xx], [kk])
            o_ts(c, c.dve, kk[:], kk[:], -1.0, cst_, ALU.mult, ALU.add, [kk], [kk])
        o_tt(c, c.dve, kk[:], kk[:], xx[:], ALU.mult, [kk, xx], [kk])
        o_ts(c, c.dve, kk[:], kk[:], -8.0, None, ALU.mult, None, [kk], [kk])
        hl_, hr_ = self.halo_exchange(2, 1, "a%d" % aj)
        contrib = c.sb("contrib", [128, 24, 2, 4], F32)
        xe = [c.sb("xe%d" % i, [128, TE], F32) for i in range(2)]
        u = [c.sb("u%d" % i, [128, T_], F32) for i in range(2)]
        ub = [c.sb("ub%d" % i, [128, T_], BF16) for i in range(2)]
        r2 = [c.sb("r_%d" % i, [128, T_], F32) for i in range(2)]
        i2 = [c.sb("i_%d" % i, [128, T_], F32) for i in range(2)]
        a_ = [c.sb("a_%d" % i, [128, T_], F32) for i in range(2)]
        b_ = [c.sb("b_%d" % i, [128, T_], F32) for i in range(2)]
        tmp2 = [c.sb("tmp%d" % i, [128, T_], F32) for i in range(2)]
        wgf = c.sb("wgf", [128, 4, 2, 256], F32)
        wgb = c.sb("wgb", [128, 4, 2, 256], BF16)
        rsum2 = [c.sb("rsum%d" % i, [128, 2], F32) for i in range(2)]
        nab = 0
        for h in range(12):
            for dg in range(4):
                d, g = dg // 2, dg % 2
                gj = ((aj * 2 + d) * 2 + g) * 12 + h
                c.dma(c.sp, wgf[:, dg, :, :], self.gatew[gj].rearrange("p (kc n) -> p kc n", kc=2), wgf)
            o_copy(c, c.pool, wgb[:], wgf[:], [wgf], [wgb])
            for oc in range(2):
                cc = 2 * h + oc
                for si, (t0, L) in enumerate(cfg.segs):
                    e0 = t0 + 3 * si
                    c.dma(c.sp, xe[oc][:, e0 + 2:e0 + 2 + L], self.XS[cc * 128:(cc + 1) * 128, t0:t0 + L], xe[oc])
                for si, (t0, L) in enumerate(cfg.segs):
                    e0 = t0 + 3 * si
                    o_copy(c, c.pool, xe[oc][:, e0:e0 + 2], hl_[:, cc, si, :], [hl_], [xe[oc]])
                    o_copy(c, c.pool, xe[oc][:, e0 + 2 + L:e0 + 3 + L], hr_[:, cc, si, :], [hr_], [xe[oc]])
                for si, (t0, L) in enumerate(cfg.segs):
                    e0 = t0 + 3 * si
                    o_ts(c, c.dve, u[oc][:, t0:t0 + L], xe[oc][:, e0:e0 + L], self.vcol("conv_w", (aj * 4 + 0) * 24 + cc),
                         self.vcol("conv_b", aj * 24 + cc), ALU.mult, ALU.add, [xe[oc], self.vec], [u[oc]])
                    for k in range(1, 4):
                        o_stt(c, c.dve, u[oc][:, t0:t0 + L], xe[oc][:, e0 + k:e0 + k + L],
                              self.vcol("conv_w", (aj * 4 + k) * 24 + cc), u[oc][:, t0:t0 + L], ALU.mult, ALU.add,
                              [xe[oc], self.vec, u[oc]], [u[oc]])
                o_copy(c, c.act, ub[oc][:], u[oc][:], [u[oc]], [ub[oc]])
            for oc in range(2):
                cc = 2 * h + oc
                for d in range(2):
                    r_, i_, tmp, rsum = r2[nab % 2], i2[nab % 2], tmp2[nab % 2], rsum2[nab % 2]
                    for t in range(NT):
                        tk = slice(t * 512, (t + 1) * 512)
                        pr, pi = ps[(t % 2) * 2], ps[(t % 2) * 2 + 1]
                        for g, pb in ((0, pr), (1, pi)):
                            for kc in range(2):
                                o_mm(c, pb[:, :], wgb[:, d * 2 + g, kc, oc * 128:(oc + 1) * 128], ub[kc][:, tk], kc == 0, kc == 1,
                                     [wgb, ub[kc]], [pb])
                        o_act(c, r_[:, tk], pr[:, :], AF.Sigmoid, [pr, self.vec], [r_],
                              bias=self.vcol("gate_b", ((aj * 2 + d) * 2 + 0) * 24 + cc))
                        o_act(c, i_[:, tk], pi[:, :], AF.Sigmoid, [pi, self.vec], [i_],
                              bias=self.vcol("gate_b", ((aj * 2 + d) * 2 + 1) * 24 + cc))
                    A, B = a_[nab % 2], b_[nab % 2]
                    nab += 1
                    kcol = kk[:, d * 24 + cc:d * 24 + cc + 1]
                    for si, (t0, L) in enumerate(cfg.segs):
                        sl = slice(t0, t0 + L)
                        c.op(c.dve, lambda h_, si=si, sl=sl, rsum=rsum, r_=r_: h_.tensor_reduce(out=rsum[:, si:si + 1], in_=r_[:, sl], axis=AX.X, op=ALU.add),
                             [r_], [rsum])
                    o_act(c, A[:], r_[:], AF.Exp, [r_, kk], [A], scale=kcol)
                    for si, (t0, L) in enumerate(cfg.segs):
                        o_act(c, contrib[:, cc, si, d * 2:d * 2 + 1], rsum[:, si:si + 1], AF.Exp, [rsum, kk], [contrib], scale=kcol)
                    o_tt(c, c.dve, tmp[:], A[:], A[:], ALU.mult, [A], [tmp])
                    o_act(c, tmp[:], tmp[:], AF.Sqrt, [tmp, self.cst], [tmp], bias=self.cst[:, 1:2], scale=-1.0)
                    o_tt(c, c.pool, i_[:], i_[:], u[oc][:], ALU.mult, [i_, u[oc]], [i_])
                    o_tt(c, c.pool, B[:], i_[:], tmp[:], ALU.mult, [i_, tmp], [B])
                    for si, (t0, L) in enumerate(cfg.segs):
                        sl = slice(t0, t0 + L)
                        if d == 0:
                            oa, da, db = r_[:, sl], A[:, sl], B[:, sl]
                            hend = r_[:, t0 + L - 1:t0 + L]
                        else:
                            oa, da, db = rev_ap(r_[:, sl]), rev_ap(A[:, sl]), rev_ap(B[:, sl])
                            hend = r_[:, t0:t0 + 1]
                        c.op(c.dve, lambda h_, oa=oa, da=da, db=db: h_.tensor_tensor_scan(out=oa, data0=da, data1=db, initial=0.0,
                                                                                     op0=ALU.mult, op1=ALU.add),
                             [A, B, r_], [r_])
                        o_copy(c, c.dve, contrib[:, cc, si, d * 2 + 1:d * 2 + 2], hend, [r_], [contrib])
                    c.dma(c.sp, self.SPL[d * 2][cc * 128:(cc + 1) * 128, :], A[:], A, load=False)
                    c.dma(c.sp, self.SPL[d * 2 + 1][cc * 128:(cc + 1) * 128, :], B[:], B, load=False)
        g2 = self._exchange4(contrib, 192, "c%d" % aj)

        def gP(r, d):
            return g2[:, r, :].rearrange("p (k e) -> p k e", e=4)[:, :, d * 2:d * 2 + 1]

        def gH(r, d):
            return g2[:, r, :].rearrange("p (k e) -> p k e", e=4)[:, :, d * 2 + 1:d * 2 + 2]
        S = [c.sb("S%d" % i, [128, 48, 1], F32) for i in range(3)]
        hin = self.hin[:].rearrange("p (k d) -> p k d", d=2)
        E = c.dve
        m = lambda j: self.masks[:, 8 + j:9 + j]
        o_copy(c, E, S[0][:], gH(0, 0), [g2], [S[0]])
        o_tt(c, E, S[1][:], gP(1, 0), S[0][:], ALU.mult, [g2, S[0]], [S[1]])
        o_tt(c, E, S[1][:], S[1][:], gH(1, 0), ALU.add, [g2, S[1]], [S[1]])
        o_tt(c, E, S[2][:], gP(2, 0), S[1][:], ALU.mult, [g2, S[1]], [S[2]])
        o_tt(c, E, S[2][:], S[2][:], gH(2, 0), ALU.add, [g2, S[2]], [S[2]])
        o_ts(c, E, hin[:, :, 0:1], S[0][:], m(1), None, ALU.mult, None, [S[0], self.masks], [self.hin])
        o_stt(c, E, hin[:, :, 0:1], S[1][:], m(2), hin[:, :, 0:1], ALU.mult, ALU.add, [S[1], self.masks, self.hin], [self.hin])
        o_stt(c, E, hin[:, :, 0:1], S[2][:], m(3), hin[:, :, 0:1], ALU.mult, ALU.add, [S[2], self.masks, self.hin], [self.hin])
        o_copy(c, E, S[0][:], gH(3, 1), [g2, self.hin], [S[0]])
        o_tt(c, E, S[1][:], gP(2, 1), S[0][:], ALU.mult, [g2, S[0]], [S[1]])
        o_tt(c, E, S[1][:], S[1][:], gH(2, 1), ALU.add, [g2, S[1]], [S[1]])
        o_tt(c, E, S[2][:], gP(1, 1), S[1][:], ALU.mult, [g2, S[1]], [S[2]])
        o_tt(c, E, S[2][:], S[2][:], gH(1, 1), ALU.add, [g2, S[2]], [S[2]])
        o_ts(c, E, hin[:, :, 1:2], S[0][:], m(2), None, ALU.mult, None, [S[0], self.masks], [self.hin])
        o_stt(c, E, hin[:, :, 1:2], S[1][:], m(1), hin[:, :, 1:2], ALU.mult, ALU.add, [S[1], self.masks, self.hin], [self.hin])
        o_stt(c, E, hin[:, :, 1:2], S[2][:], m(0), hin[:, :, 1:2], ALU.mult, ALU.add, [S[2], self.masks, self.hin], [self.hin])
        c.end_stage()
        c.begin_stage()
        ld = [[c.sb("ld%d_%d" % (q, i), [128, T_], F32) for q in range(4)] for i in range(2)]
        sg = [c.sb("sg%d" % i, [128, T_], BF16) for i in range(2)]
        hf = c.sb("hf", [128, T_], F32)
        hb = c.sb("hb", [128, T_], F32)
        yo = [c.sb("yo%d" % i, [128, T_], BF16) for i in range(2)]
        for cc in range(24):
            i = cc % 2
            rows = slice(cc * 128, (cc + 1) * 128)
            for q in range(4):
                c.dma(c.sp, ld[i][q][:], self.SPL[q][rows, :], ld[i][q])
            c.dma(c.sp, sg[i][:], self.SG[rows, :], sg[i])
            for si, (t0, L) in enumerate(cfg.segs):
                sl = slice(t0, t0 + L)
                ci = (cc * 2 + si) * 2
                f0 = slice(t0, t0 + 1)
                l0 = slice(t0 + L - 1, t0 + L)
                o_stt(c, c.dve, ld[i][1][:, f0], ld[i][0][:, f0], self.hin[:, ci:ci + 1], ld[i][1][:, f0], ALU.mult, ALU.add,
                      [ld[i][0], ld[i][1], self.hin], [ld[i][1]])
                o_stt(c, c.dve, ld[i][3][:, l0], ld[i][2][:, l0], self.hin[:, ci + 1:ci + 2], ld[i][3][:, l0], ALU.mult, ALU.add,
                      [ld[i][2], ld[i][3], self.hin], [ld[i][3]])
                c.op(c.dve, lambda h_, sl=sl, i=i: h_.tensor_tensor_scan(out=hf[:, sl], data0=ld[i][0][:, sl], data1=ld[i][1][:, sl],
                                                                    initial=0.0, op0=ALU.mult, op1=ALU.add),
                     [ld[i][0], ld[i][1]], [hf])
                c.op(c.dve, lambda h_, sl=sl, i=i: h_.tensor_tensor_scan(out=rev_ap(hb[:, sl]), data0=rev_ap(ld[i][2][:, sl]),
                                                                    data1=rev_ap(ld[i][3][:, sl]),
                                                                    initial=0.0, op0=ALU.mult, op1=ALU.add),
                     [ld[i][2], ld[i][3]], [hb])
            o_tt(c, c.pool, hf[:], hf[:], hb[:], ALU.add, [hf, hb], [hf])
            o_tt(c, c.pool, yo[i][:], hf[:], sg[i][:], ALU.mult, [hf, sg[i]], [yo[i]])
            c.dma(c.sp, self.Y[rows, :], yo[i][:], yo[i], load=False)
        c.end_stage()

    def stage_pool(self):
        c, cfg = self.c, self.cfg
        T_, NT = cfg.T, cfg.NT
        ps = self.ps
        TE = T_ + 30
        c.begin_stage()
        hl_, hr_ = self.halo_exchange(8, 7, "p")
        xe = [c.sb("xe%d" % i, [128, TE], F32) for i in range(2)]
        sA = [c.sb("sA%d" % i, [128, TE], F32) for i in range(2)]
        sB = [c.sb("sB%d" % i, [128, TE], F32) for i in range(2)]
        invg = c.sb("invg", [128, T_], F32)
        pooled = c.sb("pooled", [128, 6, T_], BF16)
        pooledT = [c.view(pooled[:, k, :]) for k in range(6)]
        wf = [c.sb("wf%d" % i, [128, 6, 128], F32) for i in range(2)]
        wb = [c.sb("wb%d" % i, [128, 6, 128], BF16) for i in range(2)]
        sgt = [c.sb("sgt%d" % i, [128, 1536], BF16) for i in range(2)]
        of = [c.sb("of%d" % i, [128, 1536], F32) for i in range(2)]
        ob = [c.sb("ob%d" % i, [128, 1536], BF16) for i in range(2)]
        for gi in range(4):
            w = 2 << gi
            c.dma(c.sp, invg[:], self.invc_d[gi:gi + 1, :].partition_broadcast(128), invg)
            for kq in range(6):
                cc = gi * 6 + kq
                i = cc % 2
                E = c.dve if i == 0 else c.pool
                for si, (t0, L) in enumerate(cfg.segs):
                    e0 = t0 + 15 * si
                    c.dma(c.sp, xe[i][:, e0 + 8:e0 + 8 + L], self.XS[cc * 128:(cc + 1) * 128, t0:t0 + L], xe[i])
                for si, (t0, L) in enumerate(cfg.segs):
                    e0 = t0 + 15 * si
                    o_copy(c, E, xe[i][:, e0:e0 + 8], hl_[:, cc, si, :], [hl_], [xe[i]])
                    o_copy(c, E, xe[i][:, e0 + 8 + L:e0 + 15 + L], hr_[:, cc, si, :], [hr_], [xe[i]])
                for si, (t0, L) in enumerate(cfg.segs):
                    e0 = t0 + 15 * si
                    Le = L + 15
                    o_tt(c, E, sA[i][:, e0 + 1:e0 + Le], xe[i][:, e0:e0 + Le - 1], xe[i][:, e0 + 1:e0 + Le], ALU.add, [xe[i]], [sA[i]])
                    cur, oth = sA[i], sB[i]
                    lo, hi, hw = 1, Le, 1
                    while hw * 2 < w:
                        nlo, nhi = lo + hw, hi - hw
                        o_tt(c, E, oth[:, e0 + nlo:e0 + nhi], cur[:, e0 + nlo - hw:e0 + nhi - hw], cur[:, e0 + nlo + hw:e0 + nhi + hw],
                             ALU.add, [cur], [oth])
                        cur, oth = oth, cur
                        lo, hi, hw = nlo, nhi, hw * 2
                    o_tt(c, E, oth[:, e0 + 8:e0 + 8 + L], cur[:, e0 + 8:e0 + 8 + L], invg[:, t0:t0 + L], ALU.mult, [cur, invg], [oth])
                    o_tt(c, E, pooled[:, kq, t0:t0 + L], oth[:, e0 + 8:e0 + 8 + L], xe[i][:, e0 + 8:e0 + 8 + L], ALU.subtract,
                         [oth, xe[i]], [pooledT[kq]])
            jobs = [dict(src=self.wgrp[gi * 6 + m_], w=128, n=gi * 6 + m_) for m_ in range(6)]
            st = {"i": 0}

            def epi(ji, job, s, banks):
                i = st["i"] % 2
                st["i"] += 1
                n = len(s) * 512
                tok0 = s[0] * 512
                cc = job["n"]
                rows = slice(cc * 128, (cc + 1) * 128)
                c.dma(c.sp, sgt[i][:, 0:n], self.SG[rows, tok0:tok0 + n], sgt[i])
                for idx, t in enumerate(s):
                    sl = slice(idx * 512, (idx + 1) * 512)
                    o_ts(c, c.dve, of[i][:, sl], banks[idx][:, :], self.vcol("c_scale", cc), None, ALU.mult, None,
                         [banks[idx], self.vec], [of[i]])
                o_tt(c, c.pool, ob[i][:, 0:n], of[i][:, 0:n], sgt[i][:, 0:n], ALU.mult, [of[i], sgt[i]], [ob[i]])
                c.dma(c.sp, self.Y[rows, tok0:tok0 + n], ob[i][:, 0:n], ob[i], load=False)
            nset = 3 if NT % 3 == 0 else 2
            self.gemm([(pooledT[k], pooled[:, k, :]) for k in range(6)], 6, list(range(NT)), jobs, epi, wf, wb,
                      [ps[0:nset], ps[nset:2 * nset]])
        c.end_stage()

    def stage_mla(self):
        c, cfg, nc = self.c, self.cfg, self.nc
        T_, NT = cfg.T, cfg.NT
        ps = self.ps
        LATi = [t.ap() for t in self.LATi]
        LATo = [t.ap() for t in self.LATo]
        c.begin_stage()
        cq = c.sb("cq", [128, 6, T_], F32)
        cqT = [c.view(cq[:, k, :]) for k in range(6)]
        sq = [c.sb("sq%d" % i, [128, T_], BF16) for i in range(2)]
        rs = [c.sb("rs%d" % i, [128, T_], F32) for i in range(2)]
        nb = [c.sb("nb%d" % i, [128, T_], BF16) for i in range(2)]
        ka = c.sb("ka", [64, T_], F32)
        kb = c.sb("kb", [64, T_], F32)
        ck = c.sb("ck", [64, T_], F32)
        sk = c.sb("sk", [64, T_], F32)
        ko = c.sb("ko", [64, T_], BF16)
        for k in range(6):
            src = self.CQ[k * 128:(k + 1) * 128, :] if k < 4 else self.CKV[(k - 4) * 128:(k - 3) * 128, :]
            c.dma(c.sp, cq[:, k, :], src, cqT[k])
        for which, (k0, k1, feat) in enumerate(((0, 4, 512.0), (4, 6, 256.0))):
            for k in range(k0, k1):
                o_act(c, sq[k % 2][:], cq[:, k, :], AF.Square, [cqT[k]], [sq[k % 2]])
                for t in range(NT):
                    o_mm(c, ps[t][:, :], self.ones[:], sq[k % 2][:, t * 512:(t + 1) * 512], k == k0, k == k1 - 1,
                         [self.ones, sq[k % 2]], [ps[t]], inc=True)
            for t in range(NT):
                tk = slice(t * 512, (t + 1) * 512)
                o_act(c, rs[which][:, tk], ps[t][:, :], AF.Sqrt, [ps[t], self.cst], [rs[which]], bias=self.cst[:, 0:1], scale=1.0 / feat)
            c.op(c.dve, lambda h, which=which: h.reciprocal(out=rs[which][:], in_=rs[which][:]), [rs[which]], [rs[which]])
        for k in range(6):
            which = 0 if k < 4 else 1
            gcol = self.vcol("q_norm", k) if k < 4 else self.vcol("kv_norm", k - 4)
            o_stt(c, c.dve, nb[k % 2][:], cq[:, k, :], gcol, rs[which][:], ALU.mult, ALU.mult, [cqT[k], rs[which], self.vec], [nb[k % 2]])
            if k < 4:
                c.dma(c.sp, self.CQN[k * 128:(k + 1) * 128, :], nb[k % 2][:], nb[k % 2], load=False)
            else:
                c.dma(c.sp, LATi[k - 4][:, :], nb[k % 2][:], nb[k % 2], load=False)
        c.dma(c.sp, ka[:], self.KR[0:64, :], ka)
        c.dma(c.sp, kb[:], self.KR[64:128, :], kb)
        c.dma(c.sp, ck[:], self.rope_d[2], ck)
        c.dma(c.sp, sk[:], self.rope_d[3], sk)
        o_tt(c, c.dve, ka[:], ka[:], ck[:], ALU.mult, [ka, ck], [ka])
        o_tt(c, c.pool, kb[:], kb[:], sk[:], ALU.mult, [kb, sk], [kb])
        o_tt(c, c.dve, ko[:], ka[:], kb[:], ALU.add, [ka, kb], [ko])
        c.dma(c.sp, LATi[2][:, :], ko[:], ko, load=False)
        c.end_stage()
        sems = c.sems
        for gi_ in range(3):
            key = "cc%d" % self.ncc
            self.ncc += 1

            def emit_cc(h, key=key, gi_=gi_):
                h.collective_compute("AllGather", ALU.bypass, replica_groups=[[0, 1, 2, 3], [4, 5, 6, 7]],
                                     ins=[self.LATi[gi_].ap().opt()], outs=[self.LATo[gi_].ap().opt()]).then_inc(sems[key])
                h.wait_ge(sems[key], 1)
            c.pool.prog.append(emit_cc)
        c.op(c.pool, lambda h: h.memset(self.cst[:, 3:4], 0.0), (), [self.cst])
        c.drain_all()
        SCALE = 192.0 ** -0.5
        import os as _os3
        seg_order = list(enumerate(cfg.segs))
        if _os3.environ.get("KSWAP"):
            seg_order = seg_order[::-1]
        for si, (t0, L) in seg_order:
            Lk = 4 * L
            NQ = L // 512
            NKB = Lk // 128
            c.begin_stage()
            ckva = c.sb("ckva", [128, 2, Lk], BF16)
            kra = c.sb("kra", [64, Lk], BF16)
            cqn = c.sb("cqn", [128, 4, L], BF16)
            Cq = c.sb("Cq", [64, L], F32)
            Sq = c.sb("Sq", [64, L], F32)
            khT = c.sb("khT", [128, Lk], BF16)
            vh = c.sb("vh", [128, NKB, 128], BF16)
            qn = c.sb("qn", [128, L], BF16)
            qr = c.sb("qr", [64, L], BF16)
            wqf = c.sb("wqf", [128, 4, 192], F32)
            wqb = c.sb("wqb", [128, 4, 256], BF16)
            wkf = c.sb("wkf", [128, 2, 256], F32)
            wkb = c.sb("wkb", [128, 2, 256], BF16)
            pT = [c.sb("pT%d" % i, [128, 512], BF16) for i in range(8)]
            t1 = c.sb("t1", [64, 512], F32)
            t2 = c.sb("t2", [64, 512], F32)
            rec = c.sb("rec", [128, 512], F32)
            ot = c.sb("ot", [128, 512], F32)
            accs = [c.sb("acc%d" % i, [128, 512], F32) for i in range(4)]
            sgt = [c.sb("sgt%d" % i, [128, L], BF16) for i in range(2)]
            yh = [c.sb("yh%d" % i, [128, L], BF16) for i in range(2)]
            for r in range(4):
                for kc_ in range(2):
                    c.dma(c.sp, ckva[:, kc_, r * L:(r + 1) * L], LATo[kc_][r * 128:(r + 1) * 128, t0:t0 + L], ckva)
                c.dma(c.sp, kra[:, r * L:(r + 1) * L], LATo[2][r * 64:(r + 1) * 64, t0:t0 + L], kra)
            c.dma(c.sp, cqn[:], self.CQN[:, t0:t0 + L].rearrange("(k p) t -> p k t", p=128), cqn)
            c.dma(c.sp, Cq[:], self.rope_d[0][:, t0:t0 + L], Cq)
            c.dma(c.sp, Sq[:], self.rope_d[1][:, t0:t0 + L], Sq)
            for hd in range(24):
                hi = hd % 2
                c.dma(c.sp, wqf[:], self.wq[hd].rearrange("p (kc n) -> p kc n", kc=4), wqf)
                c.dma(c.sp, wkf[:], self.wkv[hd].rearrange("p (kc n) -> p kc n", kc=2), wkf)
                c.dma(c.sp, sgt[hi][:], self.SG[hd * 128:(hd + 1) * 128, t0:t0 + L], sgt[hi])
                o_copy(c, c.pool, wqb[:, :, 0:192], wqf[:, :, 0:192], [wqf], [wqb])
                o_ts(c, c.pool, wqb[:, :, 192:224], wqf[:, :, 160:192], -1.0, None, ALU.mult, None, [wqf], [wqb])
                o_copy(c, c.pool, wqb[:, :, 224:256], wqf[:, :, 128:160], [wqf], [wqb])
                o_copy(c, c.pool, wkb[:], wkf[:], [wkf], [wkb])
                for qt in range(NQ):
                    tk = slice(qt * 512, (qt + 1) * 512)
                    pn, pa, pb = ps[0], ps[1], ps[2]
                    for kc in range(4):
                        o_mm(c, pn[:, :], wqb[:, kc, 0:128], cqn[:, kc, tk], kc == 0, kc == 3, [wqb, cqn], [pn])
                    for kc in range(4):
                        o_mm(c, pa[0:64, :], wqb[:, kc, 128:192], cqn[:, kc, tk], kc == 0, kc == 3, [wqb, cqn], [pa])
                    for kc in range(4):
                        o_mm(c, pb[0:64, :], wqb[:, kc, 192:256], cqn[:, kc, tk], kc == 0, kc == 3, [wqb, cqn], [pb])
                    c.op(c.act, lambda h, tk=tk, pn=pn, qn=qn: h.mul(out=qn[:, tk], in_=pn[:, :], mul=SCALE), [pn], [qn])
                    o_tt(c, c.dve, t1[:], pa[0:64, :], Cq[:, tk], ALU.mult, [pa, Cq], [t1])
                    o_tt(c, c.dve, t2[:], pb[0:64, :], Sq[:, tk], ALU.mult, [pb, Sq], [t2])
                    o_tt(c, c.dve, qr[:, tk], t1[:], t2[:], ALU.add, [t1, t2], [qr])
                for kt in range(Lk // 512):
                    tk = slice(kt * 512, (kt + 1) * 512)
                    pk = ps[kt % 2]
                    for kc in range(2):
                        o_mm(c, pk[:, :], wkb[:, kc, 0:128], ckva[:, kc, tk], kc == 0, kc == 1, [wkb, ckva], [pk])
                    o_copy(c, c.dve if kt % 2 == 0 else c.act, khT[:, tk], pk[:, :], [pk], [khT])
                for k4 in range(NKB // 4):
                    pv = ps[2 + k4 % 2]
                    for j in range(4):
                        kbk = k4 * 4 + j
                        for kc in range(2):
                            o_mm(c, pv[:, j * 128:(j + 1) * 128], ckva[:, kc, kbk * 128:(kbk + 1) * 128], wkb[:, kc, 128:256],
                                 kc == 0, kc == 1, [wkb, ckva], [pv])
                    o_copy(c, c.dve if k4 % 2 == 0 else c.act, vh[:, k4 * 4:(k4 + 1) * 4, :],
                           pv[:, :].rearrange("p (j d) -> p j d", j=4), [pv], [vh])
                NQG = min(NQ, 4)
                nsets = 4 // NQG
                for qg in range(NQ // NQG):
                    qts = [qg * NQG + i_ for i_ in range(NQG)]
                    tks = [slice(qt * 512, (qt + 1) * 512) for qt in qts]
                    pos = [ps[4 + i_] for i_ in range(NQG)]

                    def pv_mm(kbk, pts):
                        for i_ in range(NQG):
                            o_mm(c, pos[i_][:, :], vh[:, kbk, :], pts[i_][:], kbk == 0, kbk == NKB - 1, [vh, pts[i_]], [pos[i_]])
                    prev = None
                    for kbk in range(NKB):
                        sset = [ps[(kbk % nsets) * NQG + i_] for i_ in range(NQG)]
                        ks = slice(kbk * 128, (kbk + 1) * 128)
                        for i_ in range(NQG):
                            c.op(c.pe, lambda h, sb_=sset[i_], ks=ks, tk=tks[i_], khT=khT, qn=qn:
                                 h.matmul(sb_[:, :], lhsT=khT[:, ks], rhs=qn[:, tk], start=True, stop=False),
                                 [khT, qn], [sset[i_]], inc=False)
                        for i_ in range(NQG):
                            o_mm(c, sset[i_][:, :], kra[:, ks], qr[:, tks[i_]], False, True, [kra, qr], [sset[i_]])
                        pts = []
                        for i_ in range(NQG):
                            p_ = pT[(kbk % 2) * NQG + i_]
                            o_act(c, p_[:], sset[i_][:, :], AF.Exp, [sset[i_]], [p_])
                            Ea = c.dve if i_ % 2 == 0 else c.pool
                            if kbk == 0:
                                o_copy(c, Ea, accs[i_][:], p_[:], [p_], [accs[i_]])
                            else:
                                o_tt(c, Ea, accs[i_][:], accs[i_][:], p_[:], ALU.add, [accs[i_], p_], [accs[i_]])
                            pts.append(p_)
                        if prev is not None:
                            pv_mm(*prev)
                        prev = (kbk, pts)
                    pv_mm(*prev)
                    for i_ in range(NQG):
                        pm = ps[i_]
                        o_mm(c, pm[:, :], self.ones_f[:], accs[i_][:], True, True, [self.ones_f, accs[i_]], [pm])
                        c.op(c.dve, lambda h, pm=pm, rec=rec: h.reciprocal(out=rec[:], in_=pm[:, :]), [pm], [rec])
                        o_tt(c, c.dve, ot[:], pos[i_][:, :], rec[:], ALU.mult, [pos[i_], rec], [ot])
                        o_tt(c, c.pool, yh[hi][:, tks[i_]], ot[:], sgt[hi][:, tks[i_]], ALU.mult, [ot, sgt[hi]], [yh[hi]])
                c.dma(c.sp, self.Y[hd * 128:(hd + 1) * 128, t0:t0 + L], yh[hi][:], yh[hi], load=False)
            c.end_stage()


def _rope_tables(pos):
    inv_freq = (1.0 / (10000.0 ** (np.arange(0, 64, 2, dtype=np.float32) / np.float32(64.0)))).astype(np.float32)
    ang = (pos.astype(np.float32)[:, None] * inv_freq[None, :]).astype(np.float32)
    cs = np.cos(ang).astype(np.float32).T
    sn = np.sin(ang).astype(np.float32).T
    C = np.concatenate([cs, cs], axis=0)
    S = np.concatenate([sn, sn], axis=0)
    s = np.float32(192.0 ** -0.5)
    return np.ascontiguousarray(np.stack([C * s, S * s, C, S]).astype(np.float32))


def _invc(pos, S):
    out = np.zeros((4, pos.size), np.float32)
    for g, w in enumerate((2, 4, 8, 16)):
        st = np.clip(pos - w // 2, 0, S)
        en = np.clip(pos + w - w // 2, 0, S)
        out[g] = 1.0 / (en - st).astype(np.float32)
    return out


_CACHE = {}


def run(inputs, LP, LS, nlayers=4, n_cores=8):
    cfg = Cfg(LP, LS, nlayers)
    f = lambda a: np.ascontiguousarray(np.asarray(a, dtype=np.float32))
    vp = VecPack()
    vp.add("norm_pre", inputs["norm_pre"])
    vp.add("norm_post", inputs["norm_post"])
    vp.add("norm_mem", inputs["norm_mem"])
    vp.add("conv_w", inputs["a_conv_w"])
    vp.add("conv_b", inputs["a_conv_b"])
    vp.add("gate_b", inputs["a_gate_b"])
    vp.add("lam", inputs["a_lambda"])
    vp.add("q_norm", inputs["b_q_norm"])
    vp.add("kv_norm", inputs["b_kv_norm"])
    vp.add("c_scale", inputs["c_scale"])
    vecs = vp.arr()
    prog = Prog(cfg, vp.off, vecs.shape[1])
    nc = prog.build()
    def jm(Wm, cols_list):
        Wm = np.asarray(Wm, np.float32)
        K = Wm.shape[0]
        out = []
        for cols in cols_list:
            blk = Wm[:, cols]
            w = blk.shape[1]
            out.append(blk.reshape(K // 128, 128, w).transpose(1, 0, 2).reshape(128, (K // 128) * w))
        return np.ascontiguousarray(np.stack(out))

    ar = np.arange
    cols_ac = [ar(j * 128, (j + 1) * 128) for j in range(64)]
    cols_b = ([ar(n * 128, (n + 1) * 128) for n in range(6)]
              + [np.concatenate([ar(768, 832), ar(800, 832), ar(768, 800)])]
              + [ar(832 + n * 128, 832 + (n + 1) * 128) for n in range(8)]
              + [ar(1856 + n * 128, 1856 + (n + 1) * 128) for n in range(32)])
    c16 = [ar(j * 128, (j + 1) * 128) for j in range(16)]
    gw = np.asarray(inputs["a_gate_w"], np.float32).reshape(96, 256, 256)
    shared = {
        "Win0": jm(inputs["a_w_in"][0], cols_ac), "Win1": jm(inputs["b_w_in"][0], cols_b),
        "Win2": jm(inputs["c_w_in"][0], cols_ac), "Win3": jm(inputs["a_w_in"][1], cols_ac),
        "gatew": np.ascontiguousarray(np.stack([jm(gw[i], [ar(256)])[0] for i in range(96)])),
        "wq": jm(inputs["b_w_q_up"][0], [ar(h * 192, (h + 1) * 192) for h in range(24)]),
        "wkv": jm(inputs["b_w_kv_up"][0], [ar(h * 256, (h + 1) * 256) for h in range(24)]),
        "wgrp": np.ascontiguousarray(np.concatenate([jm(np.asarray(inputs["c_w_group"])[0, g], [ar(m * 128, (m + 1) * 128) for m in range(6)])
                                                     for g in range(4)])),
        "vecs": vecs, "ident": np.eye(128, dtype=np.float32),
    }
    for i in range(4):
        shared["Wout%d" % i] = jm(inputs["w_out"][i], c16)
        shared["Wmem%d" % i] = jm(inputs["w_mem_kv"][i], c16)
    xp, xs = np.asarray(inputs["x_prompt"]), np.asarray(inputs["x_sample"])
    mp, ms = np.asarray(inputs["mem_prompt"]), np.asarray(inputs["mem_sample"])
    in_maps = []
    for core in range(n_cores):
        g, j = core // 4, core % 4
        m = dict(shared)
        m["xT"] = np.ascontiguousarray(np.concatenate([xp[g, j * LP:(j + 1) * LP].T, xs[g, j * LS:(j + 1) * LS].T], axis=1), np.float32)
        m["memT"] = np.ascontiguousarray(np.concatenate([mp[g].T, ms[g].T], axis=1), np.float32)
        pos = np.concatenate([np.arange(j * LP, (j + 1) * LP), np.arange(j * LS, (j + 1) * LS)])
        m["rope"] = _rope_tables(pos)
        m["invc"] = np.ascontiguousarray(np.concatenate([_invc(pos[:LP], 4 * LP), _invc(pos[LP:], 4 * LS)], axis=1))
        mk = np.zeros((128, 12), np.float32)
        if j > 0:
            mk[:, j - 1] = 1.0
        if j < 3:
            mk[:, 4 + j + 1] = 1.0
        mk[:, 8 + j] = 1.0
        m["masks"] = mk
        in_maps.append(m)
    res = run_bass_kernel_spmd(nc, in_maps, core_ids=list(range(n_cores)))
    global LAST_RES
    LAST_RES = res
    yp = np.zeros((2, 4 * LP, D), np.float32)
    ys = np.zeros((2, 4 * LS, D), np.float32)
    for core in range(n_cores):
        g, j = core // 4, core % 4
        yT = np.asarray(res.results[core]["yT"])
        yp[g, j * LP:(j + 1) * LP] = yT[:, :LP].T
        ys[g, j * LS:(j + 1) * LS] = yT[:, LP:].T
    return yp, ys


def kernel(**inputs):
    return run(inputs, 1024, 2048, 4, 8)
```

```python
from contextlib import ExitStack
import numpy as np
import concourse.bass as bass
import concourse.mybir as mybir
from concourse.ap import AP
from concourse.bass_utils import run_bass_kernel_spmd

F32 = mybir.dt.float32
BF16 = mybir.dt.bfloat16
ALU = mybir.AluOpType
AF = mybir.ActivationFunctionType
AX = mybir.AxisListType

SAME_SYNC = True
D = 2048
MIXW = 3072
XAW = 1024
BR = 4096
EPS = 1e-6


class Eng:
    def __init__(self, name, sem, is_pe=False):
        self.name = name
        self.sem = sem
        self.cnt = 0
        self.waited = {}
        self.is_pe = is_pe
        self.prog = []


class T:
    def __init__(self, ap, name=None):
        self.ap = ap
        self.name = name
        self.w = None
        self.r = {}
        self.dsem = None

    def __getitem__(self, k):
        return self.ap[k]


class Ctx:
    def __init__(self, nc, es):
        self.nc = nc
        self.es = es
        self.sems = {}
        self.engs = {}
        for nm, pe in (("pe", True), ("act", False), ("dve", False), ("pool", False), ("sp", False)):
            self.sems["s_" + nm] = es.enter_context(nc.semaphore("s_" + nm))
            self.engs[nm] = Eng(nm, "s_" + nm, pe)
        self.pe, self.act, self.dve = self.engs["pe"], self.engs["act"], self.engs["dve"]
        self.pool, self.sp = self.engs["pool"], self.engs["sp"]
        self.all_t = []
        self.n_ins = 0
        self._dcnt = {}
        self.dfree = []
        self.stage_ts = None
        self.stage_es = None

    def _reg(self, tt):
        self.all_t.append(tt)
        if self.stage_ts is not None:
            self.stage_ts.append(tt)
        return tt

    def sb(self, name, shape, dt, persist=False):
        es = self.es if (persist or self.stage_es is None) else self.stage_es
        self.uid = getattr(self, "uid", 0) + 1
        name = "sb%d_%s" % (self.uid, name)
        t = es.enter_context(self.nc.sbuf_tensor(name, list(shape), dt))
        tt = T(t, name)
        if persist or self.stage_ts is None:
            self.all_t.append(tt)
        else:
            self._reg(tt)
        return tt

    def ps(self, name, shape, dt=F32):
        t = self.es.enter_context(self.nc.psum_tensor(name, list(shape), dt))
        tt = T(t, name)
        self.all_t.append(tt)
        return tt

    def view(self, ap, name=None):
        return self._reg(T(ap, name))

    def begin_stage(self):
        self.stage_ts = []
        self.stage_es = ExitStack()
        self.stage_es.__enter__()

    def end_stage(self):
        self.drain_all()
        for t in self.stage_ts:
            if t.dsem is not None and t.dsem not in self.dfree:
                self.dfree.append(t.dsem)
        gone = set(id(t) for t in self.stage_ts)
        self.all_t = [t for t in self.all_t if id(t) not in gone]
        self.stage_ts = None
        self.stage_es.__exit__(None, None, None)
        self.stage_es = None

    def _dsem(self, t):
        if t.dsem is None:
            if self.dfree:
                t.dsem = self.dfree.pop()
            else:
                key = "d_%d" % len(self.sems)
                self.sems[key] = self.es.enter_context(self.nc.semaphore(key))
                t.dsem = key
        return t.dsem

    def share_dsem(self, ts):
        k = self._dsem(ts[0])
        for t in ts[1:]:
            t.dsem = k

    def _collect(self, E, reads, writes):
        need = {}

        def add(k, v):
            if need.get(k, 0) < v:
                need[k] = v
        for t in reads:
            if t.w is not None:
                add(*t.w)
        for t in writes:
            if t.w is not None:
                add(*t.w)
            for k, v in t.r.items():
                add(k, v)
        out = []
        for k, v in need.items():
            if k == E.sem and (E.is_pe or not SAME_SYNC):
                continue
            if E.waited.get(k, 0) >= v:
                continue
            E.waited[k] = v
            out.append((k, v))
        return out

    def _record(self, ev, reads, writes):
        for t in reads:
            if t.r.get(ev[0], 0) < ev[1]:
                t.r[ev[0]] = ev[1]
        for t in writes:
            t.w = ev
            t.r = {}

    def op(self, E, fn, reads=(), writes=(), inc=True):
        waits = self._collect(E, reads, writes)
        sems = self.sems
        if inc:
            E.cnt += 1
            ev = (E.sem, E.cnt)
        else:
            ev = (E.sem, E.cnt + 1)
        esem = E.sem

        def emit(h):
            for k, v in waits[1:]:
                h.wait_ge(sems[k], v)
            ins = fn(h)
            if waits:
                ins._wait_ge(sems[waits[0][0]], waits[0][1])
            if inc:
                ins.then_inc(sems[esem], 1)
        E.prog.append(emit)
        self.n_ins += 1 + len(waits)
        self._record(ev, reads, writes)
        return ev

    def dma(self, Q, out, in_, sbt, load=True, reads=(), writes=(), **kw):
        reads = list(reads)
        writes = list(writes)
        if load:
            writes.append(sbt)
        else:
            reads.append(sbt)
        waits = self._collect(Q, reads, writes)
        key = self._dsem(sbt)
        tot = self._dcnt.get(key, 0) + 16
        self._dcnt[key] = tot
        ev = (key, tot)
        sems = self.sems

        def emit(h):
            for k, v in waits[1:]:
                h.wait_ge(sems[k], v)
            ins = h.dma_start(out=out, in_=in_, **kw)
            if waits:
                ins._wait_ge(sems[waits[0][0]], waits[0][1])
            ins.then_inc(sems[key], 16)
        Q.prog.append(emit)
        self.n_ins += 1 + len(waits)
        self._record(ev, reads, writes)
        return ev

    def drain_all(self):
        targets = [(e.sem, e.cnt) for e in self.engs.values() if e.cnt > 0]
        targets += list(self._dcnt.items())
        sems = self.sems
        for E in self.engs.values():
            for k, v in targets:
                if k == E.sem or E.waited.get(k, 0) >= v:
                    continue
                E.waited[k] = v
                E.prog.append(lambda h, k=k, v=v: h.wait_ge(sems[k], v))
                self.n_ins += 1
        for t in self.all_t:
            t.w = None
            t.r = {}

    def emit(self):
        with self.nc.Block() as block:
            @block.tensor
            def _(h):
                for f in self.pe.prog:
                    f(h)

            @block.scalar
            def _(h):
                for f in self.act.prog:
                    f(h)

            @block.vector
            def _(h):
                for f in self.dve.prog:
                    f(h)

            @block.gpsimd
            def _(h):
                for f in self.pool.prog:
                    f(h)

            @block.sync
            def _(h):
                for f in self.sp.prog:
                    f(h)


def o_tt(c, E, out, a, b, op, R, W):
    return c.op(E, lambda h: h.tensor_tensor(out=out, in0=a, in1=b, op=op), R, W)


def o_ts(c, E, out, a, s1, s2, op0, op1, R, W):
    if s2 is None:
        return c.op(E, lambda h: h.tensor_scalar(out=out, in0=a, scalar1=s1, scalar2=None, op0=op0), R, W)
    return c.op(E, lambda h: h.tensor_scalar(out=out, in0=a, scalar1=s1, scalar2=s2, op0=op0, op1=op1), R, W)


def o_stt(c, E, out, a, s, b, op0, op1, R, W):
    return c.op(E, lambda h: h.scalar_tensor_tensor(out=out, in0=a, scalar=s, in1=b, op0=op0, op1=op1), R, W)


def o_act(c, out, in_, func, R, W, bias=None, scale=None):
    kw = {}
    if bias is not None:
        kw["bias"] = bias
    if scale is not None:
        kw["scale"] = scale
    return c.op(c.act, lambda h: h.activation(out=out, in_=in_, func=func, **kw), R, W)


def o_copy(c, E, out, in_, R, W):
    if E is c.act:
        return c.op(E, lambda h: h.copy(out=out, in_=in_), R, W)
    return c.op(E, lambda h: h.tensor_copy(out=out, in_=in_), R, W)


def o_mm(c, out, lhsT, rhs, start, stop, R, W, inc=None):
    return c.op(c.pe, lambda h: h.matmul(out, lhsT=lhsT, rhs=rhs, start=start, stop=stop), R, W,
                inc=(stop if inc is None else inc))


def rev_ap(ap2d):
    n = ap2d.shape[-1]
    last = ap2d[:, n - 1:n]
    return AP(ap2d.tensor, last.offset, [list(last.ap[0]), [-1, n]])


class Cfg:
    def __init__(self, LP, LS, nlayers=4):
        self.LP, self.LS = LP, LS
        self.T = LP + LS
        self.NT = self.T // 512
        self.segs = [(0, LP), (LP, LS)]
        self.nlayers = nlayers


class VecPack:
    def __init__(self):
        self.cols = []
        self.off = {}
        self.n = 0

    def add(self, name, v):
        v = np.asarray(v, np.float32).reshape(-1)
        assert v.size % 128 == 0 or v.size < 128
        if v.size < 128:
            a = np.zeros((128, 1), np.float32)
            a[:v.size, 0] = v
        else:
            a = np.ascontiguousarray(v.reshape(-1, 128).T)
        self.off[name] = (self.n, a.shape[1])
        self.cols.append(a)
        self.n += a.shape[1]

    def arr(self):
        return np.ascontiguousarray(np.concatenate(self.cols, axis=1))


class Prog:
    def __init__(self, cfg, voff, nv):
        self.cfg = cfg
        self.voff = voff
        T_ = cfg.T
        import os as _os
        nc = bass.Bass("TRN2", target_bir_lowering=False)
        self.nc = nc
        dt = nc.dram_tensor
        I = dict(kind="ExternalInput")
        self.xT = dt("xT", [D, T_], F32, **I).ap()
        self.memT = dt("memT", [D, 512], F32, **I).ap()
        self.Win = [dt("Win%d" % i, [(64, 47, 64, 64)[i], 128, 16 * 128], F32, **I).ap() for i in range(4)]
        self.Wout = [dt("Wout%d" % i, [16, 128, 32 * 128], F32, **I).ap() for i in range(4)]
        self.Wmem = [dt("Wmem%d" % i, [16, 128, 16 * 128], F32, **I).ap() for i in range(4)]
        self.gatew = dt("gatew", [96, 128, 2 * 256], F32, **I).ap()
        self.wq = dt("wq", [24, 128, 4 * 192], F32, **I).ap()
        self.wkv = dt("wkv", [24, 128, 2 * 256], F32, **I).ap()
        self.wgrp = dt("wgrp", [24, 128, 6 * 128], F32, **I).ap()
        self.vecs_d = dt("vecs", [128, nv], F32, **I).ap()
        self.rope_d = dt("rope", [4, 64, T_], F32, **I).ap()
        self.invc_d = dt("invc", [4, T_], F32, **I).ap()
        self.masks_d = dt("masks", [128, 12], F32, **I).ap()
        self.ident_d = dt("ident", [128, 128], F32, **I).ap()
        self.CQN = dt("CQN", [512, T_], BF16, **(dict(kind="ExternalOutput") if _os.environ.get("KDBG") == "2" else {})).ap()
        self.yT = dt("yT", [D, T_], F32, kind="ExternalOutput").ap()
        self.X = dt("X", [D, T_], F32).ap()
        self.Z = dt("Z", [D, T_], F32).ap()
        self.XS = dt("XS", [MIXW, T_], F32).ap()
        self.XQ = dt("XQ", [XAW, T_], BF16).ap()
        self.SG = dt("SG", [BR, T_], BF16).ap()
        import os as _os2
        dbgk = dict(kind="ExternalOutput") if _os2.environ.get("KDBG") == "2" else {}
        self.Y = dt("Y", [BR, T_], BF16, **dbgk).ap()
        self.SPL = [dt("SPL%d" % i, [MIXW, T_], F32).ap() for i in range(4)]
        self.CQ = dt("CQ", [512, T_], F32).ap()
        self.CKV = dt("CKV", [256, T_], F32).ap()
        self.KR = dt("KR", [128, T_], F32).ap()
        self.LATi = [dt("LATi%d" % i, [(128, 128, 64)[i], T_], BF16) for i in range(3)]
        self.LATo = [dt("LATo%d" % i, [4 * (128, 128, 64)[i], T_], BF16) for i in range(3)]
        self.ncc = 0
        self.dbg = None
        if _os.environ.get("KDBG") == "1":
            self.dbg = [dt("XL%d" % i, [D, T_], F32, kind="ExternalOutput").ap() for i in range(cfg.nlayers)]

    def vcol(self, name, j=0, n=1):
        o, w = self.voff[name]
        return self.vec[:, o + j:o + j + n]

    def build(self):
        nc, cfg = self.nc, self.cfg
        with ExitStack() as es:
            c = Ctx(nc, es)
            self.c = c
            for i in range(10):
                c.sems["cc%d" % i] = es.enter_context(nc.semaphore("cc%d" % i))
            self.vec = c.sb("vec", [128, self.vecs_d.shape[1]], F32, persist=True)
            self.masks = c.sb("masks", [128, 12], F32, persist=True)
            self.ones = c.sb("ones", [128, 128], BF16, persist=True)
            self.cst = c.sb("cst", [128, 4], F32, persist=True)
            self.ones_f = c.sb("ones_f", [128, 128], F32, persist=True)
            self.hin = c.sb("hin", [128, 24 * 2 * 2], F32, persist=True)
            self.ps = [c.ps("ps%d" % i, [128, 512], F32) for i in range(8)]
            c.dma(c.sp, self.vec[:], self.vecs_d[:, :], self.vec)
            c.dma(c.sp, self.masks[:], self.masks_d[:, :], self.masks)
            c.op(c.pool, lambda h: h.memset(self.ones[:], 1.0), (), [self.ones])
            c.op(c.pool, lambda h: h.memset(self.ones_f[:], 1.0), (), [self.ones_f])
            c.op(c.pool, lambda h: h.memset(self.cst[:, 0:1], EPS), (), [self.cst])
            c.op(c.pool, lambda h: h.memset(self.cst[:, 1:2], 1.0), (), [self.cst])
            c.op(c.pool, lambda h: h.memset(self.cst[:, 2:3], 0.0), (), [self.cst])
            c.drain_all()
            import os as _os
            lim = int(_os.environ.get("KSTAGES", "1000"))
            ns = 0
            for li in range(cfg.nlayers):
                kind = li % 3
                xsrc = self.xT if li == 0 else self.X
                xdst = self.yT if li == cfg.nlayers - 1 else self.X
                steps = [lambda: self.stage_inproj(li, kind, xsrc), lambda: self.stage_xattn(li),
                         (lambda: self.stage_rglru(li // 3)) if kind == 0 else
                         ((lambda: self.stage_mla()) if kind == 1 else (lambda: self.stage_pool())),
                         lambda: self.stage_outproj(li, xsrc, xdst)]
                for stp in steps:
                    if ns < lim:
                        stp()
                    ns += 1
            c.emit()
        return nc

    def exchange(self, contrib, W, name):
        c, nc = self.c, self.nc
        ein = nc.dram_tensor("exi_" + name, [128, W], F32)
        eout = nc.dram_tensor("exo_" + name, [4 * 128, W], F32)
        g = c.sb("exg_" + name, [128, 4, W], F32)
        ev = c.dma(c.pool, ein.ap()[:, :], contrib[:, 0:W], contrib, load=False)
        self._cc(ein, eout, ev)
        c.dma(c.pool, g[:], eout.ap().rearrange("(r p) w -> p r w", p=128), g)
        return g

    def _cc(self, ein, eout, ev):
        c, nc = self.c, self.nc
        key = "cc%d" % self.ncc
        self.ncc += 1
        sems = c.sems
        k0, v0 = ev

        def emit(h):
            h.wait_ge(sems[k0], v0)
            h.collective_compute("AllGather", ALU.bypass, replica_groups=[[0, 1, 2, 3], [4, 5, 6, 7]],
                                 ins=[ein.ap().opt()], outs=[eout.ap().opt()]).then_inc(sems[key])
            h.wait_ge(sems[key], 1)
        c.pool.prog.append(emit)
        c.pool.waited[k0] = max(c.pool.waited.get(k0, 0), v0)
        return key

    def rank_select(self, out_ap, g_ap_fn, mcol0, outT, gT, first_E=None):
        c = self.c
        E = c.dve
        for r in range(4):
            m = self.masks[:, mcol0 + r:mcol0 + r + 1]
            if r == 0:
                o_ts(c, E, out_ap, g_ap_fn(r), m, None, ALU.mult, None, [gT, self.masks], [outT])
            else:
                o_stt(c, E, out_ap, g_ap_fn(r), m, out_ap, ALU.mult, ALU.add, [gT, self.masks, outT], [outT])

    def gemm(self, xk, KC, tiles, jobs, epi, wf, wb, bank_sets, tile_w=512, hook=None):
        c = self.c
        nset = len(bank_sets[0])
        sets = [tiles[i:i + nset] for i in range(0, len(tiles), nset)]
        state = {"si": 0}

        def load(ji):
            job = jobs[ji]
            b = ji % 2
            w = job["w"]
            c.dma(c.sp, wf[b][:, 0:KC, 0:w], job["src"].rearrange("p (kc n) -> p kc n", kc=KC), wf[b])
            neg = job.get("neg")
            if neg is None:
                o_copy(c, c.pool, wb[b][:, 0:KC, 0:w], wf[b][:, 0:KC, 0:w], [wf[b]], [wb[b]])
            else:
                lo, hi = neg
                o_copy(c, c.pool, wb[b][:, 0:KC, 0:lo], wf[b][:, 0:KC, 0:lo], [wf[b]], [wb[b]])
                o_ts(c, c.pool, wb[b][:, 0:KC, lo:hi], wf[b][:, 0:KC, lo:hi], -1.0, None, ALU.mult, None, [wf[b]], [wb[b]])
                o_copy(c, c.pool, wb[b][:, 0:KC, hi:w], wf[b][:, 0:KC, hi:w], [wf[b]], [wb[b]])
            if hook is not None:
                hook(ji)

        load(0)
        for ji, job in enumerate(jobs):
            if ji + 1 < len(jobs):
                load(ji + 1)
            b = ji % 2
            w = job["w"]
            for s in sets:
                banks = bank_sets[state["si"] % len(bank_sets)]
                state["si"] += 1
                for k in range(KC):
                    xt_, xap = xk[k]
                    kp = xap.shape[0]
                    for idx, t in enumerate(s):
                        o_mm(c, banks[idx][0:w, 0:tile_w], wb[b][0:kp, k, 0:w], xap[:, t * tile_w:(t + 1) * tile_w],
                             k == 0, k == KC - 1, [wb[b], xt_], [banks[idx]])
                epi(ji, job, s, banks)

    def rms_stats(self, tiles_fn, nchunks, ntile, tile_w, rstd, rstdT, sqbufs, feat):
        raise NotImplementedError

    def stage_inproj(self, li, kind, xsrc):
        c, cfg = self.c, self.cfg
        T_, NT = cfg.T, cfg.NT
        H = T_ // 2
        c.begin_stage()
        hg = c.sb("hg", [128, 16, T_], BF16)
        hgT = [c.view(hg[:, k, :], "hg%d" % k) for k in range(16)]
        rstd = c.sb("rstd", [128, T_], F32)
        rstdT = [c.view(rstd[:, t * 512:(t + 1) * 512]) for t in range(NT)]
        xt = [c.sb("xt%d" % i, [128, H], F32) for i in range(2)]
        sq = [c.sb("sq%d" % i, [128, H], BF16) for i in range(2)]
        wf = [c.sb("wf%d" % i, [128, 16, 128], F32) for i in range(2)]
        wb = [c.sb("wb%d" % i, [128, 16, 128], BF16) for i in range(2)]
        of = [c.sb("of%d" % i, [128, 1536], F32) for i in range(2)]
        ob = [c.sb("ob%d" % i, [128, 1536], BF16) for i in range(2)]
        ps = self.ps
        u = 0
        for ch in range(16):
            for hf in range(2):
                b = u % 2
                u += 1
                c.dma(c.sp, xt[b][:], xsrc[ch * 128:(ch + 1) * 128, hf * H:(hf + 1) * H], xt[b])
                o_ts(c, c.pool, hg[:, ch, hf * H:(hf + 1) * H], xt[b][:], self.vcol("norm_pre", li * 16 + ch), None,
                     ALU.mult, None, [xt[b], self.vec], [hgT[ch]])
                o_act(c, sq[b][:], xt[b][:], AF.Square, [xt[b]], [sq[b]])
                for tt in range(NT // 2):
                    t = hf * (NT // 2) + tt
                    o_mm(c, ps[t][:, :], self.ones[:], sq[b][:, tt * 512:(tt + 1) * 512], ch == 0, ch == 15,
                         [self.ones, sq[b]], [ps[t]], inc=True)
        for t in range(NT):
            o_act(c, rstd[:, t * 512:(t + 1) * 512], ps[t][:, :], AF.Sqrt, [ps[t], self.cst], [rstdT[t]],
                  bias=self.cst[:, 0:1], scale=1.0 / D)
            c.op(c.dve, lambda h, t=t: h.reciprocal(out=rstd[:, t * 512:(t + 1) * 512], in_=rstd[:, t * 512:(t + 1) * 512]),
                 [rstdT[t]], [rstdT[t]])
        W = self.Win[li]
        jobs = []

        def J(kind_, dest, row0, const=1.0, neg=None):
            ji = len(jobs)
            jobs.append(dict(src=W[ji], w=128, kind=kind_, dest=dest, row0=row0, const=const, neg=neg))
        if kind != 1:
            for n in range(24):
                J("f32", self.XS, n * 128)
            for n in range(8):
                J("bf16", self.XQ, n * 128, const=1.0 / 16.0)
        else:
            for n in range(4):
                J("f32", self.CQ, n * 128)
            for n in range(2):
                J("f32", self.CKV, n * 128)
            J("f32", self.KR, 0, neg=(64, 96))
            for n in range(8):
                J("bf16", self.XQ, n * 128, const=1.0 / 16.0)
        for n in range(32):
            J("gate", self.SG, n * 128)
        st = {"i": 0}

        def epi(ji, job, s, banks):
            i = st["i"] % 2
            st["i"] += 1
            n = len(s) * 512
            tok0 = s[0] * 512
            w = job["w"]
            for idx, t in enumerate(s):
                sl = slice(idx * 512, (idx + 1) * 512)
                if job["kind"] == "bf16":
                    o_stt(c, c.dve, ob[i][0:w, sl], banks[idx][0:w, :], job["const"], rstd[0:w, t * 512:(t + 1) * 512],
                          ALU.mult, ALU.mult, [banks[idx], rstdT[t]], [ob[i]])
                else:
                    o_stt(c, c.dve, of[i][0:w, sl], banks[idx][0:w, :], job["const"], rstd[0:w, t * 512:(t + 1) * 512],
                          ALU.mult, ALU.mult, [banks[idx], rstdT[t]], [of[i]])
                    if job["kind"] == "gate":
                        o_act(c, ob[i][0:w, sl], of[i][0:w, sl], AF.Silu, [of[i]], [ob[i]])
            r0 = job["row0"]
            if job["kind"] == "f32":
                c.dma(c.sp, job["dest"][r0:r0 + w, tok0:tok0 + n], of[i][0:w, 0:n], of[i], load=False)
            else:
                c.dma(c.sp, job["dest"][r0:r0 + w, tok0:tok0 + n], ob[i][0:w, 0:n], ob[i], load=False)

        xk = [(hgT[k], hg[:, k, :]) for k in range(16)]
        nset = 3 if NT % 3 == 0 else 2
        bank_sets = [ps[0:nset], ps[nset:2 * nset]]
        self.gemm(xk, 16, list(range(NT)), jobs, epi, wf, wb, bank_sets)
        c.end_stage()

    def stage_xattn(self, li):
        c, cfg = self.c, self.cfg
        T_, NT = cfg.T, cfg.NT
        ps = self.ps
        c.begin_stage()
        mem = c.sb("mem", [128, 16, 512], F32)
        memT_ = [c.view(mem[:, k, :]) for k in range(16)]
        msq = [c.sb("msq%d" % i, [128, 512], BF16) for i in range(2)]
        mg = c.sb("mg", [128, 16, 512], BF16)
        mgT = [c.view(mg[:, k, :]) for k in range(16)]
        rstdm = c.sb("rstdm", [128, 512], F32)
        kvT = c.sb("kvT", [128, 16, 512], BF16)
        kvTT = [c.view(kvT[:, k, :]) for k in range(16)]
        vtm = c.sb("vtm", [128, 4, 1024], BF16)
        identf = c.sb("identf", [128, 128], F32)
        ident = c.sb("ident", [128, 128], BF16)
        wf = [c.sb("wf%d" % i, [128, 16, 128], F32) for i in range(2)]
        wb = [c.sb("wb%d" % i, [128, 16, 128], BF16) for i in range(2)]
        xq = [c.sb("xq%d" % i, [128, 2, T_], BF16) for i in range(2)]
        sg = [c.sb("sg%d" % i, [128, 2, T_], BF16) for i in range(2)]
        yo = [c.sb("yo%d" % i, [128, 2, T_], BF16) for i in range(2)]
        pT = [c.sb("pT%d" % i, [128, 512], BF16) for i in range(4)]
        rec = [c.sb("rec%d" % i, [128, 512], F32) for i in range(2)]
        otmp = [c.sb("otmp%d" % i, [128, 512], F32) for i in range(2)]
        c.dma(c.sp, identf[:], self.ident_d[:, :], identf)
        o_copy(c, c.pool, ident[:], identf[:], [identf], [ident])
        for k in range(16):
            c.dma(c.sp, mem[:, k, :], self.memT[k * 128:(k + 1) * 128, :], memT_[k])
            o_act(c, msq[k % 2][:], mem[:, k, :], AF.Square, [memT_[k]], [msq[k % 2]])
            o_mm(c, ps[0][:, :], self.ones[:], msq[k % 2][:], k == 0, k == 15, [self.ones, msq[k % 2]], [ps[0]], inc=True)
        o_act(c, rstdm[:], ps[0][:, :], AF.Sqrt, [ps[0], self.cst], [rstdm], bias=self.cst[:, 0:1], scale=1.0 / D)
        c.op(c.dve, lambda h: h.reciprocal(out=rstdm[:], in_=rstdm[:]), [rstdm], [rstdm])
        for k in range(16):
            o_stt(c, c.dve, mg[:, k, :], mem[:, k, :], self.vcol("norm_mem", li * 16 + k), rstdm[:], ALU.mult, ALU.mult,
                  [memT_[k], rstdm, self.vec], [mgT[k]])
        W = self.Wmem[li]
        jobs = [dict(src=W[n], w=128, n=n) for n in range(16)]

        def epi(ji, job, s, banks):
            o_copy(c, c.act, kvT[:, job["n"], :], banks[0][:, :], [banks[0]], [kvTT[job["n"]]])
        self.gemm([(mgT[k], mg[:, k, :]) for k in range(16)], 16, [0], jobs, epi, wf, wb, [[ps[1]], [ps[2]]])
        for smb in range(4):
            for q4 in range(2):
                bank = ps[3 + (smb * 2 + q4) % 2]
                for vv in range(4):
                    vc = q4 * 4 + vv
                    o_mm(c, bank[:, vv * 128:(vv + 1) * 128], kvT[:, 8 + vc, smb * 128:(smb + 1) * 128], ident[:], True, True,
                         [kvTT[8 + vc], ident], [bank])
                o_copy(c, c.act, vtm[:, smb, q4 * 512:(q4 + 1) * 512], bank[:, :], [bank], [vtm])
        XQv = self.XQ.rearrange("(k p) t -> p k t", p=128)
        SGv = self.SG.rearrange("(k p) t -> p k t", p=128)
        Yv = self.Y.rearrange("(k p) t -> p k t", p=128)
        it = 0
        for hh in range(4):
            b = hh % 2
            c.dma(c.sp, xq[b][:], XQv[:, hh * 2:hh * 2 + 2, :], xq[b])
            c.dma(c.sp, sg[b][:], SGv[:, 24 + hh * 2:24 + hh * 2 + 2, :], sg[b])
            for si, (t0, L) in enumerate(cfg.segs):
                for qt in range(L // 512):
                    tk = slice(t0 + qt * 512, t0 + (qt + 1) * 512)
                    g3 = ps[2:5] if it % 2 == 0 else ps[5:8]
                    for mb in range(2):
                        sb_ = ps[(it * 2 + mb) % 2]
                        for dc in range(2):
                            o_mm(c, sb_[:, :], kvT[:, hh * 2 + dc, si * 256 + mb * 128: si * 256 + (mb + 1) * 128],
                                 xq[b][:, dc, tk], dc == 0, dc == 1, [kvTT[hh * 2 + dc], xq[b]], [sb_])
                        p_ = pT[(it * 2 + mb) % 4]
                        o_act(c, p_[:], sb_[:, :], AF.Exp, [sb_], [p_])
                    for dc in range(2):
                        for mb in range(2):
                            p_ = pT[(it * 2 + mb) % 4]
                            o_mm(c, g3[dc][:, :], vtm[:, si * 2 + mb, hh * 256 + dc * 128: hh * 256 + (dc + 1) * 128], p_[:],
                                 mb == 0, mb == 1, [vtm, p_], [g3[dc]])
                    for mb in range(2):
                        p_ = pT[(it * 2 + mb) % 4]
                        o_mm(c, g3[2][:, :], self.ones[:], p_[:], mb == 0, mb == 1, [self.ones, p_], [g3[2]])
                    r_ = rec[it % 2]
                    c.op(c.dve, lambda h, r_=r_, g3=g3: h.reciprocal(out=r_[:], in_=g3[2][:, :]), [g3[2]], [r_])
                    for dc in range(2):
                        ot = otmp[dc]
                        o_tt(c, c.dve, ot[:], g3[dc][:, :], r_[:], ALU.mult, [g3[dc], r_], [ot])
                        o_tt(c, c.pool, yo[b][:, dc, tk], ot[:], sg[b][:, dc, tk], ALU.mult, [ot, sg[b]], [yo[b]])
                    it += 1
            c.dma(c.sp, Yv[:, 24 + hh * 2:24 + hh * 2 + 2, :], yo[b][:], yo[b], load=False)
        c.end_stage()

    def stage_outproj(self, li, xsrc, xdst):
        c, cfg = self.c, self.cfg
        T_, NT = cfg.T, cfg.NT
        ps = self.ps
        NB = NT // 2
        c.begin_stage()
        yk = c.sb("yk", [128, 32, 1024], BF16)
        ykT = [c.view(yk[:, k * 8:(k + 1) * 8, :]) for k in range(4)]
        wf = [c.sb("wf%d" % i, [128, 32, 128], F32) for i in range(2)]
        wb = [c.sb("wb%d" % i, [128, 32, 128], BF16) for i in range(2)]
        zt = [c.sb("zt%d" % i, [128, 1024], F32) for i in range(2)]
        sq = [c.sb("sq%d" % i, [128, 1024], BF16) for i in range(2)]
        rstd2 = [c.sb("rstd%d" % i, [128, 1024], F32) for i in range(2)]
        zl = [c.sb("zl%d" % i, [128, 1024], F32) for i in range(3)]
        xl = [c.sb("xl%d" % i, [128, 1024], F32) for i in range(3)]
        xo = [c.sb("xo%d" % i, [128, 1024], F32) for i in range(3)]
        Yv = self.Y.rearrange("(k p) t -> p k t", p=128)
        Zt = [[c.view(self.Z[n * 128:(n + 1) * 128, blk * 1024:(blk + 1) * 1024]) for n in range(16)] for blk in range(NB)]
        W = self.Wout[li]
        jobs = [dict(src=W[n], w=128, n=n) for n in range(16)]

        def load_yk(blk):
            tok = slice(blk * 1024, (blk + 1) * 1024)
            for k in range(4):
                c.dma(c.sp, yk[:, k * 8:(k + 1) * 8, :], Yv[:, k * 8:(k + 1) * 8, tok], ykT[k])

        pst = []

        def flush_store():
            while pst:
                blk_, n_, i_ = pst.pop(0)
                tok_ = slice(blk_ * 1024, (blk_ + 1) * 1024)
                c.dma(c.sp, xdst[n_ * 128:(n_ + 1) * 128, tok_], xo[i_][:], xo[i_], load=False)
                if self.dbg is not None and li < len(self.dbg):
                    c.dma(c.sp, self.dbg[li][n_ * 128:(n_ + 1) * 128, tok_], xo[i_][:], xo[i_], load=False)

        def n2_iter(blk, n):
            tok = slice(blk * 1024, (blk + 1) * 1024)
            i = n % 3
            rstd = rstd2[blk % 2]
            flush_store()
            c.dma(c.sp, zl[i][:], self.Z[n * 128:(n + 1) * 128, tok], zl[i], reads=[Zt[blk][n]])
            c.dma(c.sp, xl[i][:], xsrc[n * 128:(n + 1) * 128, tok], xl[i])
            o_stt(c, c.dve, zl[i][:], zl[i][:], self.vcol("norm_post", li * 16 + n), rstd[:], ALU.mult, ALU.mult,
                  [zl[i], rstd, self.vec], [zl[i]])
            o_tt(c, c.pool, xo[i][:], zl[i][:], xl[i][:], ALU.add, [zl[i], xl[i]], [xo[i]])
            pst.append((blk, n, i))

        load_yk(0)
        for blk in range(NB):
            tok = slice(blk * 1024, (blk + 1) * 1024)
            rstd = rstd2[blk % 2]
            pend = []

            def ss_mm(n, i):
                for idx in range(2):
                    o_mm(c, ps[4 + idx][:, :], self.ones[:], sq[i][:, idx * 512:(idx + 1) * 512], n == 0, n == 15,
                         [self.ones, sq[i]], [ps[4 + idx]], inc=True)

            def epi(ji, job, s, banks, blk=blk, tok=tok):
                i = ji % 2
                n = job["n"]
                while pend:
                    ss_mm(*pend.pop(0))
                for idx in range(2):
                    sl = slice(idx * 512, (idx + 1) * 512)
                    o_copy(c, c.act, zt[i][:, sl], banks[idx][:, :], [banks[idx]], [zt[i]])
                    o_tt(c, c.dve, sq[i][:, sl], zt[i][:, sl], zt[i][:, sl], ALU.mult, [zt[i]], [sq[i]])
                c.dma(c.sp, self.Z[n * 128:(n + 1) * 128, tok], zt[i][:], zt[i], load=False, writes=[Zt[blk][n]])
                pend.append((n, i))
            hook = None
            if blk > 0:
                hook = (lambda ji, pb=blk - 1: n2_iter(pb, ji))
            xk = [(ykT[k // 8], yk[:, k, :]) for k in range(32)]
            self.gemm(xk, 32, [0, 1], jobs, epi, wf, wb, [ps[0:2], ps[2:4]], hook=hook)
            while pend:
                ss_mm(*pend.pop(0))
            for idx in range(2):
                sl = slice(idx * 512, (idx + 1) * 512)
                o_act(c, rstd[:, sl], ps[4 + idx][:, :], AF.Sqrt, [ps[4 + idx], self.cst], [rstd],
                      bias=self.cst[:, 0:1], scale=1.0 / D)
            c.op(c.dve, lambda h, rstd=rstd: h.reciprocal(out=rstd[:], in_=rstd[:]), [rstd], [rstd])
            if blk + 1 < NB:
                load_yk(blk + 1)
        for n in range(16):
            n2_iter(NB - 1, n)
        flush_store()
        c.end_stage()

    def halo_exchange(self, nL, nR, name):
        c, cfg = self.c, self.cfg
        ne = nL + nR
        hal = c.sb("hal_" + name, [128, 24, 2, ne], F32)
        XSv = self.XS.rearrange("(k p) t -> p k t", p=128)
        for si, (t0, L) in enumerate(cfg.segs):
            for k6 in range(4):
                ks = slice(k6 * 6, (k6 + 1) * 6)
                c.dma(c.sp, hal[:, ks, si, 0:nR], XSv[:, ks, t0:t0 + nR], hal, allow_slow_non_contiguous=True)
                c.dma(c.sp, hal[:, ks, si, nR:ne], XSv[:, ks, t0 + L - nL:t0 + L], hal, allow_slow_non_contiguous=True)
        g = self._exchange4(hal, 24 * 2 * ne, name)
        hl = c.sb("haloL_" + name, [128, 24, 2, nL], F32)
        hr = c.sb("haloR_" + name, [128, 24, 2, nR], F32)

        def gv(r, lo, hi):
            return g[:, r, :].rearrange("p (k s e) -> p k s e", k=24, s=2)[:, :, :, lo:hi]
        self.rank_select(hl[:], lambda r: gv(r, nR, ne), 0, hl, g)
        self.rank_select(hr[:], lambda r: gv(r, 0, nR), 4, hr, g)
        return hl, hr

    def _exchange4(self, tile4, W, name):
        c, nc = self.c, self.nc
        ein = nc.dram_tensor("exi_" + name, [128, W], F32)
        eout = nc.dram_tensor("exo_" + name, [4 * 128, W], F32)
        g = c.sb("exg_" + name, [128, 4, W], F32)
        nd = len(tile4.ap.shape)
        if nd == 4:
            flat = tile4[:].rearrange("p a b e -> p (a b e)")
        elif nd == 3:
            flat = tile4[:].rearrange("p a b -> p (a b)")
        else:
            flat = tile4[:]
        ev = c.dma(c.sp, ein.ap()[:, :], flat, tile4, load=False)
        key = self._cc(ein, eout, ev)
        sems = c.sems
        c.sp.prog.append(lambda h, key=key: h.wait_ge(sems[key], 1))
        c.dma(c.sp, g[:], eout.ap().rearrange("(r p) w -> p r w", p=128), g)
        return g

    def stage_rglru(self, aj):
        c, cfg = self.c, self.cfg
        T_, NT = cfg.T, cfg.NT
        ps = self.ps
        TE = T_ + 6
        c.begin_stage()
        kk = c.sb("kk", [128, 48], F32)
        xx = c.sb("xx", [128, 48], F32)
        o_act(c, xx[:], self.vcol("lam", aj * 48, 48), AF.Exp, [self.vec], [xx], scale=-1.0)
        o_ts(c, c.dve, kk[:], xx[:], -0.2, 0.25, ALU.mult, ALU.add, [xx], [kk])
        for cst_ in (1.0 / 3.0, 0.5, 1.0):
            o_tt(c, c.dve, kk[:], kk[:], xx[:], ALU.mult, [kk, xx], [kk])
            o_ts(c, c.dve, kk[:], kk[:], -1.0, cst_, ALU.mult, ALU.add, [kk], [kk])
        o_tt(c, c.dve, kk[:], kk[:], xx[:], ALU.mult, [kk, xx], [kk])
        o_ts(c, c.dve, kk[:], kk[:], -8.0, None, ALU.mult, None, [kk], [kk])
        hl_, hr_ = self.halo_exchange(2, 1, "a%d" % aj)
        contrib = c.sb("contrib", [128, 24, 2, 4], F32)
        xe = [c.sb("xe%d" % i, [128, TE], F32) for i in range(2)]
        u = [c.sb("u%d" % i, [128, T_], F32) for i in range(2)]
        ub = [c.sb("ub%d" % i, [128, T_], BF16) for i in range(2)]
        r2 = [c.sb("r_%d" % i, [128, T_], F32) for i in range(2)]
        i2 = [c.sb("i_%d" % i, [128, T_], F32) for i in range(2)]
        a_ = [c.sb("a_%d" % i, [128, T_], F32) for i in range(2)]
        b_ = [c.sb("b_%d" % i, [128, T_], F32) for i in range(2)]
        tmp2 = [c.sb("tmp%d" % i, [128, T_], F32) for i in range(2)]
        wgf = c.sb("wgf", [128, 4, 2, 256], F32)
        wgb = c.sb("wgb", [128, 4, 2, 256], BF16)
        rsum2 = [c.sb("rsum%d" % i, [128, 2], F32) for i in range(2)]
        nab = 0
        for h in range(12):
            for dg in range(4):
                d, g = dg // 2, dg % 2
                gj = ((aj * 2 + d) * 2 + g) * 12 + h
                c.dma(c.sp, wgf[:, dg, :, :], self.gatew[gj].rearrange("p (kc n) -> p kc n", kc=2), wgf)
            o_copy(c, c.pool, wgb[:], wgf[:], [wgf], [wgb])
            for oc in range(2):
                cc = 2 * h + oc
                for si, (t0, L) in enumerate(cfg.segs):
                    e0 = t0 + 3 * si
                    c.dma(c.sp, xe[oc][:, e0 + 2:e0 + 2 + L], self.XS[cc * 128:(cc + 1) * 128, t0:t0 + L], xe[oc])
                for si, (t0, L) in enumerate(cfg.segs):
                    e0 = t0 + 3 * si
                    o_copy(c, c.pool, xe[oc][:, e0:e0 + 2], hl_[:, cc, si, :], [hl_], [xe[oc]])
                    o_copy(c, c.pool, xe[oc][:, e0 + 2 + L:e0 + 3 + L], hr_[:, cc, si, :], [hr_], [xe[oc]])
                for si, (t0, L) in enumerate(cfg.segs):
                    e0 = t0 + 3 * si
                    o_ts(c, c.dve, u[oc][:, t0:t0 + L], xe[oc][:, e0:e0 + L], self.vcol("conv_w", (aj * 4 + 0) * 24 + cc),
                         self.vcol("conv_b", aj * 24 + cc), ALU.mult, ALU.add, [xe[oc], self.vec], [u[oc]])
                    for k in range(1, 4):
                        o_stt(c, c.dve, u[oc][:, t0:t0 + L], xe[oc][:, e0 + k:e0 + k + L],
                              self.vcol("conv_w", (aj * 4 + k) * 24 + cc), u[oc][:, t0:t0 + L], ALU.mult, ALU.add,
                              [xe[oc], self.vec, u[oc]], [u[oc]])
                o_copy(c, c.act, ub[oc][:], u[oc][:], [u[oc]], [ub[oc]])
            for oc in range(2):
                cc = 2 * h + oc
                for d in range(2):
                    r_, i_, tmp, rsum = r2[nab % 2], i2[nab % 2], tmp2[nab % 2], rsum2[nab % 2]
                    for t in range(NT):
                        tk = slice(t * 512, (t + 1) * 512)
                        pr, pi = ps[(t % 2) * 2], ps[(t % 2) * 2 + 1]
                        for g, pb in ((0, pr), (1, pi)):
                            for kc in range(2):
                                o_mm(c, pb[:, :], wgb[:, d * 2 + g, kc, oc * 128:(oc + 1) * 128], ub[kc][:, tk], kc == 0, kc == 1,
                                     [wgb, ub[kc]], [pb])
                        o_act(c, r_[:, tk], pr[:, :], AF.Sigmoid, [pr, self.vec], [r_],
                              bias=self.vcol("gate_b", ((aj * 2 + d) * 2 + 0) * 24 + cc))
                        o_act(c, i_[:, tk], pi[:, :], AF.Sigmoid, [pi, self.vec], [i_],
                              bias=self.vcol("gate_b", ((aj * 2 + d) * 2 + 1) * 24 + cc))
                    A, B = a_[nab % 2], b_[nab % 2]
                    nab += 1
                    kcol = kk[:, d * 24 + cc:d * 24 + cc + 1]
                    for si, (t0, L) in enumerate(cfg.segs):
                        sl = slice(t0, t0 + L)
                        c.op(c.dve, lambda h_, si=si, sl=sl, rsum=rsum, r_=r_: h_.tensor_reduce(out=rsum[:, si:si + 1], in_=r_[:, sl], axis=AX.X, op=ALU.add),
                             [r_], [rsum])
                    o_act(c, A[:], r_[:], AF.Exp, [r_, kk], [A], scale=kcol)
                    for si, (t0, L) in enumerate(cfg.segs):
                        o_act(c, contrib[:, cc, si, d * 2:d * 2 + 1], rsum[:, si:si + 1], AF.Exp, [rsum, kk], [contrib], scale=kcol)
                    o_tt(c, c.dve, tmp[:], A[:], A[:], ALU.mult, [A], [tmp])
                    o_act(c, tmp[:], tmp[:], AF.Sqrt, [tmp, self.cst], [tmp], bias=self.cst[:, 1:2], scale=-1.0)
                    o_tt(c, c.pool, i_[:], i_[:], u[oc][:], ALU.mult, [i_, u[oc]], [i_])
                    o_tt(c, c.dve, B[:], i_[:], tmp[:], ALU.mult, [i_, tmp], [B])
                    for si, (t0, L) in enumerate(cfg.segs):
                        sl = slice(t0, t0 + L)
                        if d == 0:
                            oa, da, db = r_[:, sl], A[:, sl], B[:, sl]
                            hend = r_[:, t0 + L - 1:t0 + L]
                        else:
                            oa, da, db = rev_ap(r_[:, sl]), rev_ap(A[:, sl]), rev_ap(B[:, sl])
                            hend = r_[:, t0:t0 + 1]
                        c.op(c.dve, lambda h_, oa=oa, da=da, db=db: h_.tensor_tensor_scan(out=oa, data0=da, data1=db, initial=0.0,
                                                                                     op0=ALU.mult, op1=ALU.add),
                             [A, B, r_], [r_])
                        o_copy(c, c.dve, contrib[:, cc, si, d * 2 + 1:d * 2 + 2], hend, [r_], [contrib])
                    c.dma(c.sp, self.SPL[d * 2][cc * 128:(cc + 1) * 128, :], A[:], A, load=False)
                    c.dma(c.sp, self.SPL[d * 2 + 1][cc * 128:(cc + 1) * 128, :], B[:], B, load=False)
        g2 = self._exchange4(contrib, 192, "c%d" % aj)

        def gP(r, d):
            return g2[:, r, :].rearrange("p (k e) -> p k e", e=4)[:, :, d * 2:d * 2 + 1]

        def gH(r, d):
            return g2[:, r, :].rearrange("p (k e) -> p k e", e=4)[:, :, d * 2 + 1:d * 2 + 2]
        S = [c.sb("S%d" % i, [128, 48, 1], F32) for i in range(3)]
        hin = self.hin[:].rearrange("p (k d) -> p k d", d=2)
        E = c.dve
        m = lambda j: self.masks[:, 8 + j:9 + j]
        o_copy(c, E, S[0][:], gH(0, 0), [g2], [S[0]])
        o_tt(c, E, S[1][:], gP(1, 0), S[0][:], ALU.mult, [g2, S[0]], [S[1]])
        o_tt(c, E, S[1][:], S[1][:], gH(1, 0), ALU.add, [g2, S[1]], [S[1]])
        o_tt(c, E, S[2][:], gP(2, 0), S[1][:], ALU.mult, [g2, S[1]], [S[2]])
        o_tt(c, E, S[2][:], S[2][:], gH(2, 0), ALU.add, [g2, S[2]], [S[2]])
        o_ts(c, E, hin[:, :, 0:1], S[0][:], m(1), None, ALU.mult, None, [S[0], self.masks], [self.hin])
        o_stt(c, E, hin[:, :, 0:1], S[1][:], m(2), hin[:, :, 0:1], ALU.mult, ALU.add, [S[1], self.masks, self.hin], [self.hin])
        o_stt(c, E, hin[:, :, 0:1], S[2][:], m(3), hin[:, :, 0:1], ALU.mult, ALU.add, [S[2], self.masks, self.hin], [self.hin])
        o_copy(c, E, S[0][:], gH(3, 1), [g2, self.hin], [S[0]])
        o_tt(c, E, S[1][:], gP(2, 1), S[0][:], ALU.mult, [g2, S[0]], [S[1]])
        o_tt(c, E, S[1][:], S[1][:], gH(2, 1), ALU.add, [g2, S[1]], [S[1]])
        o_tt(c, E, S[2][:], gP(1, 1), S[1][:], ALU.mult, [g2, S[1]], [S[2]])
        o_tt(c, E, S[2][:], S[2][:], gH(1, 1), ALU.add, [g2, S[2]], [S[2]])
        o_ts(c, E, hin[:, :, 1:2], S[0][:], m(2), None, ALU.mult, None, [S[0], self.masks], [self.hin])
        o_stt(c, E, hin[:, :, 1:2], S[1][:], m(1), hin[:, :, 1:2], ALU.mult, ALU.add, [S[1], self.masks, self.hin], [self.hin])
        o_stt(c, E, hin[:, :, 1:2], S[2][:], m(0), hin[:, :, 1:2], ALU.mult, ALU.add, [S[2], self.masks, self.hin], [self.hin])
        c.end_stage()
        c.begin_stage()
        ld = [[c.sb("ld%d_%d" % (q, i), [128, T_], F32) for q in range(4)] for i in range(2)]
        sg = [c.sb("sg%d" % i, [128, T_], BF16) for i in range(2)]
        hf = c.sb("hf", [128, T_], F32)
        hb = c.sb("hb", [128, T_], F32)
        yo = [c.sb("yo%d" % i, [128, T_], BF16) for i in range(2)]
        for cc in range(24):
            i = cc % 2
            rows = slice(cc * 128, (cc + 1) * 128)
            for q in range(4):
                c.dma(c.sp if q % 2 == 0 else c.act, ld[i][q][:], self.SPL[q][rows, :], ld[i][q])
            c.dma(c.sp, sg[i][:], self.SG[rows, :], sg[i])
            for si, (t0, L) in enumerate(cfg.segs):
                sl = slice(t0, t0 + L)
                ci = (cc * 2 + si) * 2
                f0 = slice(t0, t0 + 1)
                l0 = slice(t0 + L - 1, t0 + L)
                o_stt(c, c.dve, ld[i][1][:, f0], ld[i][0][:, f0], self.hin[:, ci:ci + 1], ld[i][1][:, f0], ALU.mult, ALU.add,
                      [ld[i][0], ld[i][1], self.hin], [ld[i][1]])
                o_stt(c, c.dve, ld[i][3][:, l0], ld[i][2][:, l0], self.hin[:, ci + 1:ci + 2], ld[i][3][:, l0], ALU.mult, ALU.add,
                      [ld[i][2], ld[i][3], self.hin], [ld[i][3]])
                c.op(c.dve, lambda h_, sl=sl, i=i: h_.tensor_tensor_scan(out=hf[:, sl], data0=ld[i][0][:, sl], data1=ld[i][1][:, sl],
                                                                    initial=0.0, op0=ALU.mult, op1=ALU.add),
                     [ld[i][0], ld[i][1]], [hf])
                c.op(c.dve, lambda h_, sl=sl, i=i: h_.tensor_tensor_scan(out=rev_ap(hb[:, sl]), data0=rev_ap(ld[i][2][:, sl]),
                                                                    data1=rev_ap(ld[i][3][:, sl]),
                                                                    initial=0.0, op0=ALU.mult, op1=ALU.add),
                     [ld[i][2], ld[i][3]], [hb])
            o_tt(c, c.pool, hf[:], hf[:], hb[:], ALU.add, [hf, hb], [hf])
            o_tt(c, c.pool, yo[i][:], hf[:], sg[i][:], ALU.mult, [hf, sg[i]], [yo[i]])
            c.dma(c.act, self.Y[rows, :], yo[i][:], yo[i], load=False)
        c.end_stage()

    def stage_pool(self):
        c, cfg = self.c, self.cfg
        T_, NT = cfg.T, cfg.NT
        ps = self.ps
        TE = T_ + 30
        c.begin_stage()
        hl_, hr_ = self.halo_exchange(8, 7, "p")
        xe = [c.sb("xe%d" % i, [128, TE], F32) for i in range(2)]
        sA = [c.sb("sA%d" % i, [128, TE], F32) for i in range(2)]
        sB = [c.sb("sB%d" % i, [128, TE], F32) for i in range(2)]
        invg = c.sb("invg", [128, T_], F32)
        pooled = c.sb("pooled", [128, 6, T_], BF16)
        pooledT = [c.view(pooled[:, k, :]) for k in range(6)]
        wf = [c.sb("wf%d" % i, [128, 6, 128], F32) for i in range(2)]
        wb = [c.sb("wb%d" % i, [128, 6, 128], BF16) for i in range(2)]
        sgt = [c.sb("sgt%d" % i, [128, 1536], BF16) for i in range(2)]
        of = [c.sb("of%d" % i, [128, 1536], F32) for i in range(2)]
        ob = [c.sb("ob%d" % i, [128, 1536], BF16) for i in range(2)]
        for gi in range(4):
            w = 2 << gi
            c.dma(c.sp, invg[:], self.invc_d[gi:gi + 1, :].partition_broadcast(128), invg)
            for kq in range(6):
                cc = gi * 6 + kq
                i = cc % 2
                E = c.dve if i == 0 else c.pool
                for si, (t0, L) in enumerate(cfg.segs):
                    e0 = t0 + 15 * si
                    c.dma(c.sp, xe[i][:, e0 + 8:e0 + 8 + L], self.XS[cc * 128:(cc + 1) * 128, t0:t0 + L], xe[i])
                for si, (t0, L) in enumerate(cfg.segs):
                    e0 = t0 + 15 * si
                    o_copy(c, E, xe[i][:, e0:e0 + 8], hl_[:, cc, si, :], [hl_], [xe[i]])
                    o_copy(c, E, xe[i][:, e0 + 8 + L:e0 + 15 + L], hr_[:, cc, si, :], [hr_], [xe[i]])
                for si, (t0, L) in enumerate(cfg.segs):
                    e0 = t0 + 15 * si
                    Le = L + 15
                    o_tt(c, E, sA[i][:, e0 + 1:e0 + Le], xe[i][:, e0:e0 + Le - 1], xe[i][:, e0 + 1:e0 + Le], ALU.add, [xe[i]], [sA[i]])
                    cur, oth = sA[i], sB[i]
                    lo, hi, hw = 1, Le, 1
                    while hw * 2 < w:
                        nlo, nhi = lo + hw, hi - hw
                        o_tt(c, E, oth[:, e0 + nlo:e0 + nhi], cur[:, e0 + nlo - hw:e0 + nhi - hw], cur[:, e0 + nlo + hw:e0 + nhi + hw],
                             ALU.add, [cur], [oth])
                        cur, oth = oth, cur
                        lo, hi, hw = nlo, nhi, hw * 2
                    o_tt(c, E, oth[:, e0 + 8:e0 + 8 + L], cur[:, e0 + 8:e0 + 8 + L], invg[:, t0:t0 + L], ALU.mult, [cur, invg], [oth])
                    o_tt(c, E, pooled[:, kq, t0:t0 + L], oth[:, e0 + 8:e0 + 8 + L], xe[i][:, e0 + 8:e0 + 8 + L], ALU.subtract,
                         [oth, xe[i]], [pooledT[kq]])
            jobs = [dict(src=self.wgrp[gi * 6 + m_], w=128, n=gi * 6 + m_) for m_ in range(6)]
            st = {"i": 0}

            def epi(ji, job, s, banks):
                i = st["i"] % 2
                st["i"] += 1
                n = len(s) * 512
                tok0 = s[0] * 512
                cc = job["n"]
                rows = slice(cc * 128, (cc + 1) * 128)
                c.dma(c.sp, sgt[i][:, 0:n], self.SG[rows, tok0:tok0 + n], sgt[i])
                for idx, t in enumerate(s):
                    sl = slice(idx * 512, (idx + 1) * 512)
                    o_ts(c, c.dve, of[i][:, sl], banks[idx][:, :], self.vcol("c_scale", cc), None, ALU.mult, None,
                         [banks[idx], self.vec], [of[i]])
                o_tt(c, c.pool, ob[i][:, 0:n], of[i][:, 0:n], sgt[i][:, 0:n], ALU.mult, [of[i], sgt[i]], [ob[i]])
                c.dma(c.sp, self.Y[rows, tok0:tok0 + n], ob[i][:, 0:n], ob[i], load=False)
            nset = 3 if NT % 3 == 0 else 2
            self.gemm([(pooledT[k], pooled[:, k, :]) for k in range(6)], 6, list(range(NT)), jobs, epi, wf, wb,
                      [ps[0:nset], ps[nset:2 * nset]])
        c.end_stage()

    def stage_mla(self):
        c, cfg, nc = self.c, self.cfg, self.nc
        T_, NT = cfg.T, cfg.NT
        ps = self.ps
        LATi = [t.ap() for t in self.LATi]
        LATo = [t.ap() for t in self.LATo]
        c.begin_stage()
        cq = c.sb("cq", [128, 6, T_], F32)
        cqT = [c.view(cq[:, k, :]) for k in range(6)]
        sq = [c.sb("sq%d" % i, [128, T_], BF16) for i in range(2)]
        rs = [c.sb("rs%d" % i, [128, T_], F32) for i in range(2)]
        nb = [c.sb("nb%d" % i, [128, T_], BF16) for i in range(2)]
        ka = c.sb("ka", [64, T_], F32)
        kb = c.sb("kb", [64, T_], F32)
        ck = c.sb("ck", [64, T_], F32)
        sk = c.sb("sk", [64, T_], F32)
        ko = c.sb("ko", [64, T_], BF16)
        for k in range(6):
            src = self.CQ[k * 128:(k + 1) * 128, :] if k < 4 else self.CKV[(k - 4) * 128:(k - 3) * 128, :]
            c.dma(c.sp, cq[:, k, :], src, cqT[k])
        for which, (k0, k1, feat) in enumerate(((0, 4, 512.0), (4, 6, 256.0))):
            for k in range(k0, k1):
                o_act(c, sq[k % 2][:], cq[:, k, :], AF.Square, [cqT[k]], [sq[k % 2]])
                for t in range(NT):
                    o_mm(c, ps[t][:, :], self.ones[:], sq[k % 2][:, t * 512:(t + 1) * 512], k == k0, k == k1 - 1,
                         [self.ones, sq[k % 2]], [ps[t]], inc=True)
            for t in range(NT):
                tk = slice(t * 512, (t + 1) * 512)
                o_act(c, rs[which][:, tk], ps[t][:, :], AF.Sqrt, [ps[t], self.cst], [rs[which]], bias=self.cst[:, 0:1], scale=1.0 / feat)
            c.op(c.dve, lambda h, which=which: h.reciprocal(out=rs[which][:], in_=rs[which][:]), [rs[which]], [rs[which]])
        for k in range(6):
            which = 0 if k < 4 else 1
            gcol = self.vcol("q_norm", k) if k < 4 else self.vcol("kv_norm", k - 4)
            o_stt(c, c.dve, nb[k % 2][:], cq[:, k, :], gcol, rs[which][:], ALU.mult, ALU.mult, [cqT[k], rs[which], self.vec], [nb[k % 2]])
            if k < 4:
                c.dma(c.sp, self.CQN[k * 128:(k + 1) * 128, :], nb[k % 2][:], nb[k % 2], load=False)
            else:
                c.dma(c.sp, LATi[k - 4][:, :], nb[k % 2][:], nb[k % 2], load=False)
        c.dma(c.sp, ka[:], self.KR[0:64, :], ka)
        c.dma(c.sp, kb[:], self.KR[64:128, :], kb)
        c.dma(c.sp, ck[:], self.rope_d[2], ck)
        c.dma(c.sp, sk[:], self.rope_d[3], sk)
        o_tt(c, c.dve, ka[:], ka[:], ck[:], ALU.mult, [ka, ck], [ka])
        o_tt(c, c.pool, kb[:], kb[:], sk[:], ALU.mult, [kb, sk], [kb])
        o_tt(c, c.dve, ko[:], ka[:], kb[:], ALU.add, [ka, kb], [ko])
        c.dma(c.sp, LATi[2][:, :], ko[:], ko, load=False)
        c.end_stage()
        sems = c.sems
        for gi_ in range(3):
            key = "cc%d" % self.ncc
            self.ncc += 1

            def emit_cc(h, key=key, gi_=gi_):
                h.collective_compute("AllGather", ALU.bypass, replica_groups=[[0, 1, 2, 3], [4, 5, 6, 7]],
                                     ins=[self.LATi[gi_].ap().opt()], outs=[self.LATo[gi_].ap().opt()]).then_inc(sems[key])
                h.wait_ge(sems[key], 1)
            c.pool.prog.append(emit_cc)
        c.op(c.pool, lambda h: h.memset(self.cst[:, 3:4], 0.0), (), [self.cst])
        c.drain_all()
        SCALE = 192.0 ** -0.5
        import os as _os3
        seg_order = list(enumerate(cfg.segs))
        if _os3.environ.get("KSWAP"):
            seg_order = seg_order[::-1]
        for si, (t0, L) in seg_order:
            Lk = 4 * L
            NQ = L // 512
            NKB = Lk // 128
            c.begin_stage()
            ckva = c.sb("ckva", [128, 2, Lk], BF16)
            kra = c.sb("kra", [64, Lk], BF16)
            cqn = c.sb("cqn", [128, 4, L], BF16)
            Cq = c.sb("Cq", [64, L], F32)
            Sq = c.sb("Sq", [64, L], F32)
            khT = c.sb("khT", [128, Lk], BF16)
            vh = c.sb("vh", [128, NKB, 128], BF16)
            qn = c.sb("qn", [128, L], BF16)
            qr = c.sb("qr", [64, L], BF16)
            wqf = c.sb("wqf", [128, 4, 192], F32)
            wqb = c.sb("wqb", [128, 4, 256], BF16)
            wkf = c.sb("wkf", [128, 2, 256], F32)
            wkb = c.sb("wkb", [128, 2, 256], BF16)
            pT = [c.sb("pT%d" % i, [128, 512], BF16) for i in range(8)]
            t1 = c.sb("t1", [64, 512], F32)
            t2 = c.sb("t2", [64, 512], F32)
            rec = c.sb("rec", [128, 512], F32)
            ot = c.sb("ot", [128, 512], F32)
            accs = [c.sb("acc%d" % i, [128, 512], F32) for i in range(4)]
            sgt = [c.sb("sgt%d" % i, [128, L], BF16) for i in range(2)]
            yh = [c.sb("yh%d" % i, [128, L], BF16) for i in range(2)]
            for r in range(4):
                for kc_ in range(2):
                    c.dma(c.sp, ckva[:, kc_, r * L:(r + 1) * L], LATo[kc_][r * 128:(r + 1) * 128, t0:t0 + L], ckva)
                c.dma(c.sp, kra[:, r * L:(r + 1) * L], LATo[2][r * 64:(r + 1) * 64, t0:t0 + L], kra)
            c.dma(c.sp, cqn[:], self.CQN[:, t0:t0 + L].rearrange("(k p) t -> p k t", p=128), cqn)
            c.dma(c.sp, Cq[:], self.rope_d[0][:, t0:t0 + L], Cq)
            c.dma(c.sp, Sq[:], self.rope_d[1][:, t0:t0 + L], Sq)
            for hd in range(24):
                hi = hd % 2
                c.dma(c.sp, wqf[:], self.wq[hd].rearrange("p (kc n) -> p kc n", kc=4), wqf)
                c.dma(c.sp, wkf[:], self.wkv[hd].rearrange("p (kc n) -> p kc n", kc=2), wkf)
                c.dma(c.sp, sgt[hi][:], self.SG[hd * 128:(hd + 1) * 128, t0:t0 + L], sgt[hi])
                o_copy(c, c.pool, wqb[:, :, 0:192], wqf[:, :, 0:192], [wqf], [wqb])
                o_ts(c, c.pool, wqb[:, :, 192:224], wqf[:, :, 160:192], -1.0, None, ALU.mult, None, [wqf], [wqb])
                o_copy(c, c.pool, wqb[:, :, 224:256], wqf[:, :, 128:160], [wqf], [wqb])
                o_copy(c, c.pool, wkb[:], wkf[:], [wkf], [wkb])
                for qt in range(NQ):
                    tk = slice(qt * 512, (qt + 1) * 512)
                    pn, pa, pb = ps[0], ps[1], ps[2]
                    for kc in range(4):
                        o_mm(c, pn[:, :], wqb[:, kc, 0:128], cqn[:, kc, tk], kc == 0, kc == 3, [wqb, cqn], [pn])
                    for kc in range(4):
                        o_mm(c, pa[0:64, :], wqb[:, kc, 128:192], cqn[:, kc, tk], kc == 0, kc == 3, [wqb, cqn], [pa])
                    for kc in range(4):
                        o_mm(c, pb[0:64, :], wqb[:, kc, 192:256], cqn[:, kc, tk], kc == 0, kc == 3, [wqb, cqn], [pb])
                    c.op(c.act, lambda h, tk=tk, pn=pn, qn=qn: h.mul(out=qn[:, tk], in_=pn[:, :], mul=SCALE), [pn], [qn])
                    o_tt(c, c.dve, t1[:], pa[0:64, :], Cq[:, tk], ALU.mult, [pa, Cq], [t1])
                    o_tt(c, c.dve, t2[:], pb[0:64, :], Sq[:, tk], ALU.mult, [pb, Sq], [t2])
                    o_tt(c, c.dve, qr[:, tk], t1[:], t2[:], ALU.add, [t1, t2], [qr])
                for kt in range(Lk // 512):
                    tk = slice(kt * 512, (kt + 1) * 512)
                    pk = ps[kt % 2]
                    for kc in range(2):
                        o_mm(c, pk[:, :], wkb[:, kc, 0:128], ckva[:, kc, tk], kc == 0, kc == 1, [wkb, ckva], [pk])
                    o_copy(c, c.dve if kt % 2 == 0 else c.act, khT[:, tk], pk[:, :], [pk], [khT])
                for k4 in range(NKB // 4):
                    pv = ps[2 + k4 % 2]
                    for j in range(4):
                        kbk = k4 * 4 + j
                        for kc in range(2):
                            o_mm(c, pv[:, j * 128:(j + 1) * 128], ckva[:, kc, kbk * 128:(kbk + 1) * 128], wkb[:, kc, 128:256],
                                 kc == 0, kc == 1, [wkb, ckva], [pv])
                    o_copy(c, c.dve if k4 % 2 == 0 else c.act, vh[:, k4 * 4:(k4 + 1) * 4, :],
                           pv[:, :].rearrange("p (j d) -> p j d", j=4), [pv], [vh])
                NQG = min(NQ, 4)
                nsets = 4 // NQG
                for qg in range(NQ // NQG):
                    qts = [qg * NQG + i_ for i_ in range(NQG)]
                    tks = [slice(qt * 512, (qt + 1) * 512) for qt in qts]
                    pos = [ps[4 + i_] for i_ in range(NQG)]

                    def pv_mm(kbk, pts):
                        for i_ in range(NQG):
                            o_mm(c, pos[i_][:, :], vh[:, kbk, :], pts[i_][:], kbk == 0, kbk == NKB - 1, [vh, pts[i_]], [pos[i_]])
                    prev = None
                    for kbk in range(NKB):
                        sset = [ps[(kbk % nsets) * NQG + i_] for i_ in range(NQG)]
                        ks = slice(kbk * 128, (kbk + 1) * 128)
                        for i_ in range(NQG):
                            c.op(c.pe, lambda h, sb_=sset[i_], ks=ks, tk=tks[i_], khT=khT, qn=qn:
                                 h.matmul(sb_[:, :], lhsT=khT[:, ks], rhs=qn[:, tk], start=True, stop=False),
                                 [khT, qn], [sset[i_]], inc=False)
                        for i_ in range(NQG):
                            o_mm(c, sset[i_][:, :], kra[:, ks], qr[:, tks[i_]], False, True, [kra, qr], [sset[i_]])
                        pts = []
                        for i_ in range(NQG):
                            p_ = pT[(kbk % 2) * NQG + i_]
                            o_act(c, p_[:], sset[i_][:, :], AF.Exp, [sset[i_]], [p_])
                            Ea = c.dve if i_ % 2 == 0 else c.pool
                            if kbk == 0:
                                o_copy(c, Ea, accs[i_][:], p_[:], [p_], [accs[i_]])
                            else:
                                o_tt(c, Ea, accs[i_][:], accs[i_][:], p_[:], ALU.add, [accs[i_], p_], [accs[i_]])
                            pts.append(p_)
                        if prev is not None:
                            pv_mm(*prev)
                        prev = (kbk, pts)
                    pv_mm(*prev)
                    for i_ in range(NQG):
                        pm = ps[i_]
                        o_mm(c, pm[:, :], self.ones_f[:], accs[i_][:], True, True, [self.ones_f, accs[i_]], [pm])
                        c.op(c.dve, lambda h, pm=pm, rec=rec: h.reciprocal(out=rec[:], in_=pm[:, :]), [pm], [rec])
                        o_tt(c, c.dve, ot[:], pos[i_][:, :], rec[:], ALU.mult, [pos[i_], rec], [ot])
                        o_tt(c, c.pool, yh[hi][:, tks[i_]], ot[:], sgt[hi][:, tks[i_]], ALU.mult, [ot, sgt[hi]], [yh[hi]])
                c.dma(c.sp, self.Y[hd * 128:(hd + 1) * 128, t0:t0 + L], yh[hi][:], yh[hi], load=False)
            c.end_stage()


def _rope_tables(pos):
    inv_freq = (1.0 / (10000.0 ** (np.arange(0, 64, 2, dtype=np.float32) / np.float32(64.0)))).astype(np.float32)
    ang = (pos.astype(np.float32)[:, None] * inv_freq[None, :]).astype(np.float32)
    cs = np.cos(ang).astype(np.float32).T
    sn = np.sin(ang).astype(np.float32).T
    C = np.concatenate([cs, cs], axis=0)
    S = np.concatenate([sn, sn], axis=0)
    s = np.float32(192.0 ** -0.5)
    return np.ascontiguousarray(np.stack([C * s, S * s, C, S]).astype(np.float32))


def _invc(pos, S):
    out = np.zeros((4, pos.size), np.float32)
    for g, w in enumerate((2, 4, 8, 16)):
        st = np.clip(pos - w // 2, 0, S)
        en = np.clip(pos + w - w // 2, 0, S)
        out[g] = 1.0 / (en - st).astype(np.float32)
    return out


_CACHE = {}


def run(inputs, LP, LS, nlayers=4, n_cores=8):
    cfg = Cfg(LP, LS, nlayers)
    f = lambda a: np.ascontiguousarray(np.asarray(a, dtype=np.float32))
    vp = VecPack()
    vp.add("norm_pre", inputs["norm_pre"])
    vp.add("norm_post", inputs["norm_post"])
    vp.add("norm_mem", inputs["norm_mem"])
    vp.add("conv_w", inputs["a_conv_w"])
    vp.add("conv_b", inputs["a_conv_b"])
    vp.add("gate_b", inputs["a_gate_b"])
    vp.add("lam", inputs["a_lambda"])
    vp.add("q_norm", inputs["b_q_norm"])
    vp.add("kv_norm", inputs["b_kv_norm"])
    vp.add("c_scale", inputs["c_scale"])
    vecs = vp.arr()
    prog = Prog(cfg, vp.off, vecs.shape[1])
    nc = prog.build()
    def jm(Wm, cols_list):
        Wm = np.asarray(Wm, np.float32)
        K = Wm.shape[0]
        out = []
        for cols in cols_list:
            blk = Wm[:, cols]
            w = blk.shape[1]
            out.append(blk.reshape(K // 128, 128, w).transpose(1, 0, 2).reshape(128, (K // 128) * w))
        return np.ascontiguousarray(np.stack(out))

    ar = np.arange
    cols_ac = [ar(j * 128, (j + 1) * 128) for j in range(64)]
    cols_b = ([ar(n * 128, (n + 1) * 128) for n in range(6)]
              + [np.concatenate([ar(768, 832), ar(800, 832), ar(768, 800)])]
              + [ar(832 + n * 128, 832 + (n + 1) * 128) for n in range(8)]
              + [ar(1856 + n * 128, 1856 + (n + 1) * 128) for n in range(32)])
    c16 = [ar(j * 128, (j + 1) * 128) for j in range(16)]
    gw = np.asarray(inputs["a_gate_w"], np.float32).reshape(96, 256, 256)
    shared = {
        "Win0": jm(inputs["a_w_in"][0], cols_ac), "Win1": jm(inputs["b_w_in"][0], cols_b),
        "Win2": jm(inputs["c_w_in"][0], cols_ac), "Win3": jm(inputs["a_w_in"][1], cols_ac),
        "gatew": np.ascontiguousarray(np.stack([jm(gw[i], [ar(256)])[0] for i in range(96)])),
        "wq": jm(inputs["b_w_q_up"][0], [ar(h * 192, (h + 1) * 192) for h in range(24)]),
        "wkv": jm(inputs["b_w_kv_up"][0], [ar(h * 256, (h + 1) * 256) for h in range(24)]),
        "wgrp": np.ascontiguousarray(np.concatenate([jm(np.asarray(inputs["c_w_group"])[0, g], [ar(m * 128, (m + 1) * 128) for m in range(6)])
                                                     for g in range(4)])),
        "vecs": vecs, "ident": np.eye(128, dtype=np.float32),
    }
    for i in range(4):
        shared["Wout%d" % i] = jm(inputs["w_out"][i], c16)
        shared["Wmem%d" % i] = jm(inputs["w_mem_kv"][i], c16)
    xp, xs = np.asarray(inputs["x_prompt"]), np.asarray(inputs["x_sample"])
    mp, ms = np.asarray(inputs["mem_prompt"]), np.asarray(inputs["mem_sample"])
    in_maps = []
    for core in range(n_cores):
        g, j = core // 4, core % 4
        m = dict(shared)
        m["xT"] = np.ascontiguousarray(np.concatenate([xp[g, j * LP:(j + 1) * LP].T, xs[g, j * LS:(j + 1) * LS].T], axis=1), np.float32)
        m["memT"] = np.ascontiguousarray(np.concatenate([mp[g].T, ms[g].T], axis=1), np.float32)
        pos = np.concatenate([np.arange(j * LP, (j + 1) * LP), np.arange(j * LS, (j + 1) * LS)])
        m["rope"] = _rope_tables(pos)
        m["invc"] = np.ascontiguousarray(np.concatenate([_invc(pos[:LP], 4 * LP), _invc(pos[LP:], 4 * LS)], axis=1))
        mk = np.zeros((128, 12), np.float32)
        if j > 0:
            mk[:, j - 1] = 1.0
        if j < 3:
            mk[:, 4 + j + 1] = 1.0
        mk[:, 8 + j] = 1.0
        m["masks"] = mk
        in_maps.append(m)
    res = run_bass_kernel_spmd(nc, in_maps, core_ids=list(range(n_cores)))
    global LAST_RES
    LAST_RES = res
    yp = np.zeros((2, 4 * LP, D), np.float32)
    ys = np.zeros((2, 4 * LS, D), np.float32)
    for core in range(n_cores):
        g, j = core // 4, core % 4
        yT = np.asarray(res.results[core]["yT"])
        yp[g, j * LP:(j + 1) * LP] = yT[:, :LP].T
        ys[g, j * LS:(j + 1) * LS] = yT[:, LP:].T
    return yp, ys


def kernel(**inputs):
    return run(inputs, 1024, 2048, 4, 8)
```

```python
from contextlib import ExitStack
import numpy as np
import concourse.bass as bass
import concourse.mybir as mybir
from concourse.ap import AP
from concourse.bass_utils import run_bass_kernel_spmd

F32 = mybir.dt.float32
BF16 = mybir.dt.bfloat16
ALU = mybir.AluOpType
AF = mybir.ActivationFunctionType
AX = mybir.AxisListType

SAME_SYNC = True
D = 2048
MIXW = 3072
XAW = 1024
BR = 4096
EPS = 1e-6


class Eng:
    def __init__(self, name, sem, is_pe=False):
        self.name = name
        self.sem = sem
        self.cnt = 0
        self.waited = {}
        self.is_pe = is_pe
        self.prog = []


class T:
    def __init__(self, ap, name=None):
        self.ap = ap
        self.name = name
        self.w = None
        self.r = {}
        self.dsem = None

    def __getitem__(self, k):
        return self.ap[k]


class Ctx:
    def __init__(self, nc, es):
        self.nc = nc
        self.es = es
        self.sems = {}
        self.engs = {}
        for nm, pe in (("pe", True), ("act", False), ("dve", False), ("pool", False), ("sp", False)):
            self.sems["s_" + nm] = es.enter_context(nc.semaphore("s_" + nm))
            self.engs[nm] = Eng(nm, "s_" + nm, pe)
        self.pe, self.act, self.dve = self.engs["pe"], self.engs["act"], self.engs["dve"]
        self.pool, self.sp = self.engs["pool"], self.engs["sp"]
        self.all_t = []
        self.n_ins = 0
        self._dcnt = {}
        self.dfree = []
        self.stage_ts = None
        self.stage_es = None

    def _reg(self, tt):
        self.all_t.append(tt)
        if self.stage_ts is not None:
            self.stage_ts.append(tt)
        return tt

    def sb(self, name, shape, dt, persist=False):
        es = self.es if (persist or self.stage_es is None) else self.stage_es
        self.uid = getattr(self, "uid", 0) + 1
        name = "sb%d_%s" % (self.uid, name)
        t = es.enter_context(self.nc.sbuf_tensor(name, list(shape), dt))
        tt = T(t, name)
        if persist or self.stage_ts is None:
            self.all_t.append(tt)
        else:
            self._reg(tt)
        return tt

    def ps(self, name, shape, dt=F32):
        t = self.es.enter_context(self.nc.psum_tensor(name, list(shape), dt))
        tt = T(t, name)
        self.all_t.append(tt)
        return tt

    def view(self, ap, name=None):
        return self._reg(T(ap, name))

    def begin_stage(self):
        self.stage_ts = []
        self.stage_es = ExitStack()
        self.stage_es.__enter__()

    def end_stage(self):
        self.drain_all()
        for t in self.stage_ts:
            if t.dsem is not None and t.dsem not in self.dfree:
                self.dfree.append(t.dsem)
        gone = set(id(t) for t in self.stage_ts)
        self.all_t = [t for t in self.all_t if id(t) not in gone]
        self.stage_ts = None
        self.stage_es.__exit__(None, None, None)
        self.stage_es = None

    def _dsem(self, t):
        if t.dsem is None:
            if self.dfree:
                t.dsem = self.dfree.pop()
            else:
                key = "d_%d" % len(self.sems)
                self.sems[key] = self.es.enter_context(self.nc.semaphore(key))
                t.dsem = key
        return t.dsem

    def share_dsem(self, ts):
        k = self._dsem(ts[0])
        for t in ts[1:]:
            t.dsem = k

    def _collect(self, E, reads, writes):
        need = {}

        def add(k, v):
            if need.get(k, 0) < v:
                need[k] = v
        for t in reads:
            if t.w is not None:
                add(*t.w)
        for t in writes:
            if t.w is not None:
                add(*t.w)
            for k, v in t.r.items():
                add(k, v)
        out = []
        for k, v in need.items():
            if k == E.sem and (E.is_pe or not SAME_SYNC):
                continue
            if E.waited.get(k, 0) >= v:
                continue
            E.waited[k] = v
            out.append((k, v))
        return out

    def _record(self, ev, reads, writes):
        for t in reads:
            if t.r.get(ev[0], 0) < ev[1]:
                t.r[ev[0]] = ev[1]
        for t in writes:
            t.w = ev
            t.r = {}

    def op(self, E, fn, reads=(), writes=(), inc=True):
        waits = self._collect(E, reads, writes)
        sems = self.sems
        if inc:
            E.cnt += 1
            ev = (E.sem, E.cnt)
        else:
            ev = (E.sem, E.cnt + 1)
        esem = E.sem

        def emit(h):
            for k, v in waits[1:]:
                h.wait_ge(sems[k], v)
            ins = fn(h)
            if waits:
                ins._wait_ge(sems[waits[0][0]], waits[0][1])
            if inc:
                ins.then_inc(sems[esem], 1)
        E.prog.append(emit)
        self.n_ins += 1 + len(waits)
        self._record(ev, reads, writes)
        return ev

    def dma(self, Q, out, in_, sbt, load=True, reads=(), writes=(), **kw):
        reads = list(reads)
        writes = list(writes)
        if load:
            writes.append(sbt)
        else:
            reads.append(sbt)
        waits = self._collect(Q, reads, writes)
        key = self._dsem(sbt)
        tot = self._dcnt.get(key, 0) + 16
        self._dcnt[key] = tot
        ev = (key, tot)
        sems = self.sems

        def emit(h):
            for k, v in waits[1:]:
                h.wait_ge(sems[k], v)
            ins = h.dma_start(out=out, in_=in_, **kw)
            if waits:
                ins._wait_ge(sems[waits[0][0]], waits[0][1])
            ins.then_inc(sems[key], 16)
        Q.prog.append(emit)
        self.n_ins += 1 + len(waits)
        self._record(ev, reads, writes)
        return ev

    def drain_all(self):
        targets = [(e.sem, e.cnt) for e in self.engs.values() if e.cnt > 0]
        targets += list(self._dcnt.items())
        sems = self.sems
        for E in self.engs.values():
            for k, v in targets:
                if k == E.sem or E.waited.get(k, 0) >= v:
                    continue
                E.waited[k] = v
                E.prog.append(lambda h, k=k, v=v: h.wait_ge(sems[k], v))
                self.n_ins += 1
        for t in self.all_t:
            t.w = None
            t.r = {}

    def emit(self):
        with self.nc.Block() as block:
            @block.tensor
            def _(h):
                for f in self.pe.prog:
                    f(h)

            @block.scalar
            def _(h):
                for f in self.act.prog:
                    f(h)

            @block.vector
            def _(h):
                for f in self.dve.prog:
                    f(h)

            @block.gpsimd
            def _(h):
                for f in self.pool.prog:
                    f(h)

            @block.sync
            def _(h):
                for f in self.sp.prog:
                    f(h)


def o_tt(c, E, out, a, b, op, R, W):
    return c.op(E, lambda h: h.tensor_tensor(out=out, in0=a, in1=b, op=op), R, W)


def o_ts(c, E, out, a, s1, s2, op0, op1, R, W):
    if s2 is None:
        return c.op(E, lambda h: h.tensor_scalar(out=out, in0=a, scalar1=s1, scalar2=None, op0=op0), R, W)
    return c.op(E, lambda h: h.tensor_scalar(out=out, in0=a, scalar1=s1, scalar2=s2, op0=op0, op1=op1), R, W)


def o_stt(c, E, out, a, s, b, op0, op1, R, W):
    return c.op(E, lambda h: h.scalar_tensor_tensor(out=out, in0=a, scalar=s, in1=b, op0=op0, op1=op1), R, W)


def o_act(c, out, in_, func, R, W, bias=None, scale=None):
    kw = {}
    if bias is not None:
        kw["bias"] = bias
    if scale is not None:
        kw["scale"] = scale
    return c.op(c.act, lambda h: h.activation(out=out, in_=in_, func=func, **kw), R, W)


def o_copy(c, E, out, in_, R, W):
    if E is c.act:
        return c.op(E, lambda h: h.copy(out=out, in_=in_), R, W)
    return c.op(E, lambda h: h.tensor_copy(out=out, in_=in_), R, W)


def o_mm(c, out, lhsT, rhs, start, stop, R, W, inc=None):
    return c.op(c.pe, lambda h: h.matmul(out, lhsT=lhsT, rhs=rhs, start=start, stop=stop), R, W,
                inc=(stop if inc is None else inc))


def rev_ap(ap2d):
    n = ap2d.shape[-1]
    last = ap2d[:, n - 1:n]
    return AP(ap2d.tensor, last.offset, [list(last.ap[0]), [-1, n]])


class Cfg:
    def __init__(self, LP, LS, nlayers=4):
        self.LP, self.LS = LP, LS
        self.T = LP + LS
        self.NT = self.T // 512
        self.segs = [(0, LP), (LP, LS)]
        self.nlayers = nlayers


class VecPack:
    def __init__(self):
        self.cols = []
        self.off = {}
        self.n = 0

    def add(self, name, v):
        v = np.asarray(v, np.float32).reshape(-1)
        assert v.size % 128 == 0 or v.size < 128
        if v.size < 128:
            a = np.zeros((128, 1), np.float32)
            a[:v.size, 0] = v
        else:
            a = np.ascontiguousarray(v.reshape(-1, 128).T)
        self.off[name] = (self.n, a.shape[1])
        self.cols.append(a)
        self.n += a.shape[1]

    def arr(self):
        return np.ascontiguousarray(np.concatenate(self.cols, axis=1))


class Prog:
    def __init__(self, cfg, voff, nv):
        self.cfg = cfg
        self.voff = voff
        T_ = cfg.T
        import os as _os
        nc = bass.Bass("TRN2", target_bir_lowering=False)
        self.nc = nc
        dt = nc.dram_tensor
        I = dict(kind="ExternalInput")
        self.xT = dt("xT", [D, T_], F32, **I).ap()
        self.memT = dt("memT", [D, 512], F32, **I).ap()
        self.Win = [dt("Win%d" % i, [(64, 47, 64, 64)[i], 128, 16 * 128], F32, **I).ap() for i in range(4)]
        self.Wout = [dt("Wout%d" % i, [16, 128, 32 * 128], F32, **I).ap() for i in range(4)]
        self.Wmem = [dt("Wmem%d" % i, [16, 128, 16 * 128], F32, **I).ap() for i in range(4)]
        self.gatew = dt("gatew", [96, 128, 2 * 256], F32, **I).ap()
        self.wq = dt("wq", [24, 128, 4 * 192], F32, **I).ap()
        self.wkv = dt("wkv", [24, 128, 2 * 256], F32, **I).ap()
        self.wgrp = dt("wgrp", [24, 128, 6 * 128], F32, **I).ap()
        self.vecs_d = dt("vecs", [128, nv], F32, **I).ap()
        self.rope_d = dt("rope", [4, 64, T_], F32, **I).ap()
        self.invc_d = dt("invc", [4, T_], F32, **I).ap()
        self.masks_d = dt("masks", [128, 12], F32, **I).ap()
        self.ident_d = dt("ident", [128, 128], F32, **I).ap()
        self.CQN = dt("CQN", [512, T_], BF16, **(dict(kind="ExternalOutput") if _os.environ.get("KDBG") == "2" else {})).ap()
        self.yT = dt("yT", [D, T_], F32, kind="ExternalOutput").ap()
        self.X = dt("X", [D, T_], F32).ap()
        self.Z = dt("Z", [D, T_], F32).ap()
        self.XS = dt("XS", [MIXW, T_], F32).ap()
        self.XQ = dt("XQ", [XAW, T_], BF16).ap()
        self.SG = dt("SG", [BR, T_], BF16).ap()
        import os as _os2
        dbgk = dict(kind="ExternalOutput") if _os2.environ.get("KDBG") == "2" else {}
        self.Y = dt("Y", [BR, T_], BF16, **dbgk).ap()
        self.SPL = [dt("SPL%d" % i, [MIXW, T_], F32).ap() for i in range(4)]
        self.CQ = dt("CQ", [512, T_], F32).ap()
        self.CKV = dt("CKV", [256, T_], F32).ap()
        self.KR = dt("KR", [128, T_], F32).ap()
        self.LATi = [dt("LATi%d" % i, [(128, 128, 64)[i], T_], BF16) for i in range(3)]
        self.LATo = [dt("LATo%d" % i, [4 * (128, 128, 64)[i], T_], BF16) for i in range(3)]
        self.ncc = 0
        self.dbg = None
        if _os.environ.get("KDBG") == "1":
            self.dbg = [dt("XL%d" % i, [D, T_], F32, kind="ExternalOutput").ap() for i in range(cfg.nlayers)]

    def vcol(self, name, j=0, n=1):
        o, w = self.voff[name]
        return self.vec[:, o + j:o + j + n]

    def build(self):
        nc, cfg = self.nc, self.cfg
        with ExitStack() as es:
            c = Ctx(nc, es)
            self.c = c
            for i in range(10):
                c.sems["cc%d" % i] = es.enter_context(nc.semaphore("cc%d" % i))
            self.vec = c.sb("vec", [128, self.vecs_d.shape[1]], F32, persist=True)
            self.masks = c.sb("masks", [128, 12], F32, persist=True)
            self.ones = c.sb("ones", [128, 128], BF16, persist=True)
            self.cst = c.sb("cst", [128, 4], F32, persist=True)
            self.ones_f = c.sb("ones_f", [128, 128], F32, persist=True)
            self.hin = c.sb("hin", [128, 24 * 2 * 2], F32, persist=True)
            self.ps = [c.ps("ps%d" % i, [128, 512], F32) for i in range(8)]
            c.dma(c.sp, self.vec[:], self.vecs_d[:, :], self.vec)
            c.dma(c.sp, self.masks[:], self.masks_d[:, :], self.masks)
            c.op(c.pool, lambda h: h.memset(self.ones[:], 1.0), (), [self.ones])
            c.op(c.pool, lambda h: h.memset(self.ones_f[:], 1.0), (), [self.ones_f])
            c.op(c.pool, lambda h: h.memset(self.cst[:, 0:1], EPS), (), [self.cst])
            c.op(c.pool, lambda h: h.memset(self.cst[:, 1:2], 1.0), (), [self.cst])
            c.op(c.pool, lambda h: h.memset(self.cst[:, 2:3], 0.0), (), [self.cst])
            c.drain_all()
            import os as _os
            lim = int(_os.environ.get("KSTAGES", "1000"))
            ns = 0
            for li in range(cfg.nlayers):
                kind = li % 3
                xsrc = self.xT if li == 0 else self.X
                xdst = self.yT if li == cfg.nlayers - 1 else self.X
                steps = [lambda: self.stage_inproj(li, kind, xsrc), lambda: self.stage_xattn(li),
                         (lambda: self.stage_rglru(li // 3)) if kind == 0 else
                         ((lambda: self.stage_mla()) if kind == 1 else (lambda: self.stage_pool())),
                         lambda: self.stage_outproj(li, xsrc, xdst)]
                for stp in steps:
                    if ns < lim:
                        stp()
                    ns += 1
            c.emit()
        return nc

    def exchange(self, contrib, W, name):
        c, nc = self.c, self.nc
        ein = nc.dram_tensor("exi_" + name, [128, W], F32)
        eout = nc.dram_tensor("exo_" + name, [4 * 128, W], F32)
        g = c.sb("exg_" + name, [128, 4, W], F32)
        ev = c.dma(c.pool, ein.ap()[:, :], contrib[:, 0:W], contrib, load=False)
        self._cc(ein, eout, ev)
        c.dma(c.pool, g[:], eout.ap().rearrange("(r p) w -> p r w", p=128), g)
        return g

    def _cc(self, ein, eout, ev):
        c, nc = self.c, self.nc
        key = "cc%d" % self.ncc
        self.ncc += 1
        sems = c.sems
        k0, v0 = ev

        def emit(h):
            h.wait_ge(sems[k0], v0)
            h.collective_compute("AllGather", ALU.bypass, replica_groups=[[0, 1, 2, 3], [4, 5, 6, 7]],
                                 ins=[ein.ap().opt()], outs=[eout.ap().opt()]).then_inc(sems[key])
            h.wait_ge(sems[key], 1)
        c.pool.prog.append(emit)
        c.pool.waited[k0] = max(c.pool.waited.get(k0, 0), v0)
        return key

    def rank_select(self, out_ap, g_ap_fn, mcol0, outT, gT, first_E=None):
        c = self.c
        E = c.dve
        for r in range(4):
            m = self.masks[:, mcol0 + r:mcol0 + r + 1]
            if r == 0:
                o_ts(c, E, out_ap, g_ap_fn(r), m, None, ALU.mult, None, [gT, self.masks], [outT])
            else:
                o_stt(c, E, out_ap, g_ap_fn(r), m, out_ap, ALU.mult, ALU.add, [gT, self.masks, outT], [outT])

    def gemm(self, xk, KC, tiles, jobs, epi, wf, wb, bank_sets, tile_w=512, hook=None):
        c = self.c
        nset = len(bank_sets[0])
        sets = [tiles[i:i + nset] for i in range(0, len(tiles), nset)]
        state = {"si": 0}

        def load(ji):
            job = jobs[ji]
            b = ji % 2
            w = job["w"]
            c.dma(c.sp, wf[b][:, 0:KC, 0:w], job["src"].rearrange("p (kc n) -> p kc n", kc=KC), wf[b])
            neg = job.get("neg")
            if neg is None:
                o_copy(c, c.pool, wb[b][:, 0:KC, 0:w], wf[b][:, 0:KC, 0:w], [wf[b]], [wb[b]])
            else:
                lo, hi = neg
                o_copy(c, c.pool, wb[b][:, 0:KC, 0:lo], wf[b][:, 0:KC, 0:lo], [wf[b]], [wb[b]])
                o_ts(c, c.pool, wb[b][:, 0:KC, lo:hi], wf[b][:, 0:KC, lo:hi], -1.0, None, ALU.mult, None, [wf[b]], [wb[b]])
                o_copy(c, c.pool, wb[b][:, 0:KC, hi:w], wf[b][:, 0:KC, hi:w], [wf[b]], [wb[b]])
            if hook is not None:
                hook(ji)

        load(0)
        for ji, job in enumerate(jobs):
            if ji + 1 < len(jobs):
                load(ji + 1)
            b = ji % 2
            w = job["w"]
            for s in sets:
                banks = bank_sets[state["si"] % len(bank_sets)]
                state["si"] += 1
                for k in range(KC):
                    xt_, xap = xk[k]
                    kp = xap.shape[0]
                    for idx, t in enumerate(s):
                        o_mm(c, banks[idx][0:w, 0:tile_w], wb[b][0:kp, k, 0:w], xap[:, t * tile_w:(t + 1) * tile_w],
                             k == 0, k == KC - 1, [wb[b], xt_], [banks[idx]])
                epi(ji, job, s, banks)

    def rms_stats(self, tiles_fn, nchunks, ntile, tile_w, rstd, rstdT, sqbufs, feat):
        raise NotImplementedError

    def stage_inproj(self, li, kind, xsrc):
        c, cfg = self.c, self.cfg
        T_, NT = cfg.T, cfg.NT
        H = T_ // 2
        c.begin_stage()
        hg = c.sb("hg", [128, 16, T_], BF16)
        hgT = [c.view(hg[:, k, :], "hg%d" % k) for k in range(16)]
        rstd = c.sb("rstd", [128, T_], F32)
        rstdT = [c.view(rstd[:, t * 512:(t + 1) * 512]) for t in range(NT)]
        xt = [c.sb("xt%d" % i, [128, H], F32) for i in range(2)]
        sq = [c.sb("sq%d" % i, [128, H], BF16) for i in range(2)]
        wf = [c.sb("wf%d" % i, [128, 16, 128], F32) for i in range(2)]
        wb = [c.sb("wb%d" % i, [128, 16, 128], BF16) for i in range(2)]
        of = [c.sb("of%d" % i, [128, 1536], F32) for i in range(2)]
        ob = [c.sb("ob%d" % i, [128, 1536], BF16) for i in range(2)]
        ps = self.ps
        u = 0
        for ch in range(16):
            for hf in range(2):
                b = u % 2
                u += 1
                c.dma(c.sp, xt[b][:], xsrc[ch * 128:(ch + 1) * 128, hf * H:(hf + 1) * H], xt[b])
                o_ts(c, c.pool, hg[:, ch, hf * H:(hf + 1) * H], xt[b][:], self.vcol("norm_pre", li * 16 + ch), None,
                     ALU.mult, None, [xt[b], self.vec], [hgT[ch]])
                o_act(c, sq[b][:], xt[b][:], AF.Square, [xt[b]], [sq[b]])
                for tt in range(NT // 2):
                    t = hf * (NT // 2) + tt
                    o_mm(c, ps[t][:, :], self.ones[:], sq[b][:, tt * 512:(tt + 1) * 512], ch == 0, ch == 15,
                         [self.ones, sq[b]], [ps[t]], inc=True)
        for t in range(NT):
            o_act(c, rstd[:, t * 512:(t + 1) * 512], ps[t][:, :], AF.Sqrt, [ps[t], self.cst], [rstdT[t]],
                  bias=self.cst[:, 0:1], scale=1.0 / D)
            c.op(c.dve, lambda h, t=t: h.reciprocal(out=rstd[:, t * 512:(t + 1) * 512], in_=rstd[:, t * 512:(t + 1) * 512]),
                 [rstdT[t]], [rstdT[t]])
        W = self.Win[li]
        jobs = []

        def J(kind_, dest, row0, const=1.0, neg=None):
            ji = len(jobs)
            jobs.append(dict(src=W[ji], w=128, kind=kind_, dest=dest, row0=row0, const=const, neg=neg))
        if kind != 1:
            for n in range(24):
                J("f32", self.XS, n * 128)
            for n in range(8):
                J("bf16", self.XQ, n * 128, const=1.0 / 16.0)
        else:
            for n in range(4):
                J("f32", self.CQ, n * 128)
            for n in range(2):
                J("f32", self.CKV, n * 128)
            J("f32", self.KR, 0, neg=(64, 96))
            for n in range(8):
                J("bf16", self.XQ, n * 128, const=1.0 / 16.0)
        for n in range(32):
            J("gate", self.SG, n * 128)
        st = {"i": 0}

        def epi(ji, job, s, banks):
            i = st["i"] % 2
            st["i"] += 1
            n = len(s) * 512
            tok0 = s[0] * 512
            w = job["w"]
            for idx, t in enumerate(s):
                sl = slice(idx * 512, (idx + 1) * 512)
                if job["kind"] == "bf16":
                    o_stt(c, c.dve, ob[i][0:w, sl], banks[idx][0:w, :], job["const"], rstd[0:w, t * 512:(t + 1) * 512],
                          ALU.mult, ALU.mult, [banks[idx], rstdT[t]], [ob[i]])
                else:
                    o_stt(c, c.dve, of[i][0:w, sl], banks[idx][0:w, :], job["const"], rstd[0:w, t * 512:(t + 1) * 512],
                          ALU.mult, ALU.mult, [banks[idx], rstdT[t]], [of[i]])
                    if job["kind"] == "gate":
                        o_act(c, ob[i][0:w, sl], of[i][0:w, sl], AF.Silu, [of[i]], [ob[i]])
            r0 = job["row0"]
            if job["kind"] == "f32":
                c.dma(c.sp, job["dest"][r0:r0 + w, tok0:tok0 + n], of[i][0:w, 0:n], of[i], load=False)
            else:
                c.dma(c.sp, job["dest"][r0:r0 + w, tok0:tok0 + n], ob[i][0:w, 0:n], ob[i], load=False)

        xk = [(hgT[k], hg[:, k, :]) for k in range(16)]
        nset = 3 if NT % 3 == 0 else 2
        bank_sets = [ps[0:nset], ps[nset:2 * nset]]
        self.gemm(xk, 16, list(range(NT)), jobs, epi, wf, wb, bank_sets)
        c.end_stage()

    def stage_xattn(self, li):
        c, cfg = self.c, self.cfg
        T_, NT = cfg.T, cfg.NT
        ps = self.ps
        c.begin_stage()
        mem = c.sb("mem", [128, 16, 512], F32)
        memT_ = [c.view(mem[:, k, :]) for k in range(16)]
        msq = [c.sb("msq%d" % i, [128, 512], BF16) for i in range(2)]
        mg = c.sb("mg", [128, 16, 512], BF16)
        mgT = [c.view(mg[:, k, :]) for k in range(16)]
        rstdm = c.sb("rstdm", [128, 512], F32)
        kvT = c.sb("kvT", [128, 16, 512], BF16)
        kvTT = [c.view(kvT[:, k, :]) for k in range(16)]
        vtm = c.sb("vtm", [128, 4, 1024], BF16)
        identf = c.sb("identf", [128, 128], F32)
        ident = c.sb("ident", [128, 128], BF16)
        wf = [c.sb("wf%d" % i, [128, 16, 128], F32) for i in range(2)]
        wb = [c.sb("wb%d" % i, [128, 16, 128], BF16) for i in range(2)]
        xq = [c.sb("xq%d" % i, [128, 2, T_], BF16) for i in range(2)]
        sg = [c.sb("sg%d" % i, [128, 2, T_], BF16) for i in range(2)]
        yo = [c.sb("yo%d" % i, [128, 2, T_], BF16) for i in range(2)]
        pT = [c.sb("pT%d" % i, [128, 512], BF16) for i in range(4)]
        rec = [c.sb("rec%d" % i, [128, 512], F32) for i in range(2)]
        otmp = [c.sb("otmp%d" % i, [128, 512], F32) for i in range(2)]
        c.dma(c.sp, identf[:], self.ident_d[:, :], identf)
        o_copy(c, c.pool, ident[:], identf[:], [identf], [ident])
        for k in range(16):
            c.dma(c.sp, mem[:, k, :], self.memT[k * 128:(k + 1) * 128, :], memT_[k])
            o_act(c, msq[k % 2][:], mem[:, k, :], AF.Square, [memT_[k]], [msq[k % 2]])
            o_mm(c, ps[0][:, :], self.ones[:], msq[k % 2][:], k == 0, k == 15, [self.ones, msq[k % 2]], [ps[0]], inc=True)
        o_act(c, rstdm[:], ps[0][:, :], AF.Sqrt, [ps[0], self.cst], [rstdm], bias=self.cst[:, 0:1], scale=1.0 / D)
        c.op(c.dve, lambda h: h.reciprocal(out=rstdm[:], in_=rstdm[:]), [rstdm], [rstdm])
        for k in range(16):
            o_stt(c, c.dve, mg[:, k, :], mem[:, k, :], self.vcol("norm_mem", li * 16 + k), rstdm[:], ALU.mult, ALU.mult,
                  [memT_[k], rstdm, self.vec], [mgT[k]])
        W = self.Wmem[li]
        jobs = [dict(src=W[n], w=128, n=n) for n in range(16)]

        def epi(ji, job, s, banks):
            o_copy(c, c.act, kvT[:, job["n"], :], banks[0][:, :], [banks[0]], [kvTT[job["n"]]])
        self.gemm([(mgT[k], mg[:, k, :]) for k in range(16)], 16, [0], jobs, epi, wf, wb, [[ps[1]], [ps[2]]])
        for smb in range(4):
            for q4 in range(2):
                bank = ps[3 + (smb * 2 + q4) % 2]
                for vv in range(4):
                    vc = q4 * 4 + vv
                    o_mm(c, bank[:, vv * 128:(vv + 1) * 128], kvT[:, 8 + vc, smb * 128:(smb + 1) * 128], ident[:], True, True,
                         [kvTT[8 + vc], ident], [bank])
                o_copy(c, c.act, vtm[:, smb, q4 * 512:(q4 + 1) * 512], bank[:, :], [bank], [vtm])
        XQv = self.XQ.rearrange("(k p) t -> p k t", p=128)
        SGv = self.SG.rearrange("(k p) t -> p k t", p=128)
        Yv = self.Y.rearrange("(k p) t -> p k t", p=128)
        it = 0
        for hh in range(4):
            b = hh % 2
            c.dma(c.sp, xq[b][:], XQv[:, hh * 2:hh * 2 + 2, :], xq[b])
            c.dma(c.sp, sg[b][:], SGv[:, 24 + hh * 2:24 + hh * 2 + 2, :], sg[b])
            for si, (t0, L) in enumerate(cfg.segs):
                for qt in range(L // 512):
                    tk = slice(t0 + qt * 512, t0 + (qt + 1) * 512)
                    g3 = ps[2:5] if it % 2 == 0 else ps[5:8]
                    for mb in range(2):
                        sb_ = ps[(it * 2 + mb) % 2]
                        for dc in range(2):
                            o_mm(c, sb_[:, :], kvT[:, hh * 2 + dc, si * 256 + mb * 128: si * 256 + (mb + 1) * 128],
                                 xq[b][:, dc, tk], dc == 0, dc == 1, [kvTT[hh * 2 + dc], xq[b]], [sb_])
                        p_ = pT[(it * 2 + mb) % 4]
                        o_act(c, p_[:], sb_[:, :], AF.Exp, [sb_], [p_])
                    for dc in range(2):
                        for mb in range(2):
                            p_ = pT[(it * 2 + mb) % 4]
                            o_mm(c, g3[dc][:, :], vtm[:, si * 2 + mb, hh * 256 + dc * 128: hh * 256 + (dc + 1) * 128], p_[:],
                                 mb == 0, mb == 1, [vtm, p_], [g3[dc]])
                    for mb in range(2):
                        p_ = pT[(it * 2 + mb) % 4]
                        o_mm(c, g3[2][:, :], self.ones[:], p_[:], mb == 0, mb == 1, [self.ones, p_], [g3[2]])
                    r_ = rec[it % 2]
                    c.op(c.dve, lambda h, r_=r_, g3=g3: h.reciprocal(out=r_[:], in_=g3[2][:, :]), [g3[2]], [r_])
                    for dc in range(2):
                        ot = otmp[dc]
                        o_tt(c, c.dve, ot[:], g3[dc][:, :], r_[:], ALU.mult, [g3[dc], r_], [ot])
                        o_tt(c, c.pool, yo[b][:, dc, tk], ot[:], sg[b][:, dc, tk], ALU.mult, [ot, sg[b]], [yo[b]])
                    it += 1
            c.dma(c.sp, Yv[:, 24 + hh * 2:24 + hh * 2 + 2, :], yo[b][:], yo[b], load=False)
        c.end_stage()

    def stage_outproj(self, li, xsrc, xdst):
        c, cfg = self.c, self.cfg
        T_, NT = cfg.T, cfg.NT
        ps = self.ps
        NB = NT // 2
        c.begin_stage()
        yk = c.sb("yk", [128, 32, 1024], BF16)
        ykT = [c.view(yk[:, k * 8:(k + 1) * 8, :]) for k in range(4)]
        wf = [c.sb("wf%d" % i, [128, 32, 128], F32) for i in range(2)]
        wb = [c.sb("wb%d" % i, [128, 32, 128], BF16) for i in range(2)]
        zt = [c.sb("zt%d" % i, [128, 1024], F32) for i in range(2)]
        sq = [c.sb("sq%d" % i, [128, 1024], BF16) for i in range(2)]
        rstd2 = [c.sb("rstd%d" % i, [128, 1024], F32) for i in range(2)]
        zl = [c.sb("zl%d" % i, [128, 1024], F32) for i in range(3)]
        xl = [c.sb("xl%d" % i, [128, 1024], F32) for i in range(3)]
        xo = [c.sb("xo%d" % i, [128, 1024], F32) for i in range(3)]
        Yv = self.Y.rearrange("(k p) t -> p k t", p=128)
        Zt = [[c.view(self.Z[n * 128:(n + 1) * 128, blk * 1024:(blk + 1) * 1024]) for n in range(16)] for blk in range(NB)]
        W = self.Wout[li]
        jobs = [dict(src=W[n], w=128, n=n) for n in range(16)]

        def load_yk(blk):
            tok = slice(blk * 1024, (blk + 1) * 1024)
            for k in range(4):
                c.dma(c.sp, yk[:, k * 8:(k + 1) * 8, :], Yv[:, k * 8:(k + 1) * 8, tok], ykT[k])

        pst = []

        def flush_store():
            while pst:
                blk_, n_, i_ = pst.pop(0)
                tok_ = slice(blk_ * 1024, (blk_ + 1) * 1024)
                c.dma(c.sp, xdst[n_ * 128:(n_ + 1) * 128, tok_], xo[i_][:], xo[i_], load=False)
                if self.dbg is not None and li < len(self.dbg):
                    c.dma(c.sp, self.dbg[li][n_ * 128:(n_ + 1) * 128, tok_], xo[i_][:], xo[i_], load=False)

        def n2_iter(blk, n):
            tok = slice(blk * 1024, (blk + 1) * 1024)
            i = n % 3
            rstd = rstd2[blk % 2]
            flush_store()
            c.dma(c.sp, zl[i][:], self.Z[n * 128:(n + 1) * 128, tok], zl[i], reads=[Zt[blk][n]])
            c.dma(c.sp, xl[i][:], xsrc[n * 128:(n + 1) * 128, tok], xl[i])
            o_stt(c, c.dve, zl[i][:], zl[i][:], self.vcol("norm_post", li * 16 + n), rstd[:], ALU.mult, ALU.mult,
                  [zl[i], rstd, self.vec], [zl[i]])
            o_tt(c, c.pool, xo[i][:], zl[i][:], xl[i][:], ALU.add, [zl[i], xl[i]], [xo[i]])
            pst.append((blk, n, i))

        load_yk(0)
        for blk in range(NB):
            tok = slice(blk * 1024, (blk + 1) * 1024)
            rstd = rstd2[blk % 2]
            pend = []

            def ss_mm(n, i):
                for idx in range(2):
                    o_mm(c, ps[4 + idx][:, :], self.ones[:], sq[i][:, idx * 512:(idx + 1) * 512], n == 0, n == 15,
                         [self.ones, sq[i]], [ps[4 + idx]], inc=True)

            def epi(ji, job, s, banks, blk=blk, tok=tok):
                i = ji % 2
                n = job["n"]
                while pend:
                    ss_mm(*pend.pop(0))
                for idx in range(2):
                    sl = slice(idx * 512, (idx + 1) * 512)
                    o_copy(c, c.act, zt[i][:, sl], banks[idx][:, :], [banks[idx]], [zt[i]])
                    o_tt(c, c.dve, sq[i][:, sl], zt[i][:, sl], zt[i][:, sl], ALU.mult, [zt[i]], [sq[i]])
                c.dma(c.sp, self.Z[n * 128:(n + 1) * 128, tok], zt[i][:], zt[i], load=False, writes=[Zt[blk][n]])
                pend.append((n, i))
            hook = None
            if blk > 0:
                hook = (lambda ji, pb=blk - 1: n2_iter(pb, ji))
            xk = [(ykT[k // 8], yk[:, k, :]) for k in range(32)]
            self.gemm(xk, 32, [0, 1], jobs, epi, wf, wb, [ps[0:2], ps[2:4]], hook=hook)
            while pend:
                ss_mm(*pend.pop(0))
            for idx in range(2):
                sl = slice(idx * 512, (idx + 1) * 512)
                o_act(c, rstd[:, sl], ps[4 + idx][:, :], AF.Sqrt, [ps[4 + idx], self.cst], [rstd],
                      bias=self.cst[:, 0:1], scale=1.0 / D)
            c.op(c.dve, lambda h, rstd=rstd: h.reciprocal(out=rstd[:], in_=rstd[:]), [rstd], [rstd])
            if blk + 1 < NB:
                load_yk(blk + 1)
        for n in range(16):
            n2_iter(NB - 1, n)
        flush_store()
        c.end_stage()

    def halo_exchange(self, nL, nR, name):
        c, cfg = self.c, self.cfg
        ne = nL + nR
        hal = c.sb("hal_" + name, [128, 24, 2, ne], F32)
        XSv = self.XS.rearrange("(k p) t -> p k t", p=128)
        for si, (t0, L) in enumerate(cfg.segs):
            for k6 in range(4):
                ks = slice(k6 * 6, (k6 + 1) * 6)
                c.dma(c.sp, hal[:, ks, si, 0:nR], XSv[:, ks, t0:t0 + nR], hal, allow_slow_non_contiguous=True)
                c.dma(c.sp, hal[:, ks, si, nR:ne], XSv[:, ks, t0 + L - nL:t0 + L], hal, allow_slow_non_contiguous=True)
        g = self._exchange4(hal, 24 * 2 * ne, name)
        hl = c.sb("haloL_" + name, [128, 24, 2, nL], F32)
        hr = c.sb("haloR_" + name, [128, 24, 2, nR], F32)

        def gv(r, lo, hi):
            return g[:, r, :].rearrange("p (k s e) -> p k s e", k=24, s=2)[:, :, :, lo:hi]
        self.rank_select(hl[:], lambda r: gv(r, nR, ne), 0, hl, g)
        self.rank_select(hr[:], lambda r: gv(r, 0, nR), 4, hr, g)
        return hl, hr

    def _exchange4(self, tile4, W, name):
        c, nc = self.c, self.nc
        ein = nc.dram_tensor("exi_" + name, [128, W], F32)
        eout = nc.dram_tensor("exo_" + name, [4 * 128, W], F32)
        g = c.sb("exg_" + name, [128, 4, W], F32)
        nd = len(tile4.ap.shape)
        if nd == 4:
            flat = tile4[:].rearrange("p a b e -> p (a b e)")
        elif nd == 3:
            flat = tile4[:].rearrange("p a b -> p (a b)")
        else:
            flat = tile4[:]
        ev = c.dma(c.sp, ein.ap()[:, :], flat, tile4, load=False)
        key = self._cc(ein, eout, ev)
        sems = c.sems
        c.sp.prog.append(lambda h, key=key: h.wait_ge(sems[key], 1))
        c.dma(c.sp, g[:], eout.ap().rearrange("(r p) w -> p r w", p=128), g)
        return g

    def stage_rglru(self, aj):
        c, cfg = self.c, self.cfg
        T_, NT = cfg.T, cfg.NT
        ps = self.ps
        TE = T_ + 6
        c.begin_stage()
        kk = c.sb("kk", [128, 48], F32)
        xx = c.sb("xx", [128, 48], F32)
        o_act(c, xx[:], self.vcol("lam", aj * 48, 48), AF.Exp, [self.vec], [xx], scale=-1.0)
        o_ts(c, c.dve, kk[:], xx[:], -0.2, 0.25, ALU.mult, ALU.add, [xx], [kk])
        for cst_ in (1.0 / 3.0, 0.5, 1.0):
            o_tt(c, c.dve, kk[:], kk[:], xx[:], ALU.mult, [kk, xx], [kk])
            o_ts(c, c.dve, kk[:], kk[:], -1.0, cst_, ALU.mult, ALU.add, [kk], [kk])
        o_tt(c, c.dve, kk[:], kk[:], xx[:], ALU.mult, [kk, xx], [kk])
        o_ts(c, c.dve, kk[:], kk[:], -8.0, None, ALU.mult, None, [kk], [kk])
        hl_, hr_ = self.halo_exchange(2, 1, "a%d" % aj)
        contrib = c.sb("contrib", [128, 24, 2, 4], F32)
        xe = [c.sb("xe%d" % i, [128, TE], F32) for i in range(2)]
        u = [c.sb("u%d" % i, [128, T_], F32) for i in range(2)]
        ub = [c.sb("ub%d" % i, [128, T_], BF16) for i in range(2)]
        r2 = [c.sb("r_%d" % i, [128, T_], F32) for i in range(2)]
        i2 = [c.sb("i_%d" % i, [128, T_], F32) for i in range(2)]
        a_ = [c.sb("a_%d" % i, [128, T_], F32) for i in range(2)]
        b_ = [c.sb("b_%d" % i, [128, T_], F32) for i in range(2)]
        tmp2 = [c.sb("tmp%d" % i, [128, T_], F32) for i in range(2)]
        wgf = c.sb("wgf", [128, 4, 2, 256], F32)
        wgb = c.sb("wgb", [128, 4, 2, 256], BF16)
        rsum2 = [c.sb("rsum%d" % i, [128, 2], F32) for i in range(2)]
        nab = 0
        for h in range(12):
            for dg in range(4):
                d, g = dg // 2, dg % 2
                gj = ((aj * 2 + d) * 2 + g) * 12 + h
                c.dma(c.sp, wgf[:, dg, :, :], self.gatew[gj].rearrange("p (kc n) -> p kc n", kc=2), wgf)
            o_copy(c, c.pool, wgb[:], wgf[:], [wgf], [wgb])
            for oc in range(2):
                cc = 2 * h + oc
                for si, (t0, L) in enumerate(cfg.segs):
                    e0 = t0 + 3 * si
                    c.dma(c.sp, xe[oc][:, e0 + 2:e0 + 2 + L], self.XS[cc * 128:(cc + 1) * 128, t0:t0 + L], xe[oc])
                for si, (t0, L) in enumerate(cfg.segs):
                    e0 = t0 + 3 * si
                    o_copy(c, c.pool, xe[oc][:, e0:e0 + 2], hl_[:, cc, si, :], [hl_], [xe[oc]])
                    o_copy(c, c.pool, xe[oc][:, e0 + 2 + L:e0 + 3 + L], hr_[:, cc, si, :], [hr_], [xe[oc]])
                for si, (t0, L) in enumerate(cfg.segs):
                    e0 = t0 + 3 * si
                    o_ts(c, c.dve, u[oc][:, t0:t0 + L], xe[oc][:, e0:e0 + L], self.vcol("conv_w", (aj * 4 + 0) * 24 + cc),
                         self.vcol("conv_b", aj * 24 + cc), ALU.mult, ALU.add, [xe[oc], self.vec], [u[oc]])
                    for k in range(1, 4):
                        o_stt(c, c.dve, u[oc][:, t0:t0 + L], xe[oc][:, e0 + k:e0 + k + L],
                              self.vcol("conv_w", (aj * 4 + k) * 24 + cc), u[oc][:, t0:t0 + L], ALU.mult, ALU.add,
                              [xe[oc], self.vec, u[oc]], [u[oc]])
                o_copy(c, c.act, ub[oc][:], u[oc][:], [u[oc]], [ub[oc]])
            for oc in range(2):
                cc = 2 * h + oc
                for d in range(2):
                    r_, i_, tmp, rsum = r2[nab % 2], i2[nab % 2], tmp2[nab % 2], rsum2[nab % 2]
                    for t in range(NT):
                        tk = slice(t * 512, (t + 1) * 512)
                        pr, pi = ps[(t % 2) * 2], ps[(t % 2) * 2 + 1]
                        for g, pb in ((0, pr), (1, pi)):
                            for kc in range(2):
                                o_mm(c, pb[:, :], wgb[:, d * 2 + g, kc, oc * 128:(oc + 1) * 128], ub[kc][:, tk], kc == 0, kc == 1,
                                     [wgb, ub[kc]], [pb])
                        o_act(c, r_[:, tk], pr[:, :], AF.Sigmoid, [pr, self.vec], [r_],
                              bias=self.vcol("gate_b", ((aj * 2 + d) * 2 + 0) * 24 + cc))
                        o_act(c, i_[:, tk], pi[:, :], AF.Sigmoid, [pi, self.vec], [i_],
                              bias=self.vcol("gate_b", ((aj * 2 + d) * 2 + 1) * 24 + cc))
                    A, B = a_[nab % 2], b_[nab % 2]
                    nab += 1
                    kcol = kk[:, d * 24 + cc:d * 24 + cc + 1]
                    for si, (t0, L) in enumerate(cfg.segs):
                        sl = slice(t0, t0 + L)
                        c.op(c.dve, lambda h_, si=si, sl=sl, rsum=rsum, r_=r_: h_.tensor_reduce(out=rsum[:, si:si + 1], in_=r_[:, sl], axis=AX.X, op=ALU.add),
                             [r_], [rsum])
                    o_act(c, A[:], r_[:], AF.Exp, [r_, kk], [A], scale=kcol)
                    for si, (t0, L) in enumerate(cfg.segs):
                        o_act(c, contrib[:, cc, si, d * 2:d * 2 + 1], rsum[:, si:si + 1], AF.Exp, [rsum, kk], [contrib], scale=kcol)
                    o_act(c, tmp[:], A[:], AF.Square, [A], [tmp])
                    o_act(c, tmp[:], tmp[:], AF.Sqrt, [tmp, self.cst], [tmp], bias=self.cst[:, 1:2], scale=-1.0)
                    o_tt(c, c.pool, i_[:], i_[:], u[oc][:], ALU.mult, [i_, u[oc]], [i_])
                    o_tt(c, c.dve, B[:], i_[:], tmp[:], ALU.mult, [i_, tmp], [B])
                    for si, (t0, L) in enumerate(cfg.segs):
                        sl = slice(t0, t0 + L)
                        if d == 0:
                            oa, da, db = r_[:, sl], A[:, sl], B[:, sl]
                            hend = r_[:, t0 + L - 1:t0 + L]
                        else:
                            oa, da, db = rev_ap(r_[:, sl]), rev_ap(A[:, sl]), rev_ap(B[:, sl])
                            hend = r_[:, t0:t0 + 1]
                        c.op(c.dve, lambda h_, oa=oa, da=da, db=db: h_.tensor_tensor_scan(out=oa, data0=da, data1=db, initial=0.0,
                                                                                     op0=ALU.mult, op1=ALU.add),
                             [A, B, r_], [r_])
                        o_copy(c, c.dve, contrib[:, cc, si, d * 2 + 1:d * 2 + 2], hend, [r_], [contrib])
                    c.dma(c.sp, self.SPL[d * 2][cc * 128:(cc + 1) * 128, :], A[:], A, load=False)
                    c.dma(c.sp, self.SPL[d * 2 + 1][cc * 128:(cc + 1) * 128, :], B[:], B, load=False)
        g2 = self._exchange4(contrib, 192, "c%d" % aj)

        def gP(r, d):
            return g2[:, r, :].rearrange("p (k e) -> p k e", e=4)[:, :, d * 2:d * 2 + 1]

        def gH(r, d):
            return g2[:, r, :].rearrange("p (k e) -> p k e", e=4)[:, :, d * 2 + 1:d * 2 + 2]
        S = [c.sb("S%d" % i, [128, 48, 1], F32) for i in range(3)]
        hin = self.hin[:].rearrange("p (k d) -> p k d", d=2)
        E = c.dve
        m = lambda j: self.masks[:, 8 + j:9 + j]
        o_copy(c, E, S[0][:], gH(0, 0), [g2], [S[0]])
        o_tt(c, E, S[1][:], gP(1, 0), S[0][:], ALU.mult, [g2, S[0]], [S[1]])
        o_tt(c, E, S[1][:], S[1][:], gH(1, 0), ALU.add, [g2, S[1]], [S[1]])
        o_tt(c, E, S[2][:], gP(2, 0), S[1][:], ALU.mult, [g2, S[1]], [S[2]])
        o_tt(c, E, S[2][:], S[2][:], gH(2, 0), ALU.add, [g2, S[2]], [S[2]])
        o_ts(c, E, hin[:, :, 0:1], S[0][:], m(1), None, ALU.mult, None, [S[0], self.masks], [self.hin])
        o_stt(c, E, hin[:, :, 0:1], S[1][:], m(2), hin[:, :, 0:1], ALU.mult, ALU.add, [S[1], self.masks, self.hin], [self.hin])
        o_stt(c, E, hin[:, :, 0:1], S[2][:], m(3), hin[:, :, 0:1], ALU.mult, ALU.add, [S[2], self.masks, self.hin], [self.hin])
        o_copy(c, E, S[0][:], gH(3, 1), [g2, self.hin], [S[0]])
        o_tt(c, E, S[1][:], gP(2, 1), S[0][:], ALU.mult, [g2, S[0]], [S[1]])
        o_tt(c, E, S[1][:], S[1][:], gH(2, 1), ALU.add, [g2, S[1]], [S[1]])
        o_tt(c, E, S[2][:], gP(1, 1), S[1][:], ALU.mult, [g2, S[1]], [S[2]])
        o_tt(c, E, S[2][:], S[2][:], gH(1, 1), ALU.add, [g2, S[2]], [S[2]])
        o_ts(c, E, hin[:, :, 1:2], S[0][:], m(2), None, ALU.mult, None, [S[0], self.masks], [self.hin])
        o_stt(c, E, hin[:, :, 1:2], S[1][:], m(1), hin[:, :, 1:2], ALU.mult, ALU.add, [S[1], self.masks, self.hin], [self.hin])
        o_stt(c, E, hin[:, :, 1:2], S[2][:], m(0), hin[:, :, 1:2], ALU.mult, ALU.add, [S[2], self.masks, self.hin], [self.hin])
        c.end_stage()
        c.begin_stage()
        ld = [[c.sb("ld%d_%d" % (q, i), [128, T_], F32) for q in range(4)] for i in range(2)]
        sg = [c.sb("sg%d" % i, [128, T_], BF16) for i in range(2)]
        hf = c.sb("hf", [128, T_], F32)
        hb = c.sb("hb", [128, T_], F32)
        yo = [c.sb("yo%d" % i, [128, T_], BF16) for i in range(2)]
        for cc in range(24):
            i = cc % 2
            rows = slice(cc * 128, (cc + 1) * 128)
            for q in range(4):
                c.dma(c.sp if q % 2 == 0 else c.act, ld[i][q][:], self.SPL[q][rows, :], ld[i][q])
            c.dma(c.sp, sg[i][:], self.SG[rows, :], sg[i])
            for si, (t0, L) in enumerate(cfg.segs):
                sl = slice(t0, t0 + L)
                ci = (cc * 2 + si) * 2
                f0 = slice(t0, t0 + 1)
                l0 = slice(t0 + L - 1, t0 + L)
                o_stt(c, c.dve, ld[i][1][:, f0], ld[i][0][:, f0], self.hin[:, ci:ci + 1], ld[i][1][:, f0], ALU.mult, ALU.add,
                      [ld[i][0], ld[i][1], self.hin], [ld[i][1]])
                o_stt(c, c.dve, ld[i][3][:, l0], ld[i][2][:, l0], self.hin[:, ci + 1:ci + 2], ld[i][3][:, l0], ALU.mult, ALU.add,
                      [ld[i][2], ld[i][3], self.hin], [ld[i][3]])
                c.op(c.dve, lambda h_, sl=sl, i=i: h_.tensor_tensor_scan(out=hf[:, sl], data0=ld[i][0][:, sl], data1=ld[i][1][:, sl],
                                                                    initial=0.0, op0=ALU.mult, op1=ALU.add),
                     [ld[i][0], ld[i][1]], [hf])
                c.op(c.dve, lambda h_, sl=sl, i=i: h_.tensor_tensor_scan(out=rev_ap(hb[:, sl]), data0=rev_ap(ld[i][2][:, sl]),
                                                                    data1=rev_ap(ld[i][3][:, sl]),
                                                                    initial=0.0, op0=ALU.mult, op1=ALU.add),
                     [ld[i][2], ld[i][3]], [hb])
            o_tt(c, c.pool, hf[:], hf[:], hb[:], ALU.add, [hf, hb], [hf])
            o_tt(c, c.pool, yo[i][:], hf[:], sg[i][:], ALU.mult, [hf, sg[i]], [yo[i]])
            c.dma(c.act, self.Y[rows, :], yo[i][:], yo[i], load=False)
        c.end_stage()

    def stage_pool(self):
        c, cfg = self.c, self.cfg
        T_, NT = cfg.T, cfg.NT
        ps = self.ps
        TE = T_ + 30
        c.begin_stage()
        hl_, hr_ = self.halo_exchange(8, 7, "p")
        xe = [c.sb("xe%d" % i, [128, TE], F32) for i in range(2)]
        sA = [c.sb("sA%d" % i, [128, TE], F32) for i in range(2)]
        sB = [c.sb("sB%d" % i, [128, TE], F32) for i in range(2)]
        invg = c.sb("invg", [128, T_], F32)
        pooled = c.sb("pooled", [128, 6, T_], BF16)
        pooledT = [c.view(pooled[:, k, :]) for k in range(6)]
        wf = [c.sb("wf%d" % i, [128, 6, 128], F32) for i in range(2)]
        wb = [c.sb("wb%d" % i, [128, 6, 128], BF16) for i in range(2)]
        sgt = [c.sb("sgt%d" % i, [128, 1536], BF16) for i in range(2)]
        of = [c.sb("of%d" % i, [128, 1536], F32) for i in range(2)]
        ob = [c.sb("ob%d" % i, [128, 1536], BF16) for i in range(2)]
        for gi in range(4):
            w = 2 << gi
            c.dma(c.sp, invg[:], self.invc_d[gi:gi + 1, :].partition_broadcast(128), invg)
            for kq in range(6):
                cc = gi * 6 + kq
                i = cc % 2
                E = c.dve if i == 0 else c.pool
                for si, (t0, L) in enumerate(cfg.segs):
                    e0 = t0 + 15 * si
                    c.dma(c.sp, xe[i][:, e0 + 8:e0 + 8 + L], self.XS[cc * 128:(cc + 1) * 128, t0:t0 + L], xe[i])
                for si, (t0, L) in enumerate(cfg.segs):
                    e0 = t0 + 15 * si
                    o_copy(c, E, xe[i][:, e0:e0 + 8], hl_[:, cc, si, :], [hl_], [xe[i]])
                    o_copy(c, E, xe[i][:, e0 + 8 + L:e0 + 15 + L], hr_[:, cc, si, :], [hr_], [xe[i]])
                for si, (t0, L) in enumerate(cfg.segs):
                    e0 = t0 + 15 * si
                    Le = L + 15
                    o_tt(c, E, sA[i][:, e0 + 1:e0 + Le], xe[i][:, e0:e0 + Le - 1], xe[i][:, e0 + 1:e0 + Le], ALU.add, [xe[i]], [sA[i]])
                    cur, oth = sA[i], sB[i]
                    lo, hi, hw = 1, Le, 1
                    while hw * 2 < w:
                        nlo, nhi = lo + hw, hi - hw
                        o_tt(c, E, oth[:, e0 + nlo:e0 + nhi], cur[:, e0 + nlo - hw:e0 + nhi - hw], cur[:, e0 + nlo + hw:e0 + nhi + hw],
                             ALU.add, [cur], [oth])
                        cur, oth = oth, cur
                        lo, hi, hw = nlo, nhi, hw * 2
                    o_tt(c, E, oth[:, e0 + 8:e0 + 8 + L], cur[:, e0 + 8:e0 + 8 + L], invg[:, t0:t0 + L], ALU.mult, [cur, invg], [oth])
                    o_tt(c, E, pooled[:, kq, t0:t0 + L], oth[:, e0 + 8:e0 + 8 + L], xe[i][:, e0 + 8:e0 + 8 + L], ALU.subtract,
                         [oth, xe[i]], [pooledT[kq]])
            jobs = [dict(src=self.wgrp[gi * 6 + m_], w=128, n=gi * 6 + m_) for m_ in range(6)]
            st = {"i": 0}

            def epi(ji, job, s, banks):
                i = st["i"] % 2
                st["i"] += 1
                n = len(s) * 512
                tok0 = s[0] * 512
                cc = job["n"]
                rows = slice(cc * 128, (cc + 1) * 128)
                c.dma(c.sp, sgt[i][:, 0:n], self.SG[rows, tok0:tok0 + n], sgt[i])
                for idx, t in enumerate(s):
                    sl = slice(idx * 512, (idx + 1) * 512)
                    o_ts(c, c.dve, of[i][:, sl], banks[idx][:, :], self.vcol("c_scale", cc), None, ALU.mult, None,
                         [banks[idx], self.vec], [of[i]])
                o_tt(c, c.pool, ob[i][:, 0:n], of[i][:, 0:n], sgt[i][:, 0:n], ALU.mult, [of[i], sgt[i]], [ob[i]])
                c.dma(c.sp, self.Y[rows, tok0:tok0 + n], ob[i][:, 0:n], ob[i], load=False)
            nset = 3 if NT % 3 == 0 else 2
            self.gemm([(pooledT[k], pooled[:, k, :]) for k in range(6)], 6, list(range(NT)), jobs, epi, wf, wb,
                      [ps[0:nset], ps[nset:2 * nset]])
        c.end_stage()

    def stage_mla(self):
        c, cfg, nc = self.c, self.cfg, self.nc
        T_, NT = cfg.T, cfg.NT
        ps = self.ps
        LATi = [t.ap() for t in self.LATi]
        LATo = [t.ap() for t in self.LATo]
        c.begin_stage()
        cq = c.sb("cq", [128, 6, T_], F32)
        cqT = [c.view(cq[:, k, :]) for k in range(6)]
        sq = [c.sb("sq%d" % i, [128, T_], BF16) for i in range(2)]
        rs = [c.sb("rs%d" % i, [128, T_], F32) for i in range(2)]
        nb = [c.sb("nb%d" % i, [128, T_], BF16) for i in range(2)]
        ka = c.sb("ka", [64, T_], F32)
        kb = c.sb("kb", [64, T_], F32)
        ck = c.sb("ck", [64, T_], F32)
        sk = c.sb("sk", [64, T_], F32)
        ko = c.sb("ko", [64, T_], BF16)
        for k in range(6):
            src = self.CQ[k * 128:(k + 1) * 128, :] if k < 4 else self.CKV[(k - 4) * 128:(k - 3) * 128, :]
            c.dma(c.sp, cq[:, k, :], src, cqT[k])
        for which, (k0, k1, feat) in enumerate(((0, 4, 512.0), (4, 6, 256.0))):
            for k in range(k0, k1):
                o_act(c, sq[k % 2][:], cq[:, k, :], AF.Square, [cqT[k]], [sq[k % 2]])
                for t in range(NT):
                    o_mm(c, ps[t][:, :], self.ones[:], sq[k % 2][:, t * 512:(t + 1) * 512], k == k0, k == k1 - 1,
                         [self.ones, sq[k % 2]], [ps[t]], inc=True)
            for t in range(NT):
                tk = slice(t * 512, (t + 1) * 512)
                o_act(c, rs[which][:, tk], ps[t][:, :], AF.Sqrt, [ps[t], self.cst], [rs[which]], bias=self.cst[:, 0:1], scale=1.0 / feat)
            c.op(c.dve, lambda h, which=which: h.reciprocal(out=rs[which][:], in_=rs[which][:]), [rs[which]], [rs[which]])
        for k in range(6):
            which = 0 if k < 4 else 1
            gcol = self.vcol("q_norm", k) if k < 4 else self.vcol("kv_norm", k - 4)
            o_stt(c, c.dve, nb[k % 2][:], cq[:, k, :], gcol, rs[which][:], ALU.mult, ALU.mult, [cqT[k], rs[which], self.vec], [nb[k % 2]])
            if k < 4:
                c.dma(c.sp, self.CQN[k * 128:(k + 1) * 128, :], nb[k % 2][:], nb[k % 2], load=False)
            else:
                c.dma(c.sp, LATi[k - 4][:, :], nb[k % 2][:], nb[k % 2], load=False)
        c.dma(c.sp, ka[:], self.KR[0:64, :], ka)
        c.dma(c.sp, kb[:], self.KR[64:128, :], kb)
        c.dma(c.sp, ck[:], self.rope_d[2], ck)
        c.dma(c.sp, sk[:], self.rope_d[3], sk)
        o_tt(c, c.dve, ka[:], ka[:], ck[:], ALU.mult, [ka, ck], [ka])
        o_tt(c, c.pool, kb[:], kb[:], sk[:], ALU.mult, [kb, sk], [kb])
        o_tt(c, c.dve, ko[:], ka[:], kb[:], ALU.add, [ka, kb], [ko])
        c.dma(c.sp, LATi[2][:, :], ko[:], ko, load=False)
        c.end_stage()
        sems = c.sems
        for gi_ in range(3):
            key = "cc%d" % self.ncc
            self.ncc += 1

            def emit_cc(h, key=key, gi_=gi_):
                h.collective_compute("AllGather", ALU.bypass, replica_groups=[[0, 1, 2, 3], [4, 5, 6, 7]],
                                     ins=[self.LATi[gi_].ap().opt()], outs=[self.LATo[gi_].ap().opt()]).then_inc(sems[key])
                h.wait_ge(sems[key], 1)
            c.pool.prog.append(emit_cc)
        c.op(c.pool, lambda h: h.memset(self.cst[:, 3:4], 0.0), (), [self.cst])
        c.drain_all()
        SCALE = 192.0 ** -0.5
        import os as _os3
        seg_order = list(enumerate(cfg.segs))
        if _os3.environ.get("KSWAP"):
            seg_order = seg_order[::-1]
        for si, (t0, L) in seg_order:
            Lk = 4 * L
            NQ = L // 512
            NKB = Lk // 128
            c.begin_stage()
            ckva = c.sb("ckva", [128, 2, Lk], BF16)
            kra = c.sb("kra", [64, Lk], BF16)
            cqn = c.sb("cqn", [128, 4, L], BF16)
            Cq = c.sb("Cq", [64, L], F32)
            Sq = c.sb("Sq", [64, L], F32)
            khT = c.sb("khT", [128, Lk], BF16)
            vh = c.sb("vh", [128, NKB, 128], BF16)
            qn = c.sb("qn", [128, L], BF16)
            qr = c.sb("qr", [64, L], BF16)
            wqf = c.sb("wqf", [128, 4, 192], F32)
            wqb = c.sb("wqb", [128, 4, 256], BF16)
            wkf = c.sb("wkf", [128, 2, 256], F32)
            wkb = c.sb("wkb", [128, 2, 256], BF16)
            pT = [c.sb("pT%d" % i, [128, 512], BF16) for i in range(8)]
            t1 = c.sb("t1", [64, 512], F32)
            t2 = c.sb("t2", [64, 512], F32)
            rec = c.sb("rec", [128, 512], F32)
            ot = c.sb("ot", [128, 512], F32)
            accs = [c.sb("acc%d" % i, [128, 512], F32) for i in range(4)]
            sgt = [c.sb("sgt%d" % i, [128, L], BF16) for i in range(2)]
            yh = [c.sb("yh%d" % i, [128, L], BF16) for i in range(2)]
            for r in range(4):
                for kc_ in range(2):
                    c.dma(c.sp, ckva[:, kc_, r * L:(r + 1) * L], LATo[kc_][r * 128:(r + 1) * 128, t0:t0 + L], ckva)
                c.dma(c.sp, kra[:, r * L:(r + 1) * L], LATo[2][r * 64:(r + 1) * 64, t0:t0 + L], kra)
            c.dma(c.sp, cqn[:], self.CQN[:, t0:t0 + L].rearrange("(k p) t -> p k t", p=128), cqn)
            c.dma(c.sp, Cq[:], self.rope_d[0][:, t0:t0 + L], Cq)
            c.dma(c.sp, Sq[:], self.rope_d[1][:, t0:t0 + L], Sq)
            for hd in range(24):
                hi = hd % 2
                c.dma(c.sp, wqf[:], self.wq[hd].rearrange("p (kc n) -> p kc n", kc=4), wqf)
                c.dma(c.sp, wkf[:], self.wkv[hd].rearrange("p (kc n) -> p kc n", kc=2), wkf)
                c.dma(c.sp, sgt[hi][:], self.SG[hd * 128:(hd + 1) * 128, t0:t0 + L], sgt[hi])
                o_copy(c, c.pool, wqb[:, :, 0:192], wqf[:, :, 0:192], [wqf], [wqb])
                o_ts(c, c.pool, wqb[:, :, 192:224], wqf[:, :, 160:192], -1.0, None, ALU.mult, None, [wqf], [wqb])
                o_copy(c, c.pool, wqb[:, :, 224:256], wqf[:, :, 128:160], [wqf], [wqb])
                o_copy(c, c.pool, wkb[:], wkf[:], [wkf], [wkb])
                for qt in range(NQ):
                    tk = slice(qt * 512, (qt + 1) * 512)
                    pn, pa, pb = ps[0], ps[1], ps[2]
                    for kc in range(4):
                        o_mm(c, pn[:, :], wqb[:, kc, 0:128], cqn[:, kc, tk], kc == 0, kc == 3, [wqb, cqn], [pn])
                    for kc in range(4):
                        o_mm(c, pa[0:64, :], wqb[:, kc, 128:192], cqn[:, kc, tk], kc == 0, kc == 3, [wqb, cqn], [pa])
                    for kc in range(4):
                        o_mm(c, pb[0:64, :], wqb[:, kc, 192:256], cqn[:, kc, tk], kc == 0, kc == 3, [wqb, cqn], [pb])
                    c.op(c.act, lambda h, tk=tk, pn=pn, qn=qn: h.mul(out=qn[:, tk], in_=pn[:, :], mul=SCALE), [pn], [qn])
                    o_tt(c, c.dve, t1[:], pa[0:64, :], Cq[:, tk], ALU.mult, [pa, Cq], [t1])
                    o_tt(c, c.dve, t2[:], pb[0:64, :], Sq[:, tk], ALU.mult, [pb, Sq], [t2])
                    o_tt(c, c.dve, qr[:, tk], t1[:], t2[:], ALU.add, [t1, t2], [qr])
                for kt in range(Lk // 512):
                    tk = slice(kt * 512, (kt + 1) * 512)
                    pk = ps[kt % 2]
                    for kc in range(2):
                        o_mm(c, pk[:, :], wkb[:, kc, 0:128], ckva[:, kc, tk], kc == 0, kc == 1, [wkb, ckva], [pk])
                    o_copy(c, c.dve if kt % 2 == 0 else c.act, khT[:, tk], pk[:, :], [pk], [khT])
                for k4 in range(NKB // 4):
                    pv = ps[2 + k4 % 2]
                    for j in range(4):
                        kbk = k4 * 4 + j
                        for kc in range(2):
                            o_mm(c, pv[:, j * 128:(j + 1) * 128], ckva[:, kc, kbk * 128:(kbk + 1) * 128], wkb[:, kc, 128:256],
                                 kc == 0, kc == 1, [wkb, ckva], [pv])
                    o_copy(c, c.dve if k4 % 2 == 0 else c.act, vh[:, k4 * 4:(k4 + 1) * 4, :],
                           pv[:, :].rearrange("p (j d) -> p j d", j=4), [pv], [vh])
                NQG = min(NQ, 4)
                nsets = 4 // NQG
                for qg in range(NQ // NQG):
                    qts = [qg * NQG + i_ for i_ in range(NQG)]
                    tks = [slice(qt * 512, (qt + 1) * 512) for qt in qts]
                    pos = [ps[4 + i_] for i_ in range(NQG)]

                    def pv_mm(kbk, pts):
                        for i_ in range(NQG):
                            o_mm(c, pos[i_][:, :], vh[:, kbk, :], pts[i_][:], kbk == 0, kbk == NKB - 1, [vh, pts[i_]], [pos[i_]])
                    prev = None
                    for kbk in range(NKB):
                        sset = [ps[(kbk % nsets) * NQG + i_] for i_ in range(NQG)]
                        ks = slice(kbk * 128, (kbk + 1) * 128)
                        for i_ in range(NQG):
                            c.op(c.pe, lambda h, sb_=sset[i_], ks=ks, tk=tks[i_], khT=khT, qn=qn:
                                 h.matmul(sb_[:, :], lhsT=khT[:, ks], rhs=qn[:, tk], start=True, stop=False),
                                 [khT, qn], [sset[i_]], inc=False)
                        for i_ in range(NQG):
                            o_mm(c, sset[i_][:, :], kra[:, ks], qr[:, tks[i_]], False, True, [kra, qr], [sset[i_]])
                        pts = []
                        for i_ in range(NQG):
                            p_ = pT[(kbk % 2) * NQG + i_]
                            o_act(c, p_[:], sset[i_][:, :], AF.Exp, [sset[i_]], [p_])
                            Ea = c.dve if i_ % 2 == 0 else c.pool
                            if kbk == 0:
                                o_copy(c, Ea, accs[i_][:], p_[:], [p_], [accs[i_]])
                            else:
                                o_tt(c, Ea, accs[i_][:], accs[i_][:], p_[:], ALU.add, [accs[i_], p_], [accs[i_]])
                            pts.append(p_)
                        if prev is not None:
                            pv_mm(*prev)
                        prev = (kbk, pts)
                    pv_mm(*prev)
                    for i_ in range(NQG):
                        pm = ps[i_]
                        o_mm(c, pm[:, :], self.ones_f[:], accs[i_][:], True, True, [self.ones_f, accs[i_]], [pm])
                        c.op(c.dve, lambda h, pm=pm, rec=rec: h.reciprocal(out=rec[:], in_=pm[:, :]), [pm], [rec])
                        o_tt(c, c.dve, ot[:], pos[i_][:, :], rec[:], ALU.mult, [pos[i_], rec], [ot])
                        o_tt(c, c.pool, yh[hi][:, tks[i_]], ot[:], sgt[hi][:, tks[i_]], ALU.mult, [ot, sgt[hi]], [yh[hi]])
                c.dma(c.sp, self.Y[hd * 128:(hd + 1) * 128, t0:t0 + L], yh[hi][:], yh[hi], load=False)
            c.end_stage()


def _rope_tables(pos):
    inv_freq = (1.0 / (10000.0 ** (np.arange(0, 64, 2, dtype=np.float32) / np.float32(64.0)))).astype(np.float32)
    ang = (pos.astype(np.float32)[:, None] * inv_freq[None, :]).astype(np.float32)
    cs = np.cos(ang).astype(np.float32).T
    sn = np.sin(ang).astype(np.float32).T
    C = np.concatenate([cs, cs], axis=0)
    S = np.concatenate([sn, sn], axis=0)
    s = np.float32(192.0 ** -0.5)
    return np.ascontiguousarray(np.stack([C * s, S * s, C, S]).astype(np.float32))


def _invc(pos, S):
    out = np.zeros((4, pos.size), np.float32)
    for g, w in enumerate((2, 4, 8, 16)):
        st = np.clip(pos - w // 2, 0, S)
        en = np.clip(pos + w - w // 2, 0, S)
        out[g] = 1.0 / (en - st).astype(np.float32)
    return out


_CACHE = {}


def run(inputs, LP, LS, nlayers=4, n_cores=8):
    cfg = Cfg(LP, LS, nlayers)
    f = lambda a: np.ascontiguousarray(np.asarray(a, dtype=np.float32))
    vp = VecPack()
    vp.add("norm_pre", inputs["norm_pre"])
    vp.add("norm_post", inputs["norm_post"])
    vp.add("norm_mem", inputs["norm_mem"])
    vp.add("conv_w", inputs["a_conv_w"])
    vp.add("conv_b", inputs["a_conv_b"])
    vp.add("gate_b", inputs["a_gate_b"])
    vp.add("lam", inputs["a_lambda"])
    vp.add("q_norm", inputs["b_q_norm"])
    vp.add("kv_norm", inputs["b_kv_norm"])
    vp.add("c_scale", inputs["c_scale"])
    vecs = vp.arr()
    prog = Prog(cfg, vp.off, vecs.shape[1])
    nc = prog.build()
    def jm(Wm, cols_list):
        Wm = np.asarray(Wm, np.float32)
        K = Wm.shape[0]
        out = []
        for cols in cols_list:
            blk = Wm[:, cols]
            w = blk.shape[1]
            out.append(blk.reshape(K // 128, 128, w).transpose(1, 0, 2).reshape(128, (K // 128) * w))
        return np.ascontiguousarray(np.stack(out))

    ar = np.arange
    cols_ac = [ar(j * 128, (j + 1) * 128) for j in range(64)]
    cols_b = ([ar(n * 128, (n + 1) * 128) for n in range(6)]
              + [np.concatenate([ar(768, 832), ar(800, 832), ar(768, 800)])]
              + [ar(832 + n * 128, 832 + (n + 1) * 128) for n in range(8)]
              + [ar(1856 + n * 128, 1856 + (n + 1) * 128) for n in range(32)])
    c16 = [ar(j * 128, (j + 1) * 128) for j in range(16)]
    gw = np.asarray(inputs["a_gate_w"], np.float32).reshape(96, 256, 256)
    shared = {
        "Win0": jm(inputs["a_w_in"][0], cols_ac), "Win1": jm(inputs["b_w_in"][0], cols_b),
        "Win2": jm(inputs["c_w_in"][0], cols_ac), "Win3": jm(inputs["a_w_in"][1], cols_ac),
        "gatew": np.ascontiguousarray(np.stack([jm(gw[i], [ar(256)])[0] for i in range(96)])),
        "wq": jm(inputs["b_w_q_up"][0], [ar(h * 192, (h + 1) * 192) for h in range(24)]),
        "wkv": jm(inputs["b_w_kv_up"][0], [ar(h * 256, (h + 1) * 256) for h in range(24)]),
        "wgrp": np.ascontiguousarray(np.concatenate([jm(np.asarray(inputs["c_w_group"])[0, g], [ar(m * 128, (m + 1) * 128) for m in range(6)])
                                                     for g in range(4)])),
        "vecs": vecs, "ident": np.eye(128, dtype=np.float32),
    }
    for i in range(4):
        shared["Wout%d" % i] = jm(inputs["w_out"][i], c16)
        shared["Wmem%d" % i] = jm(inputs["w_mem_kv"][i], c16)
    xp, xs = np.asarray(inputs["x_prompt"]), np.asarray(inputs["x_sample"])
    mp, ms = np.asarray(inputs["mem_prompt"]), np.asarray(inputs["mem_sample"])
    in_maps = []
    for core in range(n_cores):
        g, j = core // 4, core % 4
        m = dict(shared)
        m["xT"] = np.ascontiguousarray(np.concatenate([xp[g, j * LP:(j + 1) * LP].T, xs[g, j * LS:(j + 1) * LS].T], axis=1), np.float32)
        m["memT"] = np.ascontiguousarray(np.concatenate([mp[g].T, ms[g].T], axis=1), np.float32)
        pos = np.concatenate([np.arange(j * LP, (j + 1) * LP), np.arange(j * LS, (j + 1) * LS)])
        m["rope"] = _rope_tables(pos)
        m["invc"] = np.ascontiguousarray(np.concatenate([_invc(pos[:LP], 4 * LP), _invc(pos[LP:], 4 * LS)], axis=1))
        mk = np.zeros((128, 12), np.float32)
        if j > 0:
            mk[:, j - 1] = 1.0
        if j < 3:
            mk[:, 4 + j + 1] = 1.0
        mk[:, 8 + j] = 1.0
        m["masks"] = mk
        in_maps.append(m)
    res = run_bass_kernel_spmd(nc, in_maps, core_ids=list(range(n_cores)))
    global LAST_RES
    LAST_RES = res
    yp = np.zeros((2, 4 * LP, D), np.float32)
    ys = np.zeros((2, 4 * LS, D), np.float32)
    for core in range(n_cores):
        g, j = core // 4, core % 4
        yT = np.asarray(res.results[core]["yT"])
        yp[g, j * LP:(j + 1) * LP] = yT[:, :LP].T
        ys[g, j * LS:(j + 1) * LS] = yT[:, LP:].T
    return yp, ys


def kernel(**inputs):
    return run(inputs, 1024, 2048, 4, 8)
```
